# Optimizing a Trainium2 kernel written in Bass

```python
import jax, jax.numpy as jnp
from jax import lax
import numpy as np

D_MODEL = 1024
BATCH = 4
SEQ = 4096
DEPTH = 1

HEAD_DIM = 64
D_MIX = D_MODEL
GDN_HEADS = (D_MIX // 2) // HEAD_DIM
GDN_WIDTH = GDN_HEADS * HEAD_DIM
SWA_Q_HEADS = (D_MIX - GDN_WIDTH) // HEAD_DIM
SWA_KV_HEADS = 2
SWA_GROUP = SWA_Q_HEADS // SWA_KV_HEADS
SWA_WIDTH = SWA_Q_HEADS * HEAD_DIM
SWA_KV_WIDTH = SWA_KV_HEADS * HEAD_DIM
WINDOW = 128
CONV_WIDTH = 4
CHUNK = 64
D_FF = ((8 * D_MODEL // 3 + 255) // 256) * 256
PROJ_WIDTH = 4 * GDN_WIDTH + 2 * GDN_HEADS + SWA_WIDTH + 2 * SWA_KV_WIDTH
EPS = 1e-6

kernel_name = 'hymba_gdn_swa_adaln'


def rms_norm(x, w):
    xf = x.astype(jnp.float32)
    y = xf * lax.rsqrt(jnp.mean(xf * xf, axis=-1, keepdims=True) + EPS)
    return (y * w.astype(jnp.float32)).astype(x.dtype)


def l2_norm(x):
    xf = x.astype(jnp.float32)
    return xf * lax.rsqrt(jnp.sum(xf * xf, axis=-1, keepdims=True) + EPS)


def causal_depthwise_conv(x, w):
    return lax.conv_general_dilated(
        x, w.astype(x.dtype), window_strides=(1,), padding=[(CONV_WIDTH - 1, 0)],
        dimension_numbers=('NWC', 'WIO', 'NWC'), feature_group_count=x.shape[-1])


def gated_delta_rule_chunked(q, k, v, g, beta):
    B, T, H, Dk = q.shape
    Dv = v.shape[-1]
    N = T // CHUNK

    def chunks(t):
        t = t.reshape((B, N, CHUNK, H) + t.shape[3:])
        return jnp.moveaxis(t, 3, 1)

    q = chunks(q) * (Dk ** -0.5)
    k, v, g, beta = chunks(k), chunks(v), chunks(g), chunks(beta)
    G = jnp.cumsum(g, axis=-1)
    idx = jnp.arange(CHUNK)
    causal = idx[:, None] >= idx[None, :]
    strict = idx[:, None] > idx[None, :]
    decay = jnp.exp(jnp.where(causal, G[..., :, None] - G[..., None, :], -jnp.inf))
    kb = k * beta[..., None]
    A = jnp.where(strict, jnp.einsum('bhncd,bhnsd->bhncs', kb, k) * decay, 0.0)
    L = A + jnp.eye(CHUNK, dtype=A.dtype)
    rhs = jnp.concatenate([v * beta[..., None], kb * jnp.exp(G)[..., None]], axis=-1)
    sol = lax.linalg.triangular_solve(L, rhs, left_side=True, lower=True, unit_diagonal=True)
    u, w = sol[..., :Dv], sol[..., Dv:]
    qk = jnp.where(causal, jnp.einsum('bhncd,bhnsd->bhncs', q, k) * decay, 0.0)
    q_dec = q * jnp.exp(G)[..., None]
    k_dec = k * jnp.exp(G[..., -1:] - G)[..., None]
    chunk_decay = jnp.exp(G[..., -1])

    xs = tuple(jnp.moveaxis(t, 2, 0) for t in (u, w, qk, q_dec, k_dec, chunk_decay))

    def step(S, inp):
        u_c, w_c, qk_c, qd_c, kd_c, dec_c = inp
        v_new = u_c - jnp.einsum('bhcd,bhde->bhce', w_c, S)
        o = jnp.einsum('bhcd,bhde->bhce', qd_c, S) + jnp.einsum('bhcs,bhse->bhce', qk_c, v_new)
        S = S * dec_c[..., None, None] + jnp.einsum('bhcd,bhce->bhde', kd_c, v_new)
        return S, o

    S0 = jnp.zeros((B, H, Dk, Dv), jnp.float32)
    _, o = lax.scan(step, S0, xs)
    return jnp.transpose(o, (1, 0, 3, 2, 4)).reshape(B, T, H, Dv)


def sliding_window_attention(q, k, v, sinks):
    B, T, Hq, D = q.shape
    NB = T // WINDOW
    qb = q.reshape(B, NB, WINDOW, SWA_KV_HEADS, SWA_GROUP, D)

    def banded(t):
        t = t.reshape(B, NB, WINDOW, SWA_KV_HEADS, D)
        prev = jnp.pad(t, ((0, 0), (1, 0), (0, 0), (0, 0), (0, 0)))[:, :-1]
        return jnp.concatenate([prev, t], axis=2)

    kw, vw = banded(k), banded(v)
    s = jnp.einsum('bnqhgd,bnkhd->bhgnqk', qb, kw).astype(jnp.float32) * (D ** -0.5)
    qi = jnp.arange(WINDOW)[:, None] + WINDOW
    ki = jnp.arange(2 * WINDOW)[None, :]
    dist = (qi - ki).astype(jnp.float32)
    in_window = (qi - ki >= 0) & (qi - ki < WINDOW)
    key_exists = (jnp.arange(NB)[:, None] * WINDOW + ki - WINDOW) >= 0
    mask = in_window[None] & key_exists[:, None, :]
    slopes = 2.0 ** (-8.0 * (jnp.arange(Hq, dtype=jnp.float32) + 1.0) / Hq)
    slopes = slopes.reshape(SWA_KV_HEADS, SWA_GROUP)
    s = s - slopes[:, :, None, None, None] * dist
    s = jnp.where(mask, s, -jnp.inf)
    sink = sinks.astype(jnp.float32).reshape(SWA_KV_HEADS, SWA_GROUP)[None, :, :, None, None, None]
    m = jnp.maximum(jnp.max(s, axis=-1, keepdims=True), sink)
    p = jnp.exp(s - m)
    p = p / (jnp.sum(p, axis=-1, keepdims=True) + jnp.exp(sink - m))
    o = jnp.einsum('bhgnqk,bnkhd->bnqhgd', p.astype(v.dtype), vw)
    return o.reshape(B, T, Hq * D)


def setup_inputs(seed: int = 0) -> dict:
    key = jax.random.key(seed)
    ks = jax.random.split(key, 20)
    f32 = jnp.float32
    nrm = lambda k, shape, scale: jax.random.normal(k, shape, f32) * scale
    gain = lambda k, shape: 1.0 + 0.1 * jax.random.normal(k, shape, f32)
    a_init = jax.random.uniform(ks[6], (DEPTH, GDN_HEADS), f32, 1.0, 16.0)
    dt = jnp.exp(jax.random.uniform(ks[7], (DEPTH, GDN_HEADS), f32, np.log(1e-3), np.log(1e-1)))
    return {
        'x': nrm(ks[0], (BATCH, SEQ, D_MODEL), 1.0),
        'c': nrm(ks[1], (BATCH, D_MODEL), 1.0),
        'w_ada': nrm(ks[2], (DEPTH, D_MODEL, 6 * D_MODEL), D_MODEL ** -0.5),
        'b_ada': nrm(ks[3], (DEPTH, 6 * D_MODEL), 0.1),
        'norm1_w': gain(ks[4], (DEPTH, D_MODEL)),
        'w_in': nrm(ks[5], (DEPTH, D_MODEL, PROJ_WIDTH), D_MODEL ** -0.5),
        'conv_w': nrm(ks[8], (DEPTH, CONV_WIDTH, 1, 3 * GDN_WIDTH), CONV_WIDTH ** -0.5),
        'a_log': jnp.log(a_init),
        'dt_bias': dt + jnp.log(-jnp.expm1(-dt)),
        'gdn_norm_w': gain(ks[9], (DEPTH, HEAD_DIM)),
        'q_norm_w': gain(ks[10], (DEPTH, HEAD_DIM)),
        'k_norm_w': gain(ks[11], (DEPTH, HEAD_DIM)),
        'sinks': nrm(ks[12], (DEPTH, SWA_Q_HEADS), 1.0),
        'w_out': nrm(ks[13], (DEPTH, D_MIX, D_MODEL), D_MIX ** -0.5),
        'norm2_w': gain(ks[14], (DEPTH, D_MODEL)),
        'w_gate': nrm(ks[15], (DEPTH, D_MODEL, D_FF), D_MODEL ** -0.5),
        'w_up': nrm(ks[16], (DEPTH, D_MODEL, D_FF), D_MODEL ** -0.5),
        'w_down': nrm(ks[17], (DEPTH, D_FF, D_MODEL), D_FF ** -0.5),
    }


def reference(x, c, w_ada, b_ada, norm1_w, w_in, conv_w, a_log, dt_bias, gdn_norm_w,
              q_norm_w, k_norm_w, sinks, w_out, norm2_w, w_gate, w_up, w_down):
    B, T, _ = x.shape
    split_sizes = (GDN_WIDTH,) * 4 + (GDN_HEADS,) * 2 + (SWA_WIDTH, SWA_KV_WIDTH, SWA_KV_WIDTH)
    split_points = []
    acc = 0
    for sz in split_sizes[:-1]:
        acc += sz
        split_points.append(acc)
    c_act = jax.nn.silu(c)
    for l in range(DEPTH):
        mod = (c_act @ w_ada[l] + b_ada[l])[:, None, :]
        shift1, scale1, gate1, shift2, scale2, gate2 = jnp.split(mod, 6, axis=-1)

        h = rms_norm(x, norm1_w[l]) * (1.0 + scale1) + shift1
        proj = h @ w_in[l]
        gq, gk, gv, gz, ga, gb, sq, sk, sv = jnp.split(proj, split_points, axis=-1)

        qkv = jax.nn.silu(causal_depthwise_conv(jnp.concatenate([gq, gk, gv], axis=-1), conv_w[l]))
        gq, gk, gv = jnp.split(qkv, 3, axis=-1)
        heads = lambda t: t.reshape(B, T, -1, HEAD_DIM)
        q_g = l2_norm(heads(gq))
        k_g = l2_norm(heads(gk))
        v_g = heads(gv).astype(jnp.float32)
        beta = jax.nn.sigmoid(gb.astype(jnp.float32))
        g = -jnp.exp(a_log[l].astype(jnp.float32)) * jax.nn.softplus(
            ga.astype(jnp.float32) + dt_bias[l].astype(jnp.float32))
        o_g = gated_delta_rule_chunked(q_g, k_g, v_g, g, beta).astype(x.dtype)
        o_g = rms_norm(o_g, gdn_norm_w[l]) * jax.nn.silu(heads(gz))
        o_g = o_g.reshape(B, T, GDN_WIDTH)

        q_s = rms_norm(heads(sq), q_norm_w[l])
        k_s = rms_norm(heads(sk), k_norm_w[l])
        v_s = heads(sv)
        o_s = sliding_window_attention(q_s, k_s, v_s, sinks[l])

        mixed = jnp.concatenate([o_g, o_s], axis=-1) @ w_out[l]
        x = x + gate1 * mixed

        h2 = rms_norm(x, norm2_w[l]) * (1.0 + scale2) + shift2
        ffn = (jax.nn.silu(h2 @ w_gate[l]) * (h2 @ w_up[l])) @ w_down[l]
        x = x + gate2 * ffn
    return x
```

```python
import numpy as np
import concourse.bass as bass
import concourse.mybir as mybir
from concourse.bass_utils import run_bass_kernel_spmd
from contextlib import ExitStack

F32 = mybir.dt.float32
BF16 = mybir.dt.bfloat16
ALU = mybir.AluOpType
AF = mybir.ActivationFunctionType
AX = mybir.AxisListType

D = 1024
T_HALF = 2048
TG = 256
NGRP = T_HALF // TG
CH = 64
PROJ = 2832
DFF = 2816
EPS = 1e-6
CHAIN_PRIO = 2
PRE_STAGGER = 3
BIG = 30000.0

ENGS = ("pe", "act", "dve", "pool", "sp")
NDMASEM = 6
ATTACH_WAITS = True


import sys as _sys


def _caller_line():
    f = _sys._getframe(2)
    lines = []
    while f is not None and len(lines) < 3:
        lines.append(f.f_lineno)
        f = f.f_back
    return lines


class Prog:
    def __init__(self):
        self.q = {e: [] for e in ENGS}
        self.last_w = {}
        self.readers = {}
        self.dma_uses = {}
        self.dma_rr = {e: 0 for e in ENGS}
        self.ncomp = {e: 0 for e in ENGS}
        self.seq = 0

    def _deps(self, r, w):
        deps = set()
        for k in r:
            if k in self.last_w:
                deps.add(self.last_w[k])
        for k in w:
            if k in self.last_w:
                deps.add(self.last_w[k])
            for x in self.readers.get(k, ()):
                deps.add(x)
        return deps

    def _record(self, node, r, w):
        for k in r:
            self.readers.setdefault(k, []).append(node)
        for k in w:
            self.last_w[k] = node
            self.readers[k] = []

    def _waits(self, deps, eng, skip_same_pe=True):
        waits = {}
        for d in deps:
            if d[0] == "c":
                _, de, di = d
                if skip_same_pe and de == eng and eng == "pe":
                    continue
                key = ("c", de)
                waits[key] = max(waits.get(key, 0), di + 1)
            else:
                _, de, slot, use = d
                key = ("d", de, slot)
                waits[key] = max(waits.get(key, 0), 16 * (use + 1))
        return waits

    def op(self, eng, fn, r=(), w=()):
        for k in r:
            if k.startswith("ps") and eng != "pe":
                for x in self.readers.get(k, ()):
                    assert x[1] == eng or x[1] == "pe", ("PSUM bank read by two engines", k, eng, x, _caller_line())
        deps = self._deps(r, w)
        idx = self.ncomp[eng]
        self.ncomp[eng] += 1
        node = ("c", eng, idx)
        self.seq += 1
        self.q[eng].append(dict(fn=fn, waits=self._waits(deps, eng), dma=None, line=_caller_line(), seq=self.seq, cidx=idx + 1))
        self._record(node, r, w)
        return node

    def dma(self, eng, fn, r=(), w=()):
        deps = self._deps(r, w)
        slot = self.dma_rr[eng] % NDMASEM
        self.dma_rr[eng] += 1
        use = self.dma_uses.get((eng, slot), 0)
        self.dma_uses[(eng, slot)] = use + 1
        node = ("d", eng, slot, use)
        waits = self._waits(deps, eng, skip_same_pe=False)
        if use > 0:
            key = ("d", eng, slot)
            waits[key] = max(waits.get(key, 0), 16 * use)
        self.seq += 1
        self.q[eng].append(dict(fn=fn, waits=waits, dma=(slot, use), line=_caller_line(), seq=self.seq, cidx=None))
        self._record(node, r, w)
        return node

    def barrier(self):
        waits = {}
        for e in ENGS:
            if self.ncomp[e] > 0:
                waits[("c", e)] = self.ncomp[e]
        for (e, s), u in self.dma_uses.items():
            if u > 0:
                waits[("d", e, s)] = 16 * u
        for e in ENGS:
            w = {k: v for k, v in waits.items() if not (k[0] == "c" and k[1] == e)}
            self.seq += 1
            self.q[e].append(dict(fn=None, waits=w, dma=None, seq=self.seq, cidx=None))

    def emit(self, nc, final_wait_eng="sp"):
        with ExitStack() as es:
            csem = {e: es.enter_context(nc.semaphore("c_" + e)) for e in ENGS}
            dsem = {}
            for e in ENGS:
                if self.dma_rr[e] > 0:
                    for s in range(NDMASEM):
                        dsem[(e, s)] = es.enter_context(nc.semaphore("d_%s%d" % (e, s)))
            block = es.enter_context(nc.Block())
            prog = self

            def sem_of(key):
                if key[0] == "c":
                    return csem[key[1]]
                return dsem[(key[1], key[2])]

            allitems = sorted(((it["seq"], en, it) for en in ENGS for it in prog.q[en]), key=lambda t: t[0])
            known = {en: {} for en in ENGS}
            done_known = {}
            for _, en, it in allitems:
                kn = known[en]
                keep = []
                for key, val in sorted(it["waits"].items(), key=lambda kv: -kv[1]):
                    if kn.get(key, 0) >= val:
                        continue
                    keep.append((key, val))
                    kn[key] = val
                    if key[0] == "c" and key[1] in ("pe", "act", "dve"):
                        for k2, v2 in done_known.get((key[1], val), {}).items():
                            if kn.get(k2, 0) < v2:
                                kn[k2] = v2
                it["waits2"] = keep
                if it["cidx"] is not None and en in ("pe", "act", "dve"):
                    snap = {k: v for k, v in kn.items() if k[0] == "c" and k[1] in ("pe", "act", "dve")}
                    snap[("c", en)] = it["cidx"]
                    done_known[(en, it["cidx"])] = snap

            def run(engname, e):
                seen = {}
                for item in prog.q[engname]:
                    pend = list(item["waits2"])
                    attach = None
                    if ATTACH_WAITS and pend and item["fn"] is not None:
                        attach = pend.pop()
                    for key, val in pend:
                        e.wait_ge(sem_of(key), val)
                    if item["fn"] is None:
                        continue
                    try:
                        ins = item["fn"](e)
                    except Exception:
                        print("EMIT FAILED for op issued at line", item.get("line"), "engine", engname)
                        raise
                    if attach is not None:
                        ins._wait_ge(sem_of(attach[0]), attach[1])
                    if item["dma"] is not None:
                        slot, use = item["dma"]
                        ins.then_inc(dsem[(engname, slot)], 16)
                    else:
                        ins.then_inc(csem[engname], 1)
                if engname == final_wait_eng:
                    for en in ENGS:
                        n = prog.ncomp[en]
                        if n > 0 and en != engname:
                            e.wait_ge(csem[en], n)
                    for (en, s), sem in dsem.items():
                        u = prog.dma_uses.get((en, s), 0)
                        if u > 0:
                            e.wait_ge(sem, 16 * u)

            block.tensor(lambda e: run("pe", e))
            block.scalar(lambda e: run("act", e))
            block.vector(lambda e: run("dve", e))
            block.gpsimd(lambda e: run("pool", e))
            block.sync(lambda e: run("sp", e))


def MM(out, lhsT, rhs, start=True, stop=True):
    return lambda e: e.matmul(out, lhsT=lhsT, rhs=rhs, start=start, stop=stop)


def TR(out, in_, idn):
    return lambda e: e.transpose(out, in_, idn)


def ACT(out, in_, func, **kw):
    return lambda e: e.activation(out=out, in_=in_, func=func, **kw)


def TT(out, in0, in1, op):
    return lambda e: e.tensor_tensor(out=out, in0=in0, in1=in1, op=op)


def TS(out, in0, s1, op0, s2=None, op1=None):
    if op1 is None:
        return lambda e: e.tensor_scalar(out=out, in0=in0, scalar1=s1, scalar2=None, op0=op0)
    return lambda e: e.tensor_scalar(out=out, in0=in0, scalar1=s1, scalar2=s2, op0=op0, op1=op1)


def STT(out, in0, scalar, in1, op0, op1):
    return lambda e: e.scalar_tensor_tensor(out=out, in0=in0, scalar=scalar, in1=in1, op0=op0, op1=op1)


def CP(out, in_):
    return lambda e: e.tensor_copy(out, in_)


def ACP(out, in_):
    return lambda e: e.activation(out=out, in_=in_, func=AF.Copy)


def RED(out, in_, op=ALU.add):
    return lambda e: e.tensor_reduce(out=out, in_=in_, axis=AX.X, op=op)


def RCP(out, in_):
    return lambda e: e.reciprocal(out, in_)


def MEMSET(ap, v):
    return lambda e: e.memset(ap, v)


def DMA(out, in_):
    return lambda e: e.dma_start(out=out, in_=in_)


def b3(ap, P, h, d, axis):
    return ap.unsqueeze(axis).to_broadcast([P, h, d])


def v3(ap, h):
    return ap.rearrange("p (h d) -> p h d", h=h)


class _Stop(Exception):
    pass


def build_program(debug=None, stop=None):
    nc = bass.Bass("TRN2", target_bir_lowering=False)
    dram_in = lambda name, shape: nc.dram_tensor(name, list(shape), F32, kind="ExternalInput").ap()
    xo = dram_in("xo", [T_HALF, D])
    xp = dram_in("xp", [T_HALF, D])
    flag_d = dram_in("flag", [128, 1])
    cfm_d = dram_in("cfm", [128, 8])
    wada_d = dram_in("w_ada", [D, 6 * D])
    bada_d = dram_in("b_ada_b", [128, 6 * D])
    n1w_d = dram_in("n1w", [128, 8])
    n2w_d = dram_in("n2w", [128, 8])
    win_d = dram_in("w_in", [D, PROJ])
    convw_d = dram_in("convw", [128, 48])
    alog_d = dram_in("alog_b", [128, 8])
    dtb_d = dram_in("dtb_b", [128, 8])
    gnw_d = dram_in("gnw_b", [128, 64])
    qnw_d = dram_in("qnw_b", [128, 64])
    knw_d = dram_in("knw_b", [128, 64])
    sinks_d = dram_in("sinks_b", [128, 8])
    wout_d = dram_in("w_out", [D, D])
    wg_d = dram_in("w_gate", [D, DFF])
    wu_d = dram_in("w_up", [D, DFF])
    wd_d = dram_in("w_down", [DFF, D])
    ident_d = dram_in("ident", [128, 128])
    c64_d = dram_in("c64", [128, 256 + 11 * 64])
    abias_d = dram_in("abias", [128, 2 * 8 * 128])
    y = nc.dram_tensor("y", [T_HALF, D], F32, kind="ExternalOutput").ap()
    dbg = None
    dbgb = None
    if debug is not None:
        dbg = nc.dram_tensor("dbg", [128, debug], F32, kind="ExternalOutput").ap()
        dbgb = nc.dram_tensor("dbgb", [128, debug], BF16, kind="ExternalOutput").ap()

    P = Prog()
    top = ExitStack()

    def check(tag, ap=None):
        if stop != tag:
            return
        if ap is not None and dbg is not None:
            p, n = ap.shape[0], ap.shape[1]
            dst = dbgb if ap.dtype == BF16 else dbg
            P.dma("sp", DMA(dst[0:p, 0:n], ap), r=list(P.last_w.keys()), w=["dbg"])
        raise _Stop()

    ARENA_COLS = 53200
    arena_state = {"cur": 0, "ar": None, "marks": {}}

    def alloc(es, name, shape, dt=F32):
        st = arena_state
        if id(es) not in st["marks"]:
            st["marks"][id(es)] = st["cur"]
            mark = st["cur"]
            es.callback(lambda: st.__setitem__("cur", mark))
        p, n = shape
        nbytes = n * (4 if dt == F32 else 2)
        ncols = (nbytes + 3) // 4
        ncols = (ncols + 7) // 8 * 8
        a0 = st["cur"]
        st["cur"] += ncols
        assert st["cur"] <= ARENA_COLS, ("SBUF arena overflow", name, st["cur"])
        ap = st["ar"][0:p, a0:a0 + ncols]
        if dt != F32:
            ap = ap.bitcast(dt)
        return ap[:, 0:n]

    with top:
        arena_state["ar"] = top.enter_context(nc.sbuf_tensor("arena", [128, ARENA_COLS], F32))
        psb = [top.enter_context(nc.psum_tensor("ps%d" % i, [128, 512], F32)) for i in range(8)]
        ps_rr = {"d": 0, "s": 0}

        def psd():
            i = ps_rr["d"] % 2
            ps_rr["d"] += 1
            return psb[i], "ps%d" % i

        def pss():
            i = 6 + ps_rr["s"] % 2
            ps_rr["s"] += 1
            return psb[i], "ps%d" % i

        ps4_rr = [0]

        def pss4():
            i = 2 + ps4_rr[0] % 6
            ps4_rr[0] += 1
            return psb[i], "ps%d" % i

        def pssb():
            t, k = pss()
            return t[:, :].bitcast(BF16), k

        ps_set_rr = [0, 0]

        def pset(bs):
            i = 2 + 2 * bs + ps_set_rr[bs] % 2
            ps_set_rr[bs] += 1
            return psb[i], "ps%d" % i

        def psetb(bs):
            t, k = pset(bs)
            return t[:, :].bitcast(BF16), k

        def pfix(i):
            return psb[i], "ps%d" % i

        def pfixb(i):
            return psb[i][:, :].bitcast(BF16), "ps%d" % i

        ident = alloc(top, "ident", [128, 128])
        c64 = alloc(top, "c64", [128, 960])
        utri = c64[:, 0:128]; ones_bd = c64[:, 128:256]
        I2, maskL, maskU, strictL, mA1 = [c64[:, 256 + i * 64:256 + (i + 1) * 64] for i in range(5)]
        mTs = [c64[:, 256 + (5 + i) * 64:256 + (6 + i) * 64] for i in range(6)]
        flag = alloc(top, "flag", [128, 1])
        colv = alloc(top, "colv", [128, 32])
        a1 = alloc(top, "a1", [128, 8]); a2 = alloc(top, "a2", [128, 8])
        n1w = alloc(top, "n1w", [128, 8]); n2w = alloc(top, "n2w", [128, 8])
        gateB = alloc(top, "gateB", [128, 2048])
        identb = alloc(top, "identb", [128, 128], BF16)
        oT = alloc(top, "oT", [128, 8 * T_HALF], BF16)
        oT3 = oT[:, :].rearrange("p (k t) -> p k t", k=8)

        for dst, src, key in ((ident, ident_d, "ident"), (c64, c64_d, "c64"),
                              (flag, flag_d, "flag"), (n1w, n1w_d, "n1w"), (n2w, n2w_d, "n2w")):
            P.dma("sp", DMA(dst[:], src), w=[key])
        P.op("dve", CP(identb[:], ident[:]), r=["ident"], w=["identb"])

        try:
            sW = ExitStack()
            win = alloc(sW, "win", [128, 8 * PROJ], BF16)
            win3 = win[:, :].rearrange("p (k c) -> p k c", k=8)
            win_v = win_d.rearrange("(k p) n -> p k n", p=128)
            with ExitStack() as s0:
                cfm = alloc(s0, "cfm", [128, 8])
                cactB = alloc(s0, "cactB", [128, 8 * 128], BF16)
                NWA = 3
                wa = [alloc(s0, "wa%d" % i, [128, 8 * 512], BF16) for i in range(NWA)]
                ba = [alloc(s0, "ba%d" % i, [128, 512]) for i in range(NWA)]
                modc = alloc(s0, "modc", [128, 512])
                dtmp = alloc(s0, "dtmp", [128, 512])
                P.dma("sp", DMA(cfm[:], cfm_d), w=["cfm"])
                P.op("act", ACT(cfm[:], cfm[:], AF.Silu), r=["cfm"], w=["cfm"])
                P.op("dve", CP(v3(cactB[:, :], 8), b3(cfm[:, :], 128, 8, 128, 2)), r=["cfm"], w=["cactB"])
                wada_v = wada_d.rearrange("(k p) n -> p k n", p=128)
                for ci in range(12):
                    sl = ci % NWA
                    vec, half = ci // 2, ci % 2
                    P.dma("pool", DMA(v3(wa[sl][:, :], 8), wada_v[:, :, ci * 512:(ci + 1) * 512]), w=["wa%d" % sl])
                    if ci in (2, 5):
                        for kc in range((ci // 3) * 4, (ci // 3) * 4 + 4):
                            P.dma("pool", DMA(win3[:, kc, :], win_v[:, kc, :]), w=["win"])
                    P.dma("sp", DMA(ba[sl][:], bada_d[:, ci * 512:(ci + 1) * 512]), w=["ba%d" % sl])
                    pt, pk = psd()
                    for kc in range(8):
                        P.op("pe", MM(pt[:, :], cactB[:, kc * 128:(kc + 1) * 128], wa[sl][:, kc * 512:(kc + 1) * 512],
                                      start=(kc == 0), stop=(kc == 7)), r=["cactB", "wa%d" % sl], w=[pk])
                    if vec in (2, 5):
                        g0 = (0 if vec == 2 else 1024) + half * 512
                        P.op("dve", TT(gateB[:, g0:g0 + 512], pt[:, :], ba[sl][:], ALU.add), r=[pk, "ba%d" % sl], w=["gateB"])
                    else:
                        vi = {0: 0, 1: 1, 3: 2, 4: 3}[vec]
                        P.op("dve", TT(modc[:], pt[:, :], ba[sl][:], ALU.add), r=[pk, "ba%d" % sl], w=["modc"])
                        P.op("dve", TT(v3(dtmp[:, :], 4), v3(modc[:, :], 4), b3(ident[:, :], 128, 4, 128, 1), ALU.mult),
                             r=["modc", "ident"], w=["dtmp"])
                        c0 = vi * 8 + half * 4
                        P.op("dve", RED(colv[:, c0:c0 + 4], v3(dtmp[:, :], 4)), r=["dtmp"], w=["colv"])
                P.op("dve", STT(a1[:], colv[:, 8:16], 1.0, n1w[:], ALU.add, ALU.mult), r=["colv", "n1w"], w=["a1"])
                P.op("dve", STT(a2[:], colv[:, 24:32], 1.0, n2w[:], ALU.add, ALU.mult), r=["colv", "n2w"], w=["a2"])
                P.barrier()
                check("p0_colv", colv[:, :])
                check("p0_gate", gateB[:, :])
            s1c = colv[:, 0:8]
            s2c = colv[:, 16:24]

            with ExitStack() as sA:
                abias = alloc(sA, "abias", [128, 2048])
                convw = alloc(sA, "convw", [128, 48])
                negA = alloc(sA, "negA", [128, 8]); dtb = alloc(sA, "dtb", [128, 8])
                gnw = alloc(sA, "gnw", [128, 64])
                qnw = alloc(sA, "qnw", [128, 64]); knw = alloc(sA, "knw", [128, 64])
                esink = alloc(sA, "esink", [128, 8])
                for dst, src, key in ((abias, abias_d, "abias"), (convw, convw_d, "convw"), (negA, alog_d, "negA"),
                                      (dtb, dtb_d, "dtb"), (gnw, gnw_d, "gnw"), (qnw, qnw_d, "qnw"), (knw, knw_d, "knw"),
                                      (esink, sinks_d, "esink")):
                    P.dma("sp", DMA(dst[:], src), w=[key])
                P.op("act", ACT(negA[:], negA[:], AF.Exp), r=["negA"], w=["negA"])
                P.op("dve", TS(negA[:], negA[:], -1.0, ALU.mult), r=["negA"], w=["negA"])
                P.op("act", ACT(esink[:], esink[:], AF.Exp), r=["esink"], w=["esink"])
                P.op("dve", TS(qnw[:], qnw[:], 0.125, ALU.mult), r=["qnw"], w=["qnw"])
                xt = alloc(sA, "xt", [128, D])
                st2 = alloc(sA, "st2", [128, 2])
                hT = alloc(sA, "hT", [128, 8 * TG], BF16)
                hT3 = hT[:, :].rearrange("p (k t) -> p k t", k=8)
                halo = alloc(sA, "halo", [128, 36])
                praw = [alloc(sA, "praw%d" % i, [128, TG + 3]) for i in range(2)]
                cacc = alloc(sA, "cacc", [128, TG])
                cs = alloc(sA, "cs", [128, 12 * TG], BF16)
                NSET = 2
                tqs = [[alloc(sA, "tqkv%d_%d" % (i, b_), [128, 512], BF16) for i in range(3)] for b_ in range(NSET)]
                Sb = alloc(sA, "Sb", [64, 512], BF16)
                zts = [alloc(sA, "zt%d" % b_, [128, 512], BF16) for b_ in range(NSET)]
                S = alloc(sA, "S", [64, 512])
                smn = ("gab", "g", "e1", "beta", "nbeta", "G16", "eG", "eGL", "eGLB", "dG", "eGLmG", "beG", "ssk", "rk", "ssq", "rq", "sso", "ro")
                smw = {"gab": 16, "G16": 24}
                sms = [{n: alloc(sA, "sm%d_%s" % (b_, n), [128, smw.get(n, 8)]) for n in smn} for b_ in range(NSET)]
                gts = []
                for b_ in range(NSET):
                    d_ = {n: alloc(sA, "gt%d_%s" % (b_, n), [128, 512]) for n in ("t0", "t1", "o")}
                    d_.update({n: alloc(sA, "gt%d_%s" % (b_, n), [128, 512], BF16) for n in
                               ("D", "DT", "nbs", "bv", "kn", "qn", "P0", "P1", "Q0", "Q1", "W0", "W1", "T0", "T1",
                                "QKmT", "kd2", "r", "vn", "og")})
                    d_.update({n: alloc(sA, "gt%d_%s" % (b_, n), [64, 1024], BF16) for n in ("knT", "qnT")})
                    gts.append(d_)
                sw = {n: alloc(sA, "sw_" + n, [128, 512]) for n in ("t0", "qn", "st0", "st1", "os", "qraw")}
                sw["kv"] = sw["qraw"]
                sw["qn"] = sw["qn"][:, :].bitcast(BF16)[:, 0:512]
                sw["os"] = sw["os"][:, :].bitcast(BF16)[:, 0:512]
                junk = sw["t0"][:, :].bitcast(BF16)
                swkn = alloc(sA, "swkn", [128, 128], BF16)
                swsm = {n: alloc(sA, "swsm_" + n, [128, 8]) for n in ("ssq", "rq", "ssk", "rk", "den", "rden")}
                qT = alloc(sA, "qT", [64, 1024], BF16)
                kT = [alloc(sA, "kT%d" % i, [64, 256], BF16) for i in range(2)]
                vb1 = [alloc(sA, "vb1%d" % i, [128, 130], BF16) for i in range(2)]
                pT = [alloc(sA, "pT%d" % i, [128, 512], BF16) for i in range(4)]

                P.op("dve", MEMSET(halo[:], 0.0), w=["halo"])
                P.op("dve", MEMSET(S[:], 0.0), w=["S"])
                P.op("dve", MEMSET(Sb[:], 0.0), w=["Sb"])
                for i in range(2):
                    P.op("dve", MEMSET(vb1[i][:], 1.0), w=["vb1%d" % i])
                    P.op("dve", MEMSET(kT[i][:], 0.0), w=["kT%d" % i])

                id64 = ident[0:64, 0:64]
                idb64 = identb[0:64, 0:64]
                swa_blk = [0]

                def norm_to_hT(xsrc, row0, tcol):
                    P.dma("sp", DMA(xt[:], xsrc[row0:row0 + 128, :]), w=["xt"])
                    P.op("dve", MEMSET(st2[:, 0:1], 0.0), w=["st2"])
                    P.op("act", ACT(junk[:], xt[:], AF.Square, accum_out=st2[:, 0:1]), r=["xt", "st2"], w=["sw_t0", "st2"])
                    P.op("act", ACT(st2[:, 1:2], st2[:, 0:1], AF.Ln, scale=1.0 / D, bias=EPS), r=["st2"], w=["st2"])
                    P.op("act", ACT(st2[:, 1:2], st2[:, 1:2], AF.Exp, scale=-0.5), r=["st2"], w=["st2"])
                    P.op("dve", TS(junk[:], xt[:], st2[:, 1:2], ALU.mult), r=["xt", "st2", "sw_t0"], w=["sw_t0"])
                    for half in range(2):
                        pt, pk = pssb()
                        for j in range(4):
                            kc = half * 4 + j
                            P.op("pe", TR(pt[:, j * 128:(j + 1) * 128], junk[:, kc * 128:(kc + 1) * 128], identb[:, :]),
                                 r=["sw_t0", "identb"], w=[pk])
                        for j in range(4):
                            kc = half * 4 + j
                            P.op("act", ACT(hT3[:, kc, tcol:tcol + 128], pt[:, j * 128:(j + 1) * 128], AF.Identity,
                                            scale=a1[:, kc:kc + 1], bias=s1c[:, kc:kc + 1]),
                                 r=[pk, "a1", "colv"], w=["hT"])

                def proj_fm(cc, do_conv):
                    pr = praw[cc % 2]; prk = "praw%d" % (cc % 2)
                    pt, pk = psd()
                    for kc in range(8):
                        P.op("pe", MM(pt[:, 0:TG], win3[:, kc, cc * 128:(cc + 1) * 128], hT3[:, kc, :],
                                      start=(kc == 0), stop=(kc == 7)), r=["win", "hT"], w=[pk])
                    P.op("dve", CP(pr[:, 0:3], halo[:, cc * 3:cc * 3 + 3]), r=["halo"], w=[prk])
                    P.op("act", ACP(pr[:, 3:3 + TG], pt[:, 0:TG]), r=[pk], w=[prk])
                    P.op("dve", CP(halo[:, cc * 3:cc * 3 + 3], pr[:, TG:TG + 3]), r=[prk], w=["halo"])
                    if not do_conv:
                        return
                    P.op("dve", TS(cacc[:], pr[:, 0:TG], convw[:, cc * 4:cc * 4 + 1], ALU.mult), r=[prk, "convw"], w=["cacc"])
                    for j in range(1, 4):
                        P.op("dve", STT(cacc[:], pr[:, j:j + TG], convw[:, cc * 4 + j:cc * 4 + j + 1], cacc[:], ALU.mult, ALU.add),
                             r=[prk, "convw", "cacc"], w=["cacc"])
                    P.op("act", ACT(cs[:, cc * TG:(cc + 1) * TG], cacc[:], AF.Silu), r=["cacc"], w=["cs"])

                RA = slice(0, 64); RB = slice(64, 128)
                idbA = identb[0:64, 0:64]; idbB = identb[64:128, 64:128]

                def tok_major(typ, tcol, bs):
                    pt, pk = psetb(bs)
                    for j in range(4):
                        cc = typ * 4 + j
                        P.op("pe", TR(pt[:, j * 128:(j + 1) * 128], cs[:, cc * TG + tcol:cc * TG + tcol + 128], identb[:, :]),
                             r=["cs", "identb"], w=[pk])
                    P.op("act", ACP(tqs[bs][typ][:], pt[:, 0:512]), r=[pk], w=["tqkv%d_%d" % (typ, bs)])

                def l2n(src, srck, ss, rr, dst, dstk, extra_bias, bs):
                    g_ = gts[bs]; m_ = sms[bs]
                    K = lambda nm: "%s_%d" % (nm, bs)
                    P.op("dve", TT(g_["t0"][:], src[:], src[:], ALU.mult), r=[srck], w=[K("gt_t0")])
                    P.op("dve", RED(m_[ss][:, 0:8], v3(g_["t0"][:, :], 8)), r=[K("gt_t0")], w=[K("sm_" + ss)])
                    yield
                    P.op("act", ACT(m_[rr][:, 0:8], m_[ss][:, 0:8], AF.Ln, bias=EPS), r=[K("sm_" + ss)], w=[K("sm_" + rr)])
                    P.op("act", ACT(m_[rr][:, 0:8], m_[rr][:, 0:8], AF.Exp, scale=-0.5, bias=extra_bias), r=[K("sm_" + rr)], w=[K("sm_" + rr)])
                    yield
                    P.op("dve", TT(v3(dst[:, :], 8), v3(src[:, :], 8), b3(m_[rr][:, 0:8], 128, 8, 64, 2), ALU.mult),
                         r=[srck, K("sm_" + rr)], w=[dstk])
                    yield

                def tt_mms(outp, pk, lhs, lhsk, rhs, rhsk):
                    for h in range(8):
                        hs = slice(h * 64, (h + 1) * 64)
                        for R in (RA, RB):
                            P.op("pe", MM(outp[R, hs], lhs[R, hs], rhs[R, hs]), r=[lhsk, rhsk], w=[pk])

                def tt_trs(outp, pk, src, srck):
                    for h in range(8):
                        hs = slice(h * 64, (h + 1) * 64)
                        P.op("pe", TR(outp[RA, hs], src[RA, hs], idbA), r=[srck, "identb"], w=[pk])
                        P.op("pe", TR(outp[RB, hs], src[RB, hs], idbB), r=[srck, "identb"], w=[pk])

                def fm_mms(outp, pk, lhsT3, lhsk, rhs3, rhsk):
                    for h in range(8):
                        hs = slice(h * 64, (h + 1) * 64)
                        for R in (RA, RB):
                            P.op("pe", MM(outp[R, hs], lhsT3[:, h, R], rhs3[:, h, R]), r=[lhsk, rhsk], w=[pk])

                def gdn_pre(n2, full, bs):
                    tcol = n2 * 128
                    g_ = gts[bs]; m_ = sms[bs]; tq_ = tqs[bs]
                    K = lambda nm: "%s_%d" % (nm, bs)
                    knT3 = g_["knT"][:, :].rearrange("p (h t) -> p h t", h=8)
                    qnT3 = g_["qnT"][:, :].rearrange("p (h t) -> p h t", h=8)
                    pg, pgk = pset(bs)
                    for kc in range(8):
                        P.op("pe", MM(pg[:, 0:16], hT3[:, kc, tcol:tcol + 128], win3[:, kc, 2048:2064],
                                      start=(kc == 0), stop=(kc == 7)), r=["hT", "win"], w=[pgk])
                    if full:
                        pz, pzk = psd()
                        for kc in range(8):
                            P.op("pe", MM(pz[:, :], hT3[:, kc, tcol:tcol + 128], win3[:, kc, 1536:2048],
                                          start=(kc == 0), stop=(kc == 7)), r=["hT", "win"], w=[pzk])
                        P.op("act", ACT(zts[bs][:], pz[:, :], AF.Silu), r=[pzk], w=[K("zt")])
                    yield
                    P.op("dve", CP(m_["gab"][:, 0:16], pg[:, 0:16]), r=[pgk], w=[K("sm_gab")])
                    P.op("dve", TT(m_["g"][:, 0:8], m_["gab"][:, 0:8], dtb[:], ALU.add), r=[K("sm_gab"), "dtb"], w=[K("sm_g")])
                    yield
                    P.op("act", ACT(m_["e1"][:, 0:8], m_["g"][:, 0:8], AF.Exp), r=[K("sm_g")], w=[K("sm_e1")])
                    P.op("act", ACT(m_["e1"][:, 0:8], m_["e1"][:, 0:8], AF.Ln, bias=1.0), r=[K("sm_e1")], w=[K("sm_e1")])
                    P.op("act", ACT(m_["beta"][:, 0:8], m_["gab"][:, 8:16], AF.Exp, scale=-1.0), r=[K("sm_gab")], w=[K("sm_beta")])
                    yield
                    P.op("dve", TT(m_["g"][:, 0:8], m_["e1"][:, 0:8], negA[:], ALU.mult), r=[K("sm_e1"), "negA"], w=[K("sm_g")])
                    P.op("dve", TS(m_["beta"][:, 0:8], m_["beta"][:, 0:8], 1.0, ALU.add), r=[K("sm_beta")], w=[K("sm_beta")])
                    P.op("dve", RCP(m_["beta"][:, 0:8], m_["beta"][:, 0:8]), r=[K("sm_beta")], w=[K("sm_beta")])
                    P.op("dve", TS(m_["nbeta"][:, 0:8], m_["beta"][:, 0:8], -1.0, ALU.mult), r=[K("sm_beta")], w=[K("sm_nbeta")])
                    yield
                    pa, pak = pset(bs)
                    P.op("pe", MM(pa[:, 0:8], utri, m_["g"][:, 0:8]), r=["c64", K("sm_g")], w=[pak])
                    P.op("pe", MM(pa[:, 8:16], ones_bd, m_["g"][:, 0:8]), r=["c64", K("sm_g")], w=[pak])
                    P.op("pe", MM(pa[0:64, 16:24], ones_bd[:, 64:128], m_["g"][:, 0:8]), r=["c64", K("sm_g")], w=[pak])
                    yield
                    P.op("dve", CP(m_["G16"][:, 0:16], pa[:, 0:16]), r=[pak], w=[K("sm_G16")])
                    P.op("dve", CP(m_["G16"][0:64, 16:24], pa[0:64, 16:24]), r=[pak], w=[K("sm_G16")])
                    tok_major(1, tcol, bs); tok_major(2, tcol, bs)
                    if full:
                        tok_major(0, tcol, bs)
                    yield "inputs_done"
                    G = m_["G16"][:, 0:8]; GL = m_["G16"][:, 8:16]
                    P.op("dve", TT(m_["dG"][:, 0:8], GL, G, ALU.subtract), r=[K("sm_G16")], w=[K("sm_dG")])
                    P.op("dve", TT(v3(g_["t1"][:, :], 8), b3(I2, 128, 8, 64, 1), b3(G, 128, 8, 64, 2), ALU.mult),
                         r=["c64", K("sm_G16")], w=[K("gt_t1")])
                    yield
                    P.op("act", ACT(m_["eG"][:, 0:8], G, AF.Exp), r=[K("sm_G16")], w=[K("sm_eG")])
                    P.op("act", ACT(m_["eGL"][:, 0:8], GL, AF.Exp), r=[K("sm_G16")], w=[K("sm_eGL")])
                    P.op("act", ACT(m_["eGLB"][0:64, 0:8], m_["G16"][0:64, 16:24], AF.Exp), r=[K("sm_G16")], w=[K("sm_eGLB")])
                    P.op("act", ACT(m_["eGLmG"][:, 0:8], m_["dG"][:, 0:8], AF.Exp), r=[K("sm_dG")], w=[K("sm_eGLmG")])
                    pG, pGk = pset(bs)
                    P.op("pe", MM(pG[:, :], ones_bd, g_["t1"][:, :]), r=["c64", K("gt_t1")], w=[pGk])
                    yield
                    P.op("dve", TT(v3(g_["t0"][:, :], 8), b3(G, 128, 8, 64, 2), v3(pG[:, :], 8), ALU.subtract),
                         r=[K("sm_G16"), pGk], w=[K("gt_t0")])
                    P.op("dve", TT(v3(g_["t1"][:, :], 8), v3(g_["t0"][:, :], 8), b3(maskL, 128, 8, 64, 1), ALU.min),
                         r=[K("gt_t0"), "c64", K("gt_t1")], w=[K("gt_t1")])
                    yield
                    P.op("act", ACT(g_["D"][:], g_["t1"][:], AF.Exp), r=[K("gt_t1")], w=[K("gt_D")])
                    if full:
                        P.op("dve", STT(v3(g_["t0"][:, :], 8), v3(g_["t0"][:, :], 8), -1.0, b3(maskU, 128, 8, 64, 1), ALU.mult, ALU.min),
                             r=[K("gt_t0"), "c64"], w=[K("gt_t0")])
                        yield
                        P.op("act", ACT(g_["DT"][:], g_["t0"][:], AF.Exp), r=[K("gt_t0")], w=[K("gt_DT")])
                    yield
                    P.op("dve", TT(m_["beG"][:, 0:8], m_["beta"][:, 0:8], m_["eG"][:, 0:8], ALU.mult), r=[K("sm_beta"), K("sm_eG")], w=[K("sm_beG")])
                    tk, tv = tq_[1], tq_[2]
                    for _ in l2n(tk, K("tqkv1"), "ssk", "rk", g_["kn"], K("gt_kn"), 0.0, bs):
                        yield
                    pt, pk = psetb(bs)
                    for h in range(8):
                        P.op("pe", TR(pt[0:64, h * 128:(h + 1) * 128], g_["kn"][:, h * 64:(h + 1) * 64], identb[:, :]), r=[K("gt_kn"), "identb"], w=[pk])
                    yield
                    P.op("act", ACP(g_["knT"][:], pt[0:64, 0:1024]), r=[pk], w=[K("gt_knT")])
                    if full:
                        for _ in l2n(tq_[0], K("tqkv0"), "ssq", "rq", g_["qn"], K("gt_qn"), float(np.log(0.125)), bs):
                            yield
                        pt, pk = psetb(bs)
                        for h in range(8):
                            P.op("pe", TR(pt[0:64, h * 128:(h + 1) * 128], g_["qn"][:, h * 64:(h + 1) * 64], identb[:, :]), r=[K("gt_qn"), "identb"], w=[pk])
                        yield
                        P.op("act", ACP(g_["qnT"][:], pt[0:64, 0:1024]), r=[pk], w=[K("gt_qnT")])
                    pK, pKk = pset(bs); fm_mms(pK, pKk, knT3, K("gt_knT"), knT3, K("gt_knT"))
                    yield
                    P.op("dve", TT(g_["t1"][:], pK[:, :], g_["D"][:], ALU.mult), r=[pKk, K("gt_D"), K("gt_t1")], w=[K("gt_t1")])
                    if full:
                        pQ, pQk = pset(bs); fm_mms(pQ, pQk, knT3, K("gt_knT"), qnT3, K("gt_qnT"))
                        yield
                        P.op("dve", TT(g_["QKmT"][:], pQ[:, :], g_["DT"][:], ALU.mult), r=[pQk, K("gt_DT")], w=[K("gt_QKmT")])
                    P.op("dve", TT(v3(g_["nbs"][:, :], 8), b3(strictL, 128, 8, 64, 1), b3(m_["nbeta"][:, 0:8], 128, 8, 64, 2), ALU.mult),
                         r=["c64", K("sm_nbeta")], w=[K("gt_nbs")])
                    P.op("dve", TT(g_["P0"][:], g_["t1"][:], g_["nbs"][:], ALU.mult), r=[K("gt_t1"), K("gt_nbs")], w=[K("gt_P0")])
                    yield
                    pt, pk = psetb(bs); tt_trs(pt, pk, g_["P0"], K("gt_P0"))
                    N_, NT_ = g_["P0"], g_["Q0"]
                    P.op("dve", TT(v3(g_["T0"][:, :], 8), v3(N_[:, :], 8), b3(mA1, 128, 8, 64, 1), ALU.mult), r=[K("gt_P0"), "c64"], w=[K("gt_T0")])
                    P.op("dve", TT(v3(g_["T0"][:, :], 8), v3(g_["T0"][:, :], 8), b3(I2, 128, 8, 64, 1), ALU.add), r=[K("gt_T0"), "c64"], w=[K("gt_T0")])
                    yield
                    P.op("act", ACP(g_["Q0"][:], pt[:, 0:512]), r=[pk], w=[K("gt_Q0")])
                    yield
                    P.op("dve", TT(v3(g_["W0"][:, :], 8), v3(NT_[:, :], 8), b3(mTs[0], 128, 8, 64, 1), ALU.mult), r=[K("gt_Q0"), "c64"], w=[K("gt_W0")])
                    P.op("dve", TT(v3(g_["W0"][:, :], 8), v3(g_["W0"][:, :], 8), b3(I2, 128, 8, 64, 1), ALU.add), r=[K("gt_W0"), "c64"], w=[K("gt_W0")])
                    P.op("dve", TT(v3(g_["bv"][:, :], 8), v3(tv[:, :], 8), b3(m_["beta"][:, 0:8], 128, 8, 64, 2), ALU.mult),
                         r=[K("tqkv2"), K("sm_beta")], w=[K("gt_bv")])
                    P.op("dve", TT(v3(g_["kd2"][:, :], 8), v3(g_["kn"][:, :], 8), b3(m_["eGLmG"][:, 0:8], 128, 8, 64, 2), ALU.mult),
                         r=[K("gt_kn"), K("sm_eGLmG")], w=[K("gt_kd2")])
                    ct_, cw_ = 0, 0
                    for lv in range(1, 6):
                        Tc, Tn = g_["T%d" % ct_], g_["T%d" % (1 - ct_)]
                        Tck, Tnk = K("gt_T%d" % ct_), K("gt_T%d" % (1 - ct_))
                        Wc, Wn = g_["W%d" % cw_], g_["W%d" % (1 - cw_)]
                        Wck, Wnk = K("gt_W%d" % cw_), K("gt_W%d" % (1 - cw_))
                        p1, p1k = pset(bs); tt_mms(p1, p1k, NT_, K("gt_Q0"), Tc, Tck)
                        yield
                        P.op("dve", TT(v3(g_["Q1"][:, :], 8), v3(p1[:, :], 8), b3(mTs[lv], 128, 8, 64, 1), ALU.mult), r=[p1k, "c64"], w=[K("gt_Q1")])
                        yield
                        p2, p2k = pset(bs); tt_mms(p2, p2k, Wc, Wck, g_["Q1"], K("gt_Q1"))
                        yield
                        P.op("dve", TT(Tn[:], Tc[:], p2[:, :], ALU.add), r=[Tck, p2k], w=[Tnk])
                        yield
                        p3, p3k = psetb(bs); tt_trs(p3, p3k, Tn, Tnk)
                        yield
                        P.op("act", ACP(Wn[:], p3[:, 0:512]), r=[p3k], w=[Wnk])
                        yield
                        ct_, cw_ = 1 - ct_, 1 - cw_
                    gdn_W[bs] = (g_["W%d" % cw_], K("gt_W%d" % cw_))

                gdn_W = [None, None]

                def gdn_chain(n2, full, tok0, bs):
                    g_ = gts[bs]; m_ = sms[bs]
                    K = lambda nm: "%s_%d" % (nm, bs)
                    knT3 = g_["knT"][:, :].rearrange("p (h t) -> p h t", h=8)
                    qnT3 = g_["qnT"][:, :].rearrange("p (h t) -> p h t", h=8)
                    W, Wk = gdn_W[bs]
                    yield "chain"
                    for R in (RA, RB):
                        pC, pCk = pset(bs)
                        for h in range(8):
                            hs = slice(h * 64, (h + 1) * 64)
                            P.op("pe", MM(pC[R, hs], knT3[:, h, R], Sb[:, hs]), r=[K("gt_knT"), "Sb"], w=[pCk])
                        if full:
                            pO1, pO1k = pset(bs)
                            for h in range(8):
                                hs = slice(h * 64, (h + 1) * 64)
                                P.op("pe", MM(pO1[R, hs], qnT3[:, h, R], Sb[:, hs]), r=[K("gt_qnT"), "Sb"], w=[pO1k])
                        yield
                        P.op("dve", TT(v3(g_["t0"][R, :], 8), v3(pC[R, :], 8), b3(m_["beG"][R, 0:8], 64, 8, 64, 2), ALU.mult),
                             r=[pCk, K("sm_beG")], w=[K("gt_t0")])
                        P.op("dve", TT(g_["r"][R, :], g_["bv"][R, :], g_["t0"][R, :], ALU.subtract), r=[K("gt_bv"), K("gt_t0")], w=[K("gt_r")])
                        if full:
                            P.op("dve", TT(v3(g_["o"][R, :], 8), v3(pO1[R, :], 8), b3(m_["eG"][R, 0:8], 64, 8, 64, 2), ALU.mult),
                                 r=[pO1k, K("sm_eG")], w=[K("gt_o")])
                        yield
                        pV, pVk = pset(bs)
                        for h in range(8):
                            hs = slice(h * 64, (h + 1) * 64)
                            P.op("pe", MM(pV[R, hs], W[R, hs], g_["r"][R, hs]), r=[Wk, K("gt_r")], w=[pVk])
                        yield
                        P.op("act", ACP(g_["vn"][R, :], pV[R, :]), r=[pVk], w=[K("gt_vn")])
                        yield
                        pS, pSk = pset(bs)
                        for h in range(8):
                            hs = slice(h * 64, (h + 1) * 64)
                            P.op("pe", MM(pS[0:64, hs], g_["kd2"][R, hs], g_["vn"][R, hs]), r=[K("gt_kd2"), K("gt_vn")], w=[pSk])
                        egl = m_["eGL"][0:64, 0:8] if R is RA else m_["eGLB"][0:64, 0:8]
                        P.op("dve", TT(v3(S[:, :], 8), v3(S[:, :], 8), b3(egl, 64, 8, 64, 2), ALU.mult), r=["S", K("sm_eGL"), K("sm_eGLB")], w=["S"])
                        yield
                        P.op("dve", TT(Sb[:], S[:], pS[0:64, :], ALU.add), r=["S", pSk], w=["Sb"])
                        P.op("dve", TT(S[:], S[:], pS[0:64, :], ALU.add), r=["S", pSk], w=["S"])
                        if full:
                            pO2, pO2k = pset(bs)
                            for h in range(8):
                                hs = slice(h * 64, (h + 1) * 64)
                                P.op("pe", MM(pO2[R, hs], g_["QKmT"][R, hs], g_["vn"][R, hs]), r=[K("gt_QKmT"), K("gt_vn")], w=[pO2k])
                        yield
                        if full:
                            P.op("dve", TT(g_["o"][R, :], g_["o"][R, :], pO2[R, :], ALU.add), r=[K("gt_o"), pO2k], w=[K("gt_o")])
                        yield
                    yield "chain_done"
                    if not full:
                        return
                    P.op("dve", TT(g_["t0"][:], g_["o"][:], g_["o"][:], ALU.mult), r=[K("gt_o")], w=[K("gt_t0")])
                    P.op("dve", RED(m_["sso"][:, 0:8], v3(g_["t0"][:, :], 8)), r=[K("gt_t0")], w=[K("sm_sso")])
                    yield
                    P.op("act", ACT(m_["ro"][:, 0:8], m_["sso"][:, 0:8], AF.Ln, scale=1.0 / 64, bias=EPS), r=[K("sm_sso")], w=[K("sm_ro")])
                    P.op("act", ACT(m_["ro"][:, 0:8], m_["ro"][:, 0:8], AF.Exp, scale=-0.5), r=[K("sm_ro")], w=[K("sm_ro")])
                    yield
                    P.op("dve", TT(v3(g_["t0"][:, :], 8), v3(g_["o"][:, :], 8), b3(m_["ro"][:, 0:8], 128, 8, 64, 2), ALU.mult),
                         r=[K("gt_o"), K("sm_ro")], w=[K("gt_t0")])
                    P.op("dve", TT(v3(g_["t0"][:, :], 8), v3(g_["t0"][:, :], 8), b3(gnw[:, :], 128, 8, 64, 1), ALU.mult),
                         r=[K("gt_t0"), "gnw"], w=[K("gt_t0")])
                    P.op("dve", TT(g_["og"][:], g_["t0"][:], zts[bs][:], ALU.mult), r=[K("gt_t0"), K("zt")], w=[K("gt_og")])
                    yield
                    pt, pk = psetb(bs)
                    for kc in range(4):
                        P.op("pe", TR(pt[:, kc * 128:(kc + 1) * 128], g_["og"][:, kc * 128:(kc + 1) * 128], identb[:, :]), r=[K("gt_og"), "identb"], w=[pk])
                    yield
                    P.op("act", ACP(oT3[:, 0:4, tok0:tok0 + 128], v3(pt[:, 0:512], 4)), r=[pk], w=["oT"])

                def gdn_group(full, tokbase):
                    gens = [gdn_pre(i, full, i) for i in range(2)]
                    for _ in range(PRE_STAGGER):
                        next(gens[0])
                    live = list(gens)
                    while live:
                        for gq in list(live):
                            try:
                                next(gq)
                            except StopIteration:
                                live.remove(gq)

                def chains(full, tokbase):
                    for i in range(2):
                        for v_ in gdn_chain(i, full, tokbase + i * 128, i):
                            yield v_

                def other_gen(items):
                    import types
                    for fn in items:
                        r_ = fn()
                        if isinstance(r_, types.GeneratorType):
                            for _ in r_:
                                yield
                        yield

                def interleave(ga, gb):
                    live = [ga, gb]
                    while live:
                        for gq in list(live):
                            try:
                                for _ in range(CHAIN_PRIO if gq is ga else 1):
                                    next(gq)
                            except StopIteration:
                                live.remove(gq)

                def swa_kv(tcol, slot):
                    pt, pk = psd()
                    for kc in range(8):
                        P.op("pe", MM(pt[:, 0:256], hT3[:, kc, tcol:tcol + 128], win3[:, kc, 2576:2832],
                                      start=(kc == 0), stop=(kc == 7)), r=["hT", "win"], w=[pk])
                    P.op("act", ACP(sw["kv"][:, 0:256], pt[:, 0:256]), r=[pk], w=["sw_qraw"])
                    check("k_mm", sw["kv"][:, :])
                    P.op("act", ACT(sw["t0"][:, 0:128], sw["kv"][:, 0:128], AF.Square), r=["sw_qraw"], w=["sw_t0"])
                    P.op("dve", RED(swsm["ssk"][:, 0:2], v3(sw["t0"][:, 0:128], 2)), r=["sw_t0"], w=["swsm_ssk"])
                    P.op("act", ACT(swsm["rk"][:, 0:2], swsm["ssk"][:, 0:2], AF.Ln, scale=1.0 / 64, bias=EPS), r=["swsm_ssk"], w=["swsm_rk"])
                    P.op("act", ACT(swsm["rk"][:, 0:2], swsm["rk"][:, 0:2], AF.Exp, scale=-0.5), r=["swsm_rk"], w=["swsm_rk"])
                    P.op("dve", TT(v3(swkn[:, :], 2), v3(sw["kv"][:, 0:128], 2), b3(swsm["rk"][:, 0:2], 128, 2, 64, 2), ALU.mult),
                         r=["sw_qraw", "swsm_rk"], w=["swkn"])
                    P.op("dve", TT(v3(swkn[:, :], 2), v3(swkn[:, :], 2), b3(knw[:, :], 128, 2, 64, 1), ALU.mult), r=["swkn", "knw"], w=["swkn"])
                    check("k_norm", swkn[:, :])
                    P.op("act", ACP(vb1[slot][:, :].rearrange("p (h d) -> p h d", h=2)[:, :, 0:64], v3(sw["kv"][:, 128:256], 2)),
                         r=["sw_qraw"], w=["vb1%d" % slot])
                    check("k_v", vb1[slot][:, :])
                    p2, p2k = pssb()
                    for h in range(2):
                        P.op("pe", TR(p2[0:64, h * 128:(h + 1) * 128], swkn[:, h * 64:(h + 1) * 64], identb[:, :]), r=["swkn", "identb"], w=[p2k])
                    P.op("act", ACP(kT[slot][:], p2[0:64, 0:256]), r=[p2k], w=["kT%d" % slot])
                    check("k_T", kT[slot][:, :])

                def swa_block(tcol, tok0, first_block):
                    b = swa_blk[0]; swa_blk[0] += 1
                    cur, prv = b % 2, (b + 1) % 2
                    swa_kv(tcol, cur)
                    pq, pqk = psd()
                    for kc in range(8):
                        P.op("pe", MM(pq[:, :], hT3[:, kc, tcol:tcol + 128], win3[:, kc, 2064:2576],
                                      start=(kc == 0), stop=(kc == 7)), r=["hT", "win"], w=[pqk])
                    P.op("act", ACP(sw["qraw"][:], pq[:, :]), r=[pqk], w=["sw_qraw"])
                    P.op("act", ACT(sw["t0"][:], sw["qraw"][:], AF.Square), r=["sw_qraw"], w=["sw_t0"])
                    P.op("dve", RED(swsm["ssq"][:, 0:8], v3(sw["t0"][:, :], 8)), r=["sw_t0"], w=["swsm_ssq"])
                    P.op("act", ACT(swsm["rq"][:, 0:8], swsm["ssq"][:, 0:8], AF.Ln, scale=1.0 / 64, bias=EPS), r=["swsm_ssq"], w=["swsm_rq"])
                    P.op("act", ACT(swsm["rq"][:, 0:8], swsm["rq"][:, 0:8], AF.Exp, scale=-0.5), r=["swsm_rq"], w=["swsm_rq"])
                    P.op("dve", TT(v3(sw["qn"][:, :], 8), v3(sw["qraw"][:, :], 8), b3(swsm["rq"][:, 0:8], 128, 8, 64, 2), ALU.mult),
                         r=["sw_qraw", "swsm_rq"], w=["sw_qn"])
                    P.op("dve", TT(v3(sw["qn"][:, :], 8), v3(sw["qn"][:, :], 8), b3(qnw[:, :], 128, 8, 64, 1), ALU.mult), r=["sw_qn", "qnw"], w=["sw_qn"])
                    for half in range(2):
                        pt, pk = pssb()
                        for j in range(4):
                            h = half * 4 + j
                            P.op("pe", TR(pt[0:64, j * 128:(j + 1) * 128], sw["qn"][:, h * 64:(h + 1) * 64], identb[:, :]), r=["sw_qn", "identb"], w=[pk])
                        P.op("act", ACP(qT[:, half * 512:(half + 1) * 512], pt[0:64, 0:512]), r=[pk], w=["qT"])
                        yield
                    for kvh in range(2):
                        po, pok = psd()
                        pts = []
                        for kb, sl in ((0, prv), (1, cur)):
                            ps_, psk = pss()
                            P.op("pe", MM(ps_[:, :], kT[sl][:, kvh * 128:(kvh + 1) * 128], qT[:, kvh * 512:(kvh + 1) * 512]),
                                 r=["kT%d" % sl, "qT"], w=[psk])
                            stb = sw["st%d" % kb]; stk = "sw_st%d" % kb
                            a0 = kb * 1024 + kvh * 512
                            P.op("dve", TT(stb[:], ps_[:, :], abias[:, a0:a0 + 512], ALU.add), r=[psk, "abias"], w=[stk])
                            pi = kvh * 2 + kb
                            P.op("act", ACT(pT[pi][:], stb[:], AF.Exp), r=[stk], w=["pT%d" % pi])
                            if kb == 0 and first_block:
                                P.op("dve", TS(pT[pi][:], pT[pi][:], flag[:, 0:1], ALU.mult), r=["pT%d" % pi, "flag"], w=["pT%d" % pi])
                            pts.append((pi, sl))
                            yield
                        for g in range(4):
                            for i, (pi, sl) in enumerate(pts):
                                P.op("pe", MM(po[:, g * 65:(g + 1) * 65], pT[pi][:, g * 128:(g + 1) * 128], vb1[sl][:, kvh * 65:(kvh + 1) * 65],
                                              start=(i == 0), stop=(i == 1)), r=["pT%d" % pi, "vb1%d" % sl], w=[pok])
                        po3 = po[:, 0:260].rearrange("p (g d) -> p g d", g=4)
                        P.op("dve", TT(swsm["den"][:, 0:4].unsqueeze(2), po3[:, :, 64:65], esink[:, kvh * 4:(kvh + 1) * 4].unsqueeze(2), ALU.add),
                             r=[pok, "esink"], w=["swsm_den"])
                        P.op("dve", RCP(swsm["rden"][:, 0:4], swsm["den"][:, 0:4]), r=["swsm_den"], w=["swsm_rden"])
                        P.op("dve", TT(v3(sw["os"][:, kvh * 256:(kvh + 1) * 256], 4), po3[:, :, 0:64], b3(swsm["rden"][:, 0:4], 128, 4, 64, 2), ALU.mult),
                             r=[pok, "swsm_rden"], w=["sw_os"])
                        yield
                    pt, pk = pssb()
                    for kc in range(4):
                        P.op("pe", TR(pt[:, kc * 128:(kc + 1) * 128], sw["os"][:, kc * 128:(kc + 1) * 128], identb[:, :]), r=["sw_os", "identb"], w=[pk])
                    P.op("act", ACP(oT3[:, 4:8, tok0:tok0 + 128], v3(pt[:, 0:512], 4)), r=[pk], w=["oT"])

                def norm_items(xsrc, g):
                    return [lambda t=t: norm_to_hT(xsrc, g * TG + t * 128, t * 128) for t in range(2)]

                def proj_items(full, lastprev):
                    its = []
                    for cc in range(12):
                        if cc < 4 and not full:
                            if lastprev:
                                its.append(lambda cc=cc: proj_fm(cc, False))
                        else:
                            its.append(lambda cc=cc: proj_fm(cc, True))
                    return its

                for it in norm_items(xp, 0) + proj_items(False, False):
                    it()
                for g in range(NGRP):
                    last = (g == NGRP - 1)
                    gdn_group(False, 0)
                    others = []
                    if last:
                        others.append(lambda: swa_kv(128, 1))
                        others += norm_items(xo, 0)
                    else:
                        others += norm_items(xp, g + 1) + proj_items(False, g + 1 == NGRP - 1)
                    interleave(chains(False, 0), other_gen(others))
                P.op("dve", TS(S[:], S[:], flag[0:64, 0:1], ALU.mult), r=["S", "flag"], w=["S"])
                P.op("act", ACP(Sb[:], S[:]), r=["S"], w=["Sb"])
                P.op("dve", TS(halo[:], halo[:], flag[:, 0:1], ALU.mult), r=["halo", "flag"], w=["halo"])
                for it in proj_items(True, False):
                    it()

                for g in range(NGRP):
                    gdn_group(True, g * TG)
                    others = [lambda t=t: swa_block(t * 128, g * TG + t * 128, (g == 0 and t == 0)) for t in range(2)]
                    if g + 1 < NGRP:
                        others += norm_items(xo, g + 1) + proj_items(True, False)
                    interleave(chains(True, g * TG), other_gen(others))
                check("A_end", oT[:, :])
                P.barrier()
            sW.close()

            with ExitStack() as sBC:
                xacc = alloc(sBC, "xacc", [128, 16 * D])
                xacc3 = xacc[:, :].rearrange("p (t f) -> p t f", t=16)
                h2T = alloc(sBC, "h2T", [128, 8 * T_HALF], BF16)
                h2T3 = h2T[:, :].rearrange("p (k t) -> p k t", k=8)
                PSZ = [6, 6, 5, 5]; POFF = [0, 6, 12, 17]
                wg_v = wg_d.rearrange("(k p) n -> p k n", p=128)
                wu_v = wu_d.rearrange("(k p) n -> p k n", p=128)
                wd_v = wd_d.rearrange("(k p) n -> p k n", p=128)
                wgs = [alloc(sBC, "wg0", [128, 8 * 768], BF16)]
                wus = [alloc(sBC, "wu0", [128, 8 * 768], BF16)]

                def load_gu(pi):
                    sl = pi % 2
                    n_ = PSZ[pi] * 128; c0 = POFF[pi] * 128
                    g3 = wgs[sl][:, :].rearrange("p (k c) -> p k c", k=8)
                    u3 = wus[sl][:, :].rearrange("p (k c) -> p k c", k=8)
                    for kc in range(8):
                        P.dma("pool", DMA(g3[:, kc, 0:n_], wg_v[:, kc, c0:c0 + n_]), w=["wg%d" % sl])
                        P.dma("pool", DMA(u3[:, kc, 0:n_], wu_v[:, kc, c0:c0 + n_]), w=["wu%d" % sl])

                load_gu(0)
                sB = ExitStack()
                wst = [alloc(sB, "wst%d" % i, [128, D]) for i in range(2)]
                wob = alloc(sB, "wob", [128, 8 * D], BF16)
                wob3 = wob[:, :].rearrange("p (k c) -> p k c", k=8)
                xs2 = alloc(sB, "xs2", [128, D], BF16)
                junk2 = alloc(sB, "junk2", [128, D], BF16)
                st3 = alloc(sB, "st3", [128, 2])

                wout_v = wout_d.rearrange("(k p) n -> p k n", p=128)
                for kc in range(8):
                    sl = kc % 2
                    P.dma("sp", DMA(wst[sl][:], wout_v[:, kc, :]), w=["wst%d" % sl])
                    P.op("dve", TT(wob3[:, kc, :], wst[sl][:], gateB[:, 0:1024], ALU.mult), r=["wst%d" % sl, "gateB"], w=["wob"])
                for t in range(16):
                    P.dma("sp", DMA(xacc3[:, t, :], xo[t * 128:(t + 1) * 128, :]), w=["xacc%d" % t])
                    for half in range(2):
                        pt, pk = psd()
                        for kc in range(8):
                            P.op("pe", MM(pt[:, :], oT3[:, kc, t * 128:(t + 1) * 128], wob3[:, kc, half * 512:(half + 1) * 512],
                                          start=(kc == 0), stop=(kc == 7)), r=["oT", "wob"], w=[pk])
                        P.op("dve", TT(xacc3[:, t, half * 512:(half + 1) * 512], xacc3[:, t, half * 512:(half + 1) * 512], pt[:, :], ALU.add),
                             r=[pk, "xacc%d" % t], w=["xacc%d" % t])
                    P.op("dve", MEMSET(st3[:, 0:1], 0.0), w=["st3"])
                    P.op("act", ACT(junk2[:], xacc3[:, t, :], AF.Square, accum_out=st3[:, 0:1]), r=["xacc%d" % t, "st3"], w=["junk2", "st3"])
                    P.op("act", ACT(st3[:, 1:2], st3[:, 0:1], AF.Ln, scale=1.0 / D, bias=EPS), r=["st3"], w=["st3"])
                    P.op("act", ACT(st3[:, 1:2], st3[:, 1:2], AF.Exp, scale=-0.5), r=["st3"], w=["st3"])
                    P.op("dve", TS(xs2[:], xacc3[:, t, :], st3[:, 1:2], ALU.mult), r=["xacc%d" % t, "st3"], w=["xs2"])
                    for half in range(2):
                        pt, pk = pssb()
                        for j in range(4):
                            kc = half * 4 + j
                            P.op("pe", TR(pt[:, j * 128:(j + 1) * 128], xs2[:, kc * 128:(kc + 1) * 128], identb[:, :]), r=["xs2", "identb"], w=[pk])
                        for j in range(4):
                            kc = half * 4 + j
                            P.op("act", ACT(h2T3[:, kc, t * 128:(t + 1) * 128], pt[:, j * 128:(j + 1) * 128], AF.Identity,
                                            scale=a2[:, kc:kc + 1], bias=s2c[:, kc:kc + 1]), r=[pk, "a2", "colv"], w=["h2T"])
                P.barrier()
                sB.close()
                sC = ExitStack()
                wgs.append(alloc(sC, "wg1", [128, 8 * 768], BF16))
                wus.append(alloc(sC, "wu1", [128, 8 * 768], BF16))
                wstc = [alloc(sC, "wstc%d" % i, [128, D]) for i in range(2)]
                wds = [oT[:, i * 6 * D:(i + 1) * 6 * D].rearrange("p (k c) -> p k c", k=6) for i in range(2)]
                actT = alloc(sC, "actT", [128, 6 * 512], BF16)
                actT3 = actT[:, :].rearrange("p (k t) -> p k t", k=6)
                sg = [alloc(sC, "sg%d" % i, [128, 512]) for i in range(2)]

                def load_d(pi):
                    sl = pi % 2
                    for fc in range(PSZ[pi]):
                        ws = wstc[fc % 2]; wk = "wstc%d" % (fc % 2)
                        P.dma("sp", DMA(ws[:], wd_v[:, POFF[pi] + fc, :]), w=[wk])
                        P.op("pool", TT(wds[sl][:, fc, :], ws[:], gateB[:, 1024:2048], ALU.mult), r=[wk, "gateB"], w=["wd%d" % sl])

                def ffn_pass(pi):
                    sl = pi % 2
                    nfc = PSZ[pi]
                    g3 = wgs[sl][:, :].rearrange("p (k c) -> p k c", k=8)
                    u3 = wus[sl][:, :].rearrange("p (k c) -> p k c", k=8)
                    for tg in range(4):
                        t0 = tg * 512
                        for fc in range(nfc):
                            pg_, pgk = psd()
                            for kc in range(8):
                                P.op("pe", MM(pg_[:, :], g3[:, kc, fc * 128:(fc + 1) * 128], h2T3[:, kc, t0:t0 + 512],
                                              start=(kc == 0), stop=(kc == 7)), r=["wg%d" % sl, "h2T"], w=[pgk])
                            pu_, puk = psd()
                            for kc in range(8):
                                P.op("pe", MM(pu_[:, :], u3[:, kc, fc * 128:(fc + 1) * 128], h2T3[:, kc, t0:t0 + 512],
                                              start=(kc == 0), stop=(kc == 7)), r=["wu%d" % sl, "h2T"], w=[puk])
                            s_ = sg[fc % 2]; sk_ = "sg%d" % (fc % 2)
                            P.op("act", ACT(s_[:], pg_[:, :], AF.Silu), r=[pgk], w=[sk_])
                            P.op("dve", TT(actT3[:, fc, :], s_[:], pu_[:, :], ALU.mult), r=[sk_, puk], w=["actT"])
                        for tt in range(4):
                            t = tg * 4 + tt
                            for half in range(2):
                                pt, pk = pss4()
                                for fc in range(nfc):
                                    P.op("pe", MM(pt[:, :], actT3[:, fc, tt * 128:(tt + 1) * 128], wds[sl][:, fc, half * 512:(half + 1) * 512],
                                                  start=(fc == 0), stop=(fc == nfc - 1)), r=["actT", "wd%d" % sl], w=[pk])
                                P.op("dve", TT(xacc3[:, t, half * 512:(half + 1) * 512], xacc3[:, t, half * 512:(half + 1) * 512], pt[:, :], ALU.add),
                                     r=[pk, "xacc%d" % t], w=["xacc%d" % t])

                load_d(0)
                for pi in range(4):
                    if pi + 1 < 4:
                        load_gu(pi + 1)
                        load_d(pi + 1)
                    ffn_pass(pi)
                for t in range(16):
                    P.dma("sp", DMA(y[t * 128:(t + 1) * 128, :], xacc3[:, t, :]), r=["xacc%d" % t], w=["y%d" % t])

        except _Stop:
            pass
        P.emit(nc)
    return nc


_NC_CACHE = {}


def _host_consts():
    ident = np.eye(128, dtype=np.float32)
    i = np.arange(64)
    utri = (i[:, None] <= i[None, :]).astype(np.float32)
    ones = np.ones((64, 64), np.float32)
    maskL = np.where(i[:, None] >= i[None, :], BIG, -BIG).astype(np.float32)
    maskU = np.where(i[None, :] >= i[:, None], BIG, -BIG).astype(np.float32)
    strictL = (i[:, None] > i[None, :]).astype(np.float32)
    def m_off(b):
        p = i[:, None]; f = i[None, :]
        return ((p // (2 * b) == f // (2 * b)) & (p % (2 * b) >= b) & (f % (2 * b) < b)).astype(np.float32)
    mts = [m_off(1).T.copy()] + [m_off(b) for b in (2, 4, 8, 16, 32)]
    st2 = lambda m: np.concatenate([m, m], axis=0)
    bd = lambda m: np.block([[m, np.zeros_like(m)], [np.zeros_like(m), m]])
    c64 = np.concatenate([bd(utri), bd(ones), st2(np.eye(64, dtype=np.float32)), st2(maskL), st2(maskU), st2(strictL),
                          st2(m_off(1))] + [st2(m) for m in mts], axis=1).astype(np.float32)
    key = np.arange(128)[:, None]
    q = np.arange(128)[None, :]
    slopes = 2.0 ** (-8.0 * (np.arange(8, dtype=np.float32) + 1.0) / 8)
    ab = np.zeros((128, 2, 8, 128), np.float32)
    for h in range(8):
        d0 = (q + 128 - key).astype(np.float32)
        ab[:, 0, h, :] = np.where(key > q, -slopes[h] * d0, -BIG)
        d1 = (q - key).astype(np.float32)
        ab[:, 1, h, :] = np.where(key <= q, -slopes[h] * d1, -BIG)
    return ident, c64, ab.reshape(128, 2048)


def make_in_maps(x, c, w_ada, b_ada, norm1_w, w_in, conv_w, a_log, dt_bias, gdn_norm_w,
                 q_norm_w, k_norm_w, sinks, w_out, norm2_w, w_gate, w_up, w_down):
    f = lambda a: np.ascontiguousarray(np.asarray(a, dtype=np.float32))
    ident, c64, abias = _host_consts()
    fm = lambda v: f(np.asarray(v).reshape(8, 128).T)
    rep = lambda v, p: f(np.broadcast_to(np.asarray(v).reshape(1, -1), (p, np.asarray(v).size)))
    convw = f(np.asarray(conv_w)[0, :, 0, :].reshape(4, 12, 128).transpose(2, 1, 0).reshape(128, 48))
    shared = {
        "w_ada": f(w_ada[0]), "b_ada_b": rep(b_ada[0], 128), "n1w": fm(norm1_w[0]), "n2w": fm(norm2_w[0]),
        "w_in": f(w_in[0]), "convw": convw, "alog_b": rep(a_log[0], 128), "dtb_b": rep(dt_bias[0], 128),
        "gnw_b": rep(gdn_norm_w[0], 128), "qnw_b": rep(q_norm_w[0], 128), "knw_b": rep(k_norm_w[0], 128),
        "sinks_b": rep(sinks[0], 128), "w_out": f(w_out[0]), "w_gate": f(w_gate[0]), "w_up": f(w_up[0]),
        "w_down": f(w_down[0]), "ident": ident, "c64": c64, "abias": abias,
    }
    x = np.asarray(x); c = np.asarray(c)
    maps = []
    for core in range(8):
        b, half = core // 2, core % 2
        m = dict(shared)
        m["xo"] = f(x[b, half * T_HALF:(half + 1) * T_HALF])
        m["xp"] = f(x[b, 0:T_HALF])
        m["flag"] = np.full((128, 1), float(half), np.float32)
        m["cfm"] = fm(c[b])
        maps.append(m)
    return maps


def kernel(**inputs):
    if "nc" not in _NC_CACHE:
        _NC_CACHE["nc"] = build_program()
    nc = _NC_CACHE["nc"]
    maps = make_in_maps(**inputs)
    res = run_bass_kernel_spmd(nc, maps, core_ids=list(range(8)))
    out = np.empty((4, 2 * T_HALF, D), np.float32)
    for core in range(8):
        b, half = core // 2, core % 2
        out[b, half * T_HALF:(half + 1) * T_HALF] = res.results[core]["y"]
    return out
```

```python
import numpy as np
import concourse.bass as bass
import concourse.mybir as mybir
from concourse.bass_utils import run_bass_kernel_spmd
from contextlib import ExitStack

F32 = mybir.dt.float32
BF16 = mybir.dt.bfloat16
ALU = mybir.AluOpType
AF = mybir.ActivationFunctionType
AX = mybir.AxisListType

D = 1024
T_HALF = 2048
TG = 256
NGRP = T_HALF // TG
CH = 64
PROJ = 2832
DFF = 2816
EPS = 1e-6
CHAIN_PRIO = 2
PRE_STAGGER = 3
BIG = 30000.0

ENGS = ("pe", "act", "dve", "pool", "sp")
NDMASEM = 6
ATTACH_WAITS = True


import sys as _sys


def _caller_line():
    f = _sys._getframe(2)
    lines = []
    while f is not None and len(lines) < 3:
        lines.append(f.f_lineno)
        f = f.f_back
    return lines


class Prog:
    def __init__(self):
        self.q = {e: [] for e in ENGS}
        self.last_w = {}
        self.readers = {}
        self.dma_uses = {}
        self.dma_rr = {e: 0 for e in ENGS}
        self.ncomp = {e: 0 for e in ENGS}

    def _deps(self, r, w):
        deps = set()
        for k in r:
            if k in self.last_w:
                deps.add(self.last_w[k])
        for k in w:
            if k in self.last_w:
                deps.add(self.last_w[k])
            for x in self.readers.get(k, ()):
                deps.add(x)
        return deps

    def _record(self, node, r, w):
        for k in r:
            self.readers.setdefault(k, []).append(node)
        for k in w:
            self.last_w[k] = node
            self.readers[k] = []

    def _waits(self, deps, eng, skip_same_pe=True):
        waits = {}
        for d in deps:
            if d[0] == "c":
                _, de, di = d
                if skip_same_pe and de == eng and eng == "pe":
                    continue
                key = ("c", de)
                waits[key] = max(waits.get(key, 0), di + 1)
            else:
                _, de, slot, use = d
                key = ("d", de, slot)
                waits[key] = max(waits.get(key, 0), 16 * (use + 1))
        return waits

    def op(self, eng, fn, r=(), w=()):
        for k in r:
            if k.startswith("ps") and eng != "pe":
                for x in self.readers.get(k, ()):
                    assert x[1] == eng or x[1] == "pe", ("PSUM bank read by two engines", k, eng, x, _caller_line())
        deps = self._deps(r, w)
        idx = self.ncomp[eng]
        self.ncomp[eng] += 1
        node = ("c", eng, idx)
        self.q[eng].append(dict(fn=fn, waits=self._waits(deps, eng), dma=None, line=_caller_line()))
        self._record(node, r, w)
        return node

    def dma(self, eng, fn, r=(), w=()):
        deps = self._deps(r, w)
        slot = self.dma_rr[eng] % NDMASEM
        self.dma_rr[eng] += 1
        use = self.dma_uses.get((eng, slot), 0)
        self.dma_uses[(eng, slot)] = use + 1
        node = ("d", eng, slot, use)
        waits = self._waits(deps, eng, skip_same_pe=False)
        if use > 0:
            key = ("d", eng, slot)
            waits[key] = max(waits.get(key, 0), 16 * use)
        self.q[eng].append(dict(fn=fn, waits=waits, dma=(slot, use), line=_caller_line()))
        self._record(node, r, w)
        return node

    def barrier(self):
        waits = {}
        for e in ENGS:
            if self.ncomp[e] > 0:
                waits[("c", e)] = self.ncomp[e]
        for (e, s), u in self.dma_uses.items():
            if u > 0:
                waits[("d", e, s)] = 16 * u
        for e in ENGS:
            w = {k: v for k, v in waits.items() if not (k[0] == "c" and k[1] == e)}
            self.q[e].append(dict(fn=None, waits=w, dma=None))

    def emit(self, nc, final_wait_eng="sp"):
        with ExitStack() as es:
            csem = {e: es.enter_context(nc.semaphore("c_" + e)) for e in ENGS}
            dsem = {}
            for e in ENGS:
                if self.dma_rr[e] > 0:
                    for s in range(NDMASEM):
                        dsem[(e, s)] = es.enter_context(nc.semaphore("d_%s%d" % (e, s)))
            block = es.enter_context(nc.Block())
            prog = self

            def sem_of(key):
                if key[0] == "c":
                    return csem[key[1]]
                return dsem[(key[1], key[2])]

            def run(engname, e):
                seen = {}
                for item in prog.q[engname]:
                    pend = []
                    for key, val in item["waits"].items():
                        if seen.get(key, 0) >= val:
                            continue
                        pend.append((key, val))
                        seen[key] = val
                    attach = None
                    if ATTACH_WAITS and pend and item["fn"] is not None and item["dma"] is None:
                        attach = pend.pop()
                    for key, val in pend:
                        e.wait_ge(sem_of(key), val)
                    if item["fn"] is None:
                        continue
                    try:
                        ins = item["fn"](e)
                    except Exception:
                        print("EMIT FAILED for op issued at line", item.get("line"), "engine", engname)
                        raise
                    if attach is not None:
                        ins._wait_ge(sem_of(attach[0]), attach[1])
                    if item["dma"] is not None:
                        slot, use = item["dma"]
                        ins.then_inc(dsem[(engname, slot)], 16)
                    else:
                        ins.then_inc(csem[engname], 1)
                if engname == final_wait_eng:
                    for en in ENGS:
                        n = prog.ncomp[en]
                        if n > 0 and en != engname:
                            e.wait_ge(csem[en], n)
                    for (en, s), sem in dsem.items():
                        u = prog.dma_uses.get((en, s), 0)
                        if u > 0:
                            e.wait_ge(sem, 16 * u)

            block.tensor(lambda e: run("pe", e))
            block.scalar(lambda e: run("act", e))
            block.vector(lambda e: run("dve", e))
            block.gpsimd(lambda e: run("pool", e))
            block.sync(lambda e: run("sp", e))


def MM(out, lhsT, rhs, start=True, stop=True):
    return lambda e: e.matmul(out, lhsT=lhsT, rhs=rhs, start=start, stop=stop)


def TR(out, in_, idn):
    return lambda e: e.transpose(out, in_, idn)


def ACT(out, in_, func, **kw):
    return lambda e: e.activation(out=out, in_=in_, func=func, **kw)


def TT(out, in0, in1, op):
    return lambda e: e.tensor_tensor(out=out, in0=in0, in1=in1, op=op)


def TS(out, in0, s1, op0, s2=None, op1=None):
    if op1 is None:
        return lambda e: e.tensor_scalar(out=out, in0=in0, scalar1=s1, scalar2=None, op0=op0)
    return lambda e: e.tensor_scalar(out=out, in0=in0, scalar1=s1, scalar2=s2, op0=op0, op1=op1)


def STT(out, in0, scalar, in1, op0, op1):
    return lambda e: e.scalar_tensor_tensor(out=out, in0=in0, scalar=scalar, in1=in1, op0=op0, op1=op1)


def CP(out, in_):
    return lambda e: e.tensor_copy(out, in_)


def ACP(out, in_):
    return lambda e: e.activation(out=out, in_=in_, func=AF.Copy)


def RED(out, in_, op=ALU.add):
    return lambda e: e.tensor_reduce(out=out, in_=in_, axis=AX.X, op=op)


def RCP(out, in_):
    return lambda e: e.reciprocal(out, in_)


def MEMSET(ap, v):
    return lambda e: e.memset(ap, v)


def DMA(out, in_):
    return lambda e: e.dma_start(out=out, in_=in_)


def b3(ap, P, h, d, axis):
    return ap.unsqueeze(axis).to_broadcast([P, h, d])


def v3(ap, h):
    return ap.rearrange("p (h d) -> p h d", h=h)


class _Stop(Exception):
    pass


def build_program(debug=None, stop=None):
    nc = bass.Bass("TRN2", target_bir_lowering=False)
    dram_in = lambda name, shape: nc.dram_tensor(name, list(shape), F32, kind="ExternalInput").ap()
    xo = dram_in("xo", [T_HALF, D])
    xp = dram_in("xp", [T_HALF, D])
    flag_d = dram_in("flag", [128, 1])
    cfm_d = dram_in("cfm", [128, 8])
    wada_d = dram_in("w_ada", [D, 6 * D])
    bada_d = dram_in("b_ada_b", [128, 6 * D])
    n1w_d = dram_in("n1w", [128, 8])
    n2w_d = dram_in("n2w", [128, 8])
    win_d = dram_in("w_in", [D, PROJ])
    convw_d = dram_in("convw", [128, 48])
    alog_d = dram_in("alog_b", [128, 8])
    dtb_d = dram_in("dtb_b", [128, 8])
    gnw_d = dram_in("gnw_b", [128, 64])
    qnw_d = dram_in("qnw_b", [128, 64])
    knw_d = dram_in("knw_b", [128, 64])
    sinks_d = dram_in("sinks_b", [128, 8])
    wout_d = dram_in("w_out", [D, D])
    wg_d = dram_in("w_gate", [D, DFF])
    wu_d = dram_in("w_up", [D, DFF])
    wd_d = dram_in("w_down", [DFF, D])
    ident_d = dram_in("ident", [128, 128])
    c64_d = dram_in("c64", [128, 256 + 11 * 64])
    abias_d = dram_in("abias", [128, 2 * 8 * 128])
    y = nc.dram_tensor("y", [T_HALF, D], F32, kind="ExternalOutput").ap()
    dbg = None
    dbgb = None
    if debug is not None:
        dbg = nc.dram_tensor("dbg", [128, debug], F32, kind="ExternalOutput").ap()
        dbgb = nc.dram_tensor("dbgb", [128, debug], BF16, kind="ExternalOutput").ap()

    P = Prog()
    top = ExitStack()

    def check(tag, ap=None):
        if stop != tag:
            return
        if ap is not None and dbg is not None:
            p, n = ap.shape[0], ap.shape[1]
            dst = dbgb if ap.dtype == BF16 else dbg
            P.dma("sp", DMA(dst[0:p, 0:n], ap), r=list(P.last_w.keys()), w=["dbg"])
        raise _Stop()

    ARENA_COLS = 53200
    arena_state = {"cur": 0, "ar": None, "marks": {}}

    def alloc(es, name, shape, dt=F32):
        st = arena_state
        if id(es) not in st["marks"]:
            st["marks"][id(es)] = st["cur"]
            mark = st["cur"]
            es.callback(lambda: st.__setitem__("cur", mark))
        p, n = shape
        nbytes = n * (4 if dt == F32 else 2)
        ncols = (nbytes + 3) // 4
        ncols = (ncols + 7) // 8 * 8
        a0 = st["cur"]
        st["cur"] += ncols
        assert st["cur"] <= ARENA_COLS, ("SBUF arena overflow", name, st["cur"])
        ap = st["ar"][0:p, a0:a0 + ncols]
        if dt != F32:
            ap = ap.bitcast(dt)
        return ap[:, 0:n]

    with top:
        arena_state["ar"] = top.enter_context(nc.sbuf_tensor("arena", [128, ARENA_COLS], F32))
        psb = [top.enter_context(nc.psum_tensor("ps%d" % i, [128, 512], F32)) for i in range(8)]
        ps_rr = {"d": 0, "s": 0}

        def psd():
            i = ps_rr["d"] % 2
            ps_rr["d"] += 1
            return psb[i], "ps%d" % i

        def pss():
            i = 6 + ps_rr["s"] % 2
            ps_rr["s"] += 1
            return psb[i], "ps%d" % i

        ps4_rr = [0]

        def pss4():
            i = 2 + ps4_rr[0] % 6
            ps4_rr[0] += 1
            return psb[i], "ps%d" % i

        def pssb():
            t, k = pss()
            return t[:, :].bitcast(BF16), k

        ps_set_rr = [0, 0]

        def pset(bs):
            i = 2 + 2 * bs + ps_set_rr[bs] % 2
            ps_set_rr[bs] += 1
            return psb[i], "ps%d" % i

        def psetb(bs):
            t, k = pset(bs)
            return t[:, :].bitcast(BF16), k

        def pfix(i):
            return psb[i], "ps%d" % i

        def pfixb(i):
            return psb[i][:, :].bitcast(BF16), "ps%d" % i

        ident = alloc(top, "ident", [128, 128])
        c64 = alloc(top, "c64", [128, 960])
        utri = c64[:, 0:128]; ones_bd = c64[:, 128:256]
        I2, maskL, maskU, strictL, mA1 = [c64[:, 256 + i * 64:256 + (i + 1) * 64] for i in range(5)]
        mTs = [c64[:, 256 + (5 + i) * 64:256 + (6 + i) * 64] for i in range(6)]
        flag = alloc(top, "flag", [128, 1])
        colv = alloc(top, "colv", [128, 32])
        a1 = alloc(top, "a1", [128, 8]); a2 = alloc(top, "a2", [128, 8])
        n1w = alloc(top, "n1w", [128, 8]); n2w = alloc(top, "n2w", [128, 8])
        gateB = alloc(top, "gateB", [128, 2048])
        identb = alloc(top, "identb", [128, 128], BF16)
        oT = alloc(top, "oT", [128, 8 * T_HALF], BF16)
        oT3 = oT[:, :].rearrange("p (k t) -> p k t", k=8)

        for dst, src, key in ((ident, ident_d, "ident"), (c64, c64_d, "c64"),
                              (flag, flag_d, "flag"), (n1w, n1w_d, "n1w"), (n2w, n2w_d, "n2w")):
            P.dma("sp", DMA(dst[:], src), w=[key])
        P.op("dve", CP(identb[:], ident[:]), r=["ident"], w=["identb"])

        try:
            sW = ExitStack()
            win = alloc(sW, "win", [128, 8 * PROJ], BF16)
            win3 = win[:, :].rearrange("p (k c) -> p k c", k=8)
            win_v = win_d.rearrange("(k p) n -> p k n", p=128)
            with ExitStack() as s0:
                cfm = alloc(s0, "cfm", [128, 8])
                cactB = alloc(s0, "cactB", [128, 8 * 128], BF16)
                NWA = 3
                wa = [alloc(s0, "wa%d" % i, [128, 8 * 512], BF16) for i in range(NWA)]
                ba = [alloc(s0, "ba%d" % i, [128, 512]) for i in range(NWA)]
                modc = alloc(s0, "modc", [128, 512])
                dtmp = alloc(s0, "dtmp", [128, 512])
                P.dma("sp", DMA(cfm[:], cfm_d), w=["cfm"])
                P.op("act", ACT(cfm[:], cfm[:], AF.Silu), r=["cfm"], w=["cfm"])
                P.op("dve", CP(v3(cactB[:, :], 8), b3(cfm[:, :], 128, 8, 128, 2)), r=["cfm"], w=["cactB"])
                wada_v = wada_d.rearrange("(k p) n -> p k n", p=128)
                for ci in range(12):
                    sl = ci % NWA
                    vec, half = ci // 2, ci % 2
                    P.dma("pool", DMA(v3(wa[sl][:, :], 8), wada_v[:, :, ci * 512:(ci + 1) * 512]), w=["wa%d" % sl])
                    if ci in (2, 5):
                        for kc in range((ci // 3) * 4, (ci // 3) * 4 + 4):
                            P.dma("pool", DMA(win3[:, kc, :], win_v[:, kc, :]), w=["win"])
                    P.dma("sp", DMA(ba[sl][:], bada_d[:, ci * 512:(ci + 1) * 512]), w=["ba%d" % sl])
                    pt, pk = psd()
                    for kc in range(8):
                        P.op("pe", MM(pt[:, :], cactB[:, kc * 128:(kc + 1) * 128], wa[sl][:, kc * 512:(kc + 1) * 512],
                                      start=(kc == 0), stop=(kc == 7)), r=["cactB", "wa%d" % sl], w=[pk])
                    if vec in (2, 5):
                        g0 = (0 if vec == 2 else 1024) + half * 512
                        P.op("dve", TT(gateB[:, g0:g0 + 512], pt[:, :], ba[sl][:], ALU.add), r=[pk, "ba%d" % sl], w=["gateB"])
                    else:
                        vi = {0: 0, 1: 1, 3: 2, 4: 3}[vec]
                        P.op("dve", TT(modc[:], pt[:, :], ba[sl][:], ALU.add), r=[pk, "ba%d" % sl], w=["modc"])
                        P.op("dve", TT(v3(dtmp[:, :], 4), v3(modc[:, :], 4), b3(ident[:, :], 128, 4, 128, 1), ALU.mult),
                             r=["modc", "ident"], w=["dtmp"])
                        c0 = vi * 8 + half * 4
                        P.op("dve", RED(colv[:, c0:c0 + 4], v3(dtmp[:, :], 4)), r=["dtmp"], w=["colv"])
                P.op("dve", STT(a1[:], colv[:, 8:16], 1.0, n1w[:], ALU.add, ALU.mult), r=["colv", "n1w"], w=["a1"])
                P.op("dve", STT(a2[:], colv[:, 24:32], 1.0, n2w[:], ALU.add, ALU.mult), r=["colv", "n2w"], w=["a2"])
                P.barrier()
                check("p0_colv", colv[:, :])
                check("p0_gate", gateB[:, :])
            s1c = colv[:, 0:8]
            s2c = colv[:, 16:24]

            with ExitStack() as sA:
                abias = alloc(sA, "abias", [128, 2048])
                convw = alloc(sA, "convw", [128, 48])
                negA = alloc(sA, "negA", [128, 8]); dtb = alloc(sA, "dtb", [128, 8])
                gnw = alloc(sA, "gnw", [128, 64])
                qnw = alloc(sA, "qnw", [128, 64]); knw = alloc(sA, "knw", [128, 64])
                esink = alloc(sA, "esink", [128, 8])
                for dst, src, key in ((abias, abias_d, "abias"), (convw, convw_d, "convw"), (negA, alog_d, "negA"),
                                      (dtb, dtb_d, "dtb"), (gnw, gnw_d, "gnw"), (qnw, qnw_d, "qnw"), (knw, knw_d, "knw"),
                                      (esink, sinks_d, "esink")):
                    P.dma("sp", DMA(dst[:], src), w=[key])
                P.op("act", ACT(negA[:], negA[:], AF.Exp), r=["negA"], w=["negA"])
                P.op("dve", TS(negA[:], negA[:], -1.0, ALU.mult), r=["negA"], w=["negA"])
                P.op("act", ACT(esink[:], esink[:], AF.Exp), r=["esink"], w=["esink"])
                P.op("dve", TS(qnw[:], qnw[:], 0.125, ALU.mult), r=["qnw"], w=["qnw"])
                xt = alloc(sA, "xt", [128, D])
                st2 = alloc(sA, "st2", [128, 2])
                hT = alloc(sA, "hT", [128, 8 * TG], BF16)
                hT3 = hT[:, :].rearrange("p (k t) -> p k t", k=8)
                halo = alloc(sA, "halo", [128, 36])
                praw = [alloc(sA, "praw%d" % i, [128, TG + 3]) for i in range(2)]
                cacc = alloc(sA, "cacc", [128, TG])
                cs = alloc(sA, "cs", [128, 12 * TG], BF16)
                NSET = 2
                tqs = [[alloc(sA, "tqkv%d_%d" % (i, b_), [128, 512], BF16) for i in range(3)] for b_ in range(NSET)]
                Sb = alloc(sA, "Sb", [64, 512], BF16)
                zts = [alloc(sA, "zt%d" % b_, [128, 512], BF16) for b_ in range(NSET)]
                S = alloc(sA, "S", [64, 512])
                smn = ("gab", "g", "e1", "beta", "nbeta", "G16", "eG", "eGL", "eGLB", "dG", "eGLmG", "beG", "ssk", "rk", "ssq", "rq", "sso", "ro")
                smw = {"gab": 16, "G16": 24, "eG": 24}
                sms = [{n: alloc(sA, "sm%d_%s" % (b_, n), [128, smw.get(n, 8)]) for n in smn if n not in ("eGL", "eGLB")} for b_ in range(NSET)]
                for m__ in sms:
                    m__["eGG"] = m__["eG"]
                    m__["eGL"] = m__["eGG"][:, 8:16]
                    m__["eGLB"] = m__["eGG"][:, 16:24]
                    m__["eG"] = m__["eGG"][:, 0:8]
                gts = []
                for b_ in range(NSET):
                    d_ = {n: alloc(sA, "gt%d_%s" % (b_, n), [128, 512]) for n in ("t0", "t1", "o")}
                    d_.update({n: alloc(sA, "gt%d_%s" % (b_, n), [128, 512], BF16) for n in
                               ("D", "DT", "nbs", "bv", "kn", "qn", "P0", "P1", "Q0", "Q1", "W0", "W1", "T0", "T1",
                                "QKmT", "kd2", "r", "vn", "og")})
                    d_.update({n: alloc(sA, "gt%d_%s" % (b_, n), [64, 1024], BF16) for n in ("knT", "qnT")})
                    gts.append(d_)
                sw = {n: alloc(sA, "sw_" + n, [128, 512]) for n in ("t0", "qn", "st0", "st1", "os", "qraw")}
                sw["kv"] = sw["qraw"]
                sw["qn"] = sw["qn"][:, :].bitcast(BF16)[:, 0:512]
                sw["os"] = sw["os"][:, :].bitcast(BF16)[:, 0:512]
                junk = sw["t0"][:, :].bitcast(BF16)
                swkn = alloc(sA, "swkn", [128, 128], BF16)
                swsm = {n: alloc(sA, "swsm_" + n, [128, 8]) for n in ("ssq", "rq", "ssk", "rk", "den", "rden")}
                qT = alloc(sA, "qT", [64, 1024], BF16)
                kT = [alloc(sA, "kT%d" % i, [64, 256], BF16) for i in range(2)]
                vb1 = [alloc(sA, "vb1%d" % i, [128, 130], BF16) for i in range(2)]
                pT = [alloc(sA, "pT%d" % i, [128, 512], BF16) for i in range(4)]

                P.op("dve", MEMSET(halo[:], 0.0), w=["halo"])
                P.op("dve", MEMSET(S[:], 0.0), w=["S"])
                P.op("dve", MEMSET(Sb[:], 0.0), w=["Sb"])
                for i in range(2):
                    P.op("dve", MEMSET(vb1[i][:], 1.0), w=["vb1%d" % i])
                    P.op("dve", MEMSET(kT[i][:], 0.0), w=["kT%d" % i])

                id64 = ident[0:64, 0:64]
                idb64 = identb[0:64, 0:64]
                swa_blk = [0]

                def norm_to_hT(xsrc, row0, tcol):
                    P.dma("sp", DMA(xt[:], xsrc[row0:row0 + 128, :]), w=["xt"])
                    P.op("dve", MEMSET(st2[:, 0:1], 0.0), w=["st2"])
                    P.op("act", ACT(junk[:], xt[:], AF.Square, accum_out=st2[:, 0:1]), r=["xt", "st2"], w=["sw_t0", "st2"])
                    P.op("act", ACT(st2[:, 1:2], st2[:, 0:1], AF.Ln, scale=1.0 / D, bias=EPS), r=["st2"], w=["st2"])
                    P.op("act", ACT(st2[:, 1:2], st2[:, 1:2], AF.Exp, scale=-0.5), r=["st2"], w=["st2"])
                    P.op("dve", TS(junk[:], xt[:], st2[:, 1:2], ALU.mult), r=["xt", "st2", "sw_t0"], w=["sw_t0"])
                    for half in range(2):
                        pt, pk = pssb()
                        for j in range(4):
                            kc = half * 4 + j
                            P.op("pe", TR(pt[:, j * 128:(j + 1) * 128], junk[:, kc * 128:(kc + 1) * 128], identb[:, :]),
                                 r=["sw_t0", "identb"], w=[pk])
                        for j in range(4):
                            kc = half * 4 + j
                            P.op("act", ACT(hT3[:, kc, tcol:tcol + 128], pt[:, j * 128:(j + 1) * 128], AF.Identity,
                                            scale=a1[:, kc:kc + 1], bias=s1c[:, kc:kc + 1]),
                                 r=[pk, "a1", "colv"], w=["hT"])

                def proj_fm(cc, do_conv):
                    pr = praw[cc % 2]; prk = "praw%d" % (cc % 2)
                    pt, pk = psd()
                    for kc in range(8):
                        P.op("pe", MM(pt[:, 0:TG], win3[:, kc, cc * 128:(cc + 1) * 128], hT3[:, kc, :],
                                      start=(kc == 0), stop=(kc == 7)), r=["win", "hT"], w=[pk])
                    P.op("dve", CP(pr[:, 0:3], halo[:, cc * 3:cc * 3 + 3]), r=["halo"], w=[prk])
                    P.op("act", ACP(pr[:, 3:3 + TG], pt[:, 0:TG]), r=[pk], w=[prk])
                    P.op("dve", CP(halo[:, cc * 3:cc * 3 + 3], pr[:, TG:TG + 3]), r=[prk], w=["halo"])
                    if not do_conv:
                        return
                    P.op("dve", TS(cacc[:], pr[:, 0:TG], convw[:, cc * 4:cc * 4 + 1], ALU.mult), r=[prk, "convw"], w=["cacc"])
                    for j in range(1, 4):
                        P.op("dve", STT(cacc[:], pr[:, j:j + TG], convw[:, cc * 4 + j:cc * 4 + j + 1], cacc[:], ALU.mult, ALU.add),
                             r=[prk, "convw", "cacc"], w=["cacc"])
                    P.op("act", ACT(cs[:, cc * TG:(cc + 1) * TG], cacc[:], AF.Silu), r=["cacc"], w=["cs"])

                RA = slice(0, 64); RB = slice(64, 128)
                idbA = identb[0:64, 0:64]; idbB = identb[64:128, 64:128]

                def tok_major(typ, tcol, bs):
                    pt, pk = psetb(bs)
                    for j in range(4):
                        cc = typ * 4 + j
                        P.op("pe", TR(pt[:, j * 128:(j + 1) * 128], cs[:, cc * TG + tcol:cc * TG + tcol + 128], identb[:, :]),
                             r=["cs", "identb"], w=[pk])
                    P.op("act", ACP(tqs[bs][typ][:], pt[:, 0:512]), r=[pk], w=["tqkv%d_%d" % (typ, bs)])

                def l2n(src, srck, ss, rr, dst, dstk, extra_bias, bs):
                    g_ = gts[bs]; m_ = sms[bs]
                    K = lambda nm: "%s_%d" % (nm, bs)
                    P.op("dve", TT(g_["t0"][:], src[:], src[:], ALU.mult), r=[srck], w=[K("gt_t0")])
                    P.op("dve", RED(m_[ss][:, 0:8], v3(g_["t0"][:, :], 8)), r=[K("gt_t0")], w=[K("sm_" + ss)])
                    yield
                    P.op("act", ACT(m_[rr][:, 0:8], m_[ss][:, 0:8], AF.Ln, bias=EPS), r=[K("sm_" + ss)], w=[K("sm_" + rr)])
                    P.op("act", ACT(m_[rr][:, 0:8], m_[rr][:, 0:8], AF.Exp, scale=-0.5, bias=extra_bias), r=[K("sm_" + rr)], w=[K("sm_" + rr)])
                    yield
                    P.op("dve", TT(v3(dst[:, :], 8), v3(src[:, :], 8), b3(m_[rr][:, 0:8], 128, 8, 64, 2), ALU.mult),
                         r=[srck, K("sm_" + rr)], w=[dstk])
                    yield

                def tt_mms(outp, pk, lhs, lhsk, rhs, rhsk):
                    for h in range(8):
                        hs = slice(h * 64, (h + 1) * 64)
                        for R in (RA, RB):
                            P.op("pe", MM(outp[R, hs], lhs[R, hs], rhs[R, hs]), r=[lhsk, rhsk], w=[pk])

                def tt_trs(outp, pk, src, srck):
                    for h in range(8):
                        hs = slice(h * 64, (h + 1) * 64)
                        P.op("pe", TR(outp[RA, hs], src[RA, hs], idbA), r=[srck, "identb"], w=[pk])
                        P.op("pe", TR(outp[RB, hs], src[RB, hs], idbB), r=[srck, "identb"], w=[pk])

                def fm_mms(outp, pk, lhsT3, lhsk, rhs3, rhsk):
                    for h in range(8):
                        hs = slice(h * 64, (h + 1) * 64)
                        for R in (RA, RB):
                            P.op("pe", MM(outp[R, hs], lhsT3[:, h, R], rhs3[:, h, R]), r=[lhsk, rhsk], w=[pk])

                def gdn_pre(n2, full, bs):
                    tcol = n2 * 128
                    g_ = gts[bs]; m_ = sms[bs]; tq_ = tqs[bs]
                    K = lambda nm: "%s_%d" % (nm, bs)
                    knT3 = g_["knT"][:, :].rearrange("p (h t) -> p h t", h=8)
                    qnT3 = g_["qnT"][:, :].rearrange("p (h t) -> p h t", h=8)
                    pg, pgk = pset(bs)
                    for kc in range(8):
                        P.op("pe", MM(pg[:, 0:16], hT3[:, kc, tcol:tcol + 128], win3[:, kc, 2048:2064],
                                      start=(kc == 0), stop=(kc == 7)), r=["hT", "win"], w=[pgk])
                    if full:
                        pz, pzk = psd()
                        for kc in range(8):
                            P.op("pe", MM(pz[:, :], hT3[:, kc, tcol:tcol + 128], win3[:, kc, 1536:2048],
                                          start=(kc == 0), stop=(kc == 7)), r=["hT", "win"], w=[pzk])
                        P.op("act", ACT(zts[bs][:], pz[:, :], AF.Silu), r=[pzk], w=[K("zt")])
                    yield
                    P.op("dve", CP(m_["gab"][:, 0:16], pg[:, 0:16]), r=[pgk], w=[K("sm_gab")])
                    P.op("dve", TT(m_["g"][:, 0:8], m_["gab"][:, 0:8], dtb[:], ALU.add), r=[K("sm_gab"), "dtb"], w=[K("sm_g")])
                    yield
                    P.op("act", ACT(m_["e1"][:, 0:8], m_["g"][:, 0:8], AF.Exp), r=[K("sm_g")], w=[K("sm_e1")])
                    P.op("act", ACT(m_["e1"][:, 0:8], m_["e1"][:, 0:8], AF.Ln, bias=1.0), r=[K("sm_e1")], w=[K("sm_e1")])
                    P.op("act", ACT(m_["beta"][:, 0:8], m_["gab"][:, 8:16], AF.Exp, scale=-1.0), r=[K("sm_gab")], w=[K("sm_beta")])
                    yield
                    P.op("dve", TT(m_["g"][:, 0:8], m_["e1"][:, 0:8], negA[:], ALU.mult), r=[K("sm_e1"), "negA"], w=[K("sm_g")])
                    P.op("dve", TS(m_["beta"][:, 0:8], m_["beta"][:, 0:8], 1.0, ALU.add), r=[K("sm_beta")], w=[K("sm_beta")])
                    P.op("dve", RCP(m_["beta"][:, 0:8], m_["beta"][:, 0:8]), r=[K("sm_beta")], w=[K("sm_beta")])
                    P.op("dve", TS(m_["nbeta"][:, 0:8], m_["beta"][:, 0:8], -1.0, ALU.mult), r=[K("sm_beta")], w=[K("sm_nbeta")])
                    yield
                    pa, pak = pset(bs)
                    P.op("pe", MM(pa[:, 0:8], utri, m_["g"][:, 0:8]), r=["c64", K("sm_g")], w=[pak])
                    P.op("pe", MM(pa[:, 8:16], ones_bd, m_["g"][:, 0:8]), r=["c64", K("sm_g")], w=[pak])
                    P.op("pe", MM(pa[0:64, 16:24], ones_bd[:, 64:128], m_["g"][:, 0:8]), r=["c64", K("sm_g")], w=[pak])
                    yield
                    P.op("dve", CP(m_["G16"][:, 0:16], pa[:, 0:16]), r=[pak], w=[K("sm_G16")])
                    P.op("dve", CP(m_["G16"][0:64, 16:24], pa[0:64, 16:24]), r=[pak], w=[K("sm_G16")])
                    tok_major(1, tcol, bs); tok_major(2, tcol, bs)
                    if full:
                        tok_major(0, tcol, bs)
                    yield "inputs_done"
                    G = m_["G16"][:, 0:8]; GL = m_["G16"][:, 8:16]
                    P.op("dve", TT(m_["dG"][:, 0:8], GL, G, ALU.subtract), r=[K("sm_G16")], w=[K("sm_dG")])
                    P.op("dve", TT(v3(g_["t1"][:, :], 8), b3(I2, 128, 8, 64, 1), b3(G, 128, 8, 64, 2), ALU.mult),
                         r=["c64", K("sm_G16")], w=[K("gt_t1")])
                    yield
                    P.op("act", ACT(m_["eGG"][:, 0:24], m_["G16"][:, 0:24], AF.Exp), r=[K("sm_G16")], w=[K("sm_eG"), K("sm_eGL"), K("sm_eGLB")])
                    P.op("act", ACT(m_["eGLmG"][:, 0:8], m_["dG"][:, 0:8], AF.Exp), r=[K("sm_dG")], w=[K("sm_eGLmG")])
                    pG, pGk = pset(bs)
                    P.op("pe", MM(pG[:, :], ones_bd, g_["t1"][:, :]), r=["c64", K("gt_t1")], w=[pGk])
                    yield
                    P.op("dve", TT(v3(g_["t0"][:, :], 8), b3(G, 128, 8, 64, 2), v3(pG[:, :], 8), ALU.subtract),
                         r=[K("sm_G16"), pGk], w=[K("gt_t0")])
                    P.op("dve", TT(v3(g_["t1"][:, :], 8), v3(g_["t0"][:, :], 8), b3(maskL, 128, 8, 64, 1), ALU.min),
                         r=[K("gt_t0"), "c64", K("gt_t1")], w=[K("gt_t1")])
                    yield
                    P.op("act", ACT(g_["D"][:], g_["t1"][:], AF.Exp), r=[K("gt_t1")], w=[K("gt_D")])
                    if full:
                        P.op("dve", STT(v3(g_["t0"][:, :], 8), v3(g_["t0"][:, :], 8), -1.0, b3(maskU, 128, 8, 64, 1), ALU.mult, ALU.min),
                             r=[K("gt_t0"), "c64"], w=[K("gt_t0")])
                        yield
                        P.op("act", ACT(g_["DT"][:], g_["t0"][:], AF.Exp), r=[K("gt_t0")], w=[K("gt_DT")])
                    yield
                    P.op("dve", TT(m_["beG"][:, 0:8], m_["beta"][:, 0:8], m_["eG"][:, 0:8], ALU.mult), r=[K("sm_beta"), K("sm_eG")], w=[K("sm_beG")])
                    tk, tv = tq_[1], tq_[2]
                    for _ in l2n(tk, K("tqkv1"), "ssk", "rk", g_["kn"], K("gt_kn"), 0.0, bs):
                        yield
                    pt, pk = psetb(bs)
                    for h in range(8):
                        P.op("pe", TR(pt[0:64, h * 128:(h + 1) * 128], g_["kn"][:, h * 64:(h + 1) * 64], identb[:, :]), r=[K("gt_kn"), "identb"], w=[pk])
                    yield
                    P.op("act", ACP(g_["knT"][:], pt[0:64, 0:1024]), r=[pk], w=[K("gt_knT")])
                    if full:
                        for _ in l2n(tq_[0], K("tqkv0"), "ssq", "rq", g_["qn"], K("gt_qn"), float(np.log(0.125)), bs):
                            yield
                        pt, pk = psetb(bs)
                        for h in range(8):
                            P.op("pe", TR(pt[0:64, h * 128:(h + 1) * 128], g_["qn"][:, h * 64:(h + 1) * 64], identb[:, :]), r=[K("gt_qn"), "identb"], w=[pk])
                        yield
                        P.op("act", ACP(g_["qnT"][:], pt[0:64, 0:1024]), r=[pk], w=[K("gt_qnT")])
                    pK, pKk = pset(bs); fm_mms(pK, pKk, knT3, K("gt_knT"), knT3, K("gt_knT"))
                    yield
                    P.op("dve", TT(g_["t1"][:], pK[:, :], g_["D"][:], ALU.mult), r=[pKk, K("gt_D"), K("gt_t1")], w=[K("gt_t1")])
                    if full:
                        pQ, pQk = pset(bs); fm_mms(pQ, pQk, knT3, K("gt_knT"), qnT3, K("gt_qnT"))
                        yield
                        P.op("dve", TT(g_["QKmT"][:], pQ[:, :], g_["DT"][:], ALU.mult), r=[pQk, K("gt_DT")], w=[K("gt_QKmT")])
                    P.op("dve", TT(v3(g_["nbs"][:, :], 8), b3(strictL, 128, 8, 64, 1), b3(m_["nbeta"][:, 0:8], 128, 8, 64, 2), ALU.mult),
                         r=["c64", K("sm_nbeta")], w=[K("gt_nbs")])
                    P.op("dve", TT(g_["P0"][:], g_["t1"][:], g_["nbs"][:], ALU.mult), r=[K("gt_t1"), K("gt_nbs")], w=[K("gt_P0")])
                    yield
                    pt, pk = psetb(bs); tt_trs(pt, pk, g_["P0"], K("gt_P0"))
                    N_, NT_ = g_["P0"], g_["Q0"]
                    P.op("dve", TT(v3(g_["T0"][:, :], 8), v3(N_[:, :], 8), b3(mA1, 128, 8, 64, 1), ALU.mult), r=[K("gt_P0"), "c64"], w=[K("gt_T0")])
                    P.op("dve", TT(v3(g_["T0"][:, :], 8), v3(g_["T0"][:, :], 8), b3(I2, 128, 8, 64, 1), ALU.add), r=[K("gt_T0"), "c64"], w=[K("gt_T0")])
                    yield
                    P.op("act", ACP(g_["Q0"][:], pt[:, 0:512]), r=[pk], w=[K("gt_Q0")])
                    yield
                    P.op("dve", TT(v3(g_["W0"][:, :], 8), v3(NT_[:, :], 8), b3(mTs[0], 128, 8, 64, 1), ALU.mult), r=[K("gt_Q0"), "c64"], w=[K("gt_W0")])
                    P.op("dve", TT(v3(g_["W0"][:, :], 8), v3(g_["W0"][:, :], 8), b3(I2, 128, 8, 64, 1), ALU.add), r=[K("gt_W0"), "c64"], w=[K("gt_W0")])
                    P.op("dve", TT(v3(g_["bv"][:, :], 8), v3(tv[:, :], 8), b3(m_["beta"][:, 0:8], 128, 8, 64, 2), ALU.mult),
                         r=[K("tqkv2"), K("sm_beta")], w=[K("gt_bv")])
                    P.op("dve", TT(v3(g_["kd2"][:, :], 8), v3(g_["kn"][:, :], 8), b3(m_["eGLmG"][:, 0:8], 128, 8, 64, 2), ALU.mult),
                         r=[K("gt_kn"), K("sm_eGLmG")], w=[K("gt_kd2")])
                    ct_, cw_ = 0, 0
                    for lv in range(1, 6):
                        Tc, Tn = g_["T%d" % ct_], g_["T%d" % (1 - ct_)]
                        Tck, Tnk = K("gt_T%d" % ct_), K("gt_T%d" % (1 - ct_))
                        Wc, Wn = g_["W%d" % cw_], g_["W%d" % (1 - cw_)]
                        Wck, Wnk = K("gt_W%d" % cw_), K("gt_W%d" % (1 - cw_))
                        p1, p1k = pset(bs); tt_mms(p1, p1k, NT_, K("gt_Q0"), Tc, Tck)
                        yield
                        P.op("dve", TT(v3(g_["Q1"][:, :], 8), v3(p1[:, :], 8), b3(mTs[lv], 128, 8, 64, 1), ALU.mult), r=[p1k, "c64"], w=[K("gt_Q1")])
                        yield
                        p2, p2k = pset(bs); tt_mms(p2, p2k, Wc, Wck, g_["Q1"], K("gt_Q1"))
                        yield
                        P.op("dve", TT(Tn[:], Tc[:], p2[:, :], ALU.add), r=[Tck, p2k], w=[Tnk])
                        yield
                        p3, p3k = psetb(bs); tt_trs(p3, p3k, Tn, Tnk)
                        yield
                        P.op("act", ACP(Wn[:], p3[:, 0:512]), r=[p3k], w=[Wnk])
                        yield
                        ct_, cw_ = 1 - ct_, 1 - cw_
                    gdn_W[bs] = (g_["W%d" % cw_], K("gt_W%d" % cw_))

                gdn_W = [None, None]

                def gdn_chain(n2, full, tok0, bs):
                    g_ = gts[bs]; m_ = sms[bs]
                    K = lambda nm: "%s_%d" % (nm, bs)
                    knT3 = g_["knT"][:, :].rearrange("p (h t) -> p h t", h=8)
                    qnT3 = g_["qnT"][:, :].rearrange("p (h t) -> p h t", h=8)
                    W, Wk = gdn_W[bs]
                    yield "chain"
                    for R in (RA, RB):
                        pC, pCk = pset(bs)
                        for h in range(8):
                            hs = slice(h * 64, (h + 1) * 64)
                            P.op("pe", MM(pC[R, hs], knT3[:, h, R], Sb[:, hs]), r=[K("gt_knT"), "Sb"], w=[pCk])
                        if full:
                            pO1, pO1k = pset(bs)
                            for h in range(8):
                                hs = slice(h * 64, (h + 1) * 64)
                                P.op("pe", MM(pO1[R, hs], qnT3[:, h, R], Sb[:, hs]), r=[K("gt_qnT"), "Sb"], w=[pO1k])
                        yield
                        P.op("dve", TT(v3(g_["t0"][R, :], 8), v3(pC[R, :], 8), b3(m_["beG"][R, 0:8], 64, 8, 64, 2), ALU.mult),
                             r=[pCk, K("sm_beG")], w=[K("gt_t0")])
                        P.op("dve", TT(g_["r"][R, :], g_["bv"][R, :], g_["t0"][R, :], ALU.subtract), r=[K("gt_bv"), K("gt_t0")], w=[K("gt_r")])
                        if full:
                            P.op("dve", TT(v3(g_["o"][R, :], 8), v3(pO1[R, :], 8), b3(m_["eG"][R, 0:8], 64, 8, 64, 2), ALU.mult),
                                 r=[pO1k, K("sm_eG")], w=[K("gt_o")])
                        yield
                        pV, pVk = pset(bs)
                        for h in range(8):
                            hs = slice(h * 64, (h + 1) * 64)
                            P.op("pe", MM(pV[R, hs], W[R, hs], g_["r"][R, hs]), r=[Wk, K("gt_r")], w=[pVk])
                        yield
                        P.op("act", ACP(g_["vn"][R, :], pV[R, :]), r=[pVk], w=[K("gt_vn")])
                        yield
                        pS, pSk = pset(bs)
                        for h in range(8):
                            hs = slice(h * 64, (h + 1) * 64)
                            P.op("pe", MM(pS[0:64, hs], g_["kd2"][R, hs], g_["vn"][R, hs]), r=[K("gt_kd2"), K("gt_vn")], w=[pSk])
                        egl = m_["eGL"][0:64, 0:8] if R is RA else m_["eGLB"][0:64, 0:8]
                        P.op("dve", TT(v3(S[:, :], 8), v3(S[:, :], 8), b3(egl, 64, 8, 64, 2), ALU.mult), r=["S", K("sm_eGL"), K("sm_eGLB")], w=["S"])
                        yield
                        P.op("dve", TT(Sb[:], S[:], pS[0:64, :], ALU.add), r=["S", pSk], w=["Sb"])
                        P.op("dve", TT(S[:], S[:], pS[0:64, :], ALU.add), r=["S", pSk], w=["S"])
                        if full:
                            pO2, pO2k = pset(bs)
                            for h in range(8):
                                hs = slice(h * 64, (h + 1) * 64)
                                P.op("pe", MM(pO2[R, hs], g_["QKmT"][R, hs], g_["vn"][R, hs]), r=[K("gt_QKmT"), K("gt_vn")], w=[pO2k])
                        yield
                        if full:
                            P.op("dve", TT(g_["o"][R, :], g_["o"][R, :], pO2[R, :], ALU.add), r=[K("gt_o"), pO2k], w=[K("gt_o")])
                        yield
                    yield "chain_done"
                    if not full:
                        return
                    P.op("dve", TT(g_["t0"][:], g_["o"][:], g_["o"][:], ALU.mult), r=[K("gt_o")], w=[K("gt_t0")])
                    P.op("dve", RED(m_["sso"][:, 0:8], v3(g_["t0"][:, :], 8)), r=[K("gt_t0")], w=[K("sm_sso")])
                    yield
                    P.op("act", ACT(m_["ro"][:, 0:8], m_["sso"][:, 0:8], AF.Ln, scale=1.0 / 64, bias=EPS), r=[K("sm_sso")], w=[K("sm_ro")])
                    P.op("act", ACT(m_["ro"][:, 0:8], m_["ro"][:, 0:8], AF.Exp, scale=-0.5), r=[K("sm_ro")], w=[K("sm_ro")])
                    yield
                    P.op("dve", TT(v3(g_["t0"][:, :], 8), v3(g_["o"][:, :], 8), b3(m_["ro"][:, 0:8], 128, 8, 64, 2), ALU.mult),
                         r=[K("gt_o"), K("sm_ro")], w=[K("gt_t0")])
                    P.op("dve", TT(v3(g_["t0"][:, :], 8), v3(g_["t0"][:, :], 8), b3(gnw[:, :], 128, 8, 64, 1), ALU.mult),
                         r=[K("gt_t0"), "gnw"], w=[K("gt_t0")])
                    P.op("dve", TT(g_["og"][:], g_["t0"][:], zts[bs][:], ALU.mult), r=[K("gt_t0"), K("zt")], w=[K("gt_og")])
                    yield
                    pt, pk = psetb(bs)
                    for kc in range(4):
                        P.op("pe", TR(pt[:, kc * 128:(kc + 1) * 128], g_["og"][:, kc * 128:(kc + 1) * 128], identb[:, :]), r=[K("gt_og"), "identb"], w=[pk])
                    yield
                    P.op("act", ACP(oT3[:, 0:4, tok0:tok0 + 128], v3(pt[:, 0:512], 4)), r=[pk], w=["oT"])

                def gdn_group(full, tokbase):
                    gens = [gdn_pre(i, full, i) for i in range(2)]
                    for _ in range(PRE_STAGGER):
                        next(gens[0])
                    live = list(gens)
                    while live:
                        for gq in list(live):
                            try:
                                next(gq)
                            except StopIteration:
                                live.remove(gq)

                def chains(full, tokbase):
                    for i in range(2):
                        for v_ in gdn_chain(i, full, tokbase + i * 128, i):
                            yield v_

                def other_gen(items):
                    import types
                    for fn in items:
                        r_ = fn()
                        if isinstance(r_, types.GeneratorType):
                            for _ in r_:
                                yield
                        yield

                def interleave(ga, gb):
                    live = [ga, gb]
                    while live:
                        for gq in list(live):
                            try:
                                for _ in range(CHAIN_PRIO if gq is ga else 1):
                                    next(gq)
                            except StopIteration:
                                live.remove(gq)

                def swa_kv(tcol, slot):
                    pt, pk = psd()
                    for kc in range(8):
                        P.op("pe", MM(pt[:, 0:256], hT3[:, kc, tcol:tcol + 128], win3[:, kc, 2576:2832],
                                      start=(kc == 0), stop=(kc == 7)), r=["hT", "win"], w=[pk])
                    P.op("act", ACP(sw["kv"][:, 0:256], pt[:, 0:256]), r=[pk], w=["sw_qraw"])
                    check("k_mm", sw["kv"][:, :])
                    P.op("act", ACT(sw["t0"][:, 0:128], sw["kv"][:, 0:128], AF.Square), r=["sw_qraw"], w=["sw_t0"])
                    P.op("dve", RED(swsm["ssk"][:, 0:2], v3(sw["t0"][:, 0:128], 2)), r=["sw_t0"], w=["swsm_ssk"])
                    P.op("act", ACT(swsm["rk"][:, 0:2], swsm["ssk"][:, 0:2], AF.Ln, scale=1.0 / 64, bias=EPS), r=["swsm_ssk"], w=["swsm_rk"])
                    P.op("act", ACT(swsm["rk"][:, 0:2], swsm["rk"][:, 0:2], AF.Exp, scale=-0.5), r=["swsm_rk"], w=["swsm_rk"])
                    P.op("dve", TT(v3(swkn[:, :], 2), v3(sw["kv"][:, 0:128], 2), b3(swsm["rk"][:, 0:2], 128, 2, 64, 2), ALU.mult),
                         r=["sw_qraw", "swsm_rk"], w=["swkn"])
                    P.op("dve", TT(v3(swkn[:, :], 2), v3(swkn[:, :], 2), b3(knw[:, :], 128, 2, 64, 1), ALU.mult), r=["swkn", "knw"], w=["swkn"])
                    check("k_norm", swkn[:, :])
                    P.op("act", ACP(vb1[slot][:, :].rearrange("p (h d) -> p h d", h=2)[:, :, 0:64], v3(sw["kv"][:, 128:256], 2)),
                         r=["sw_qraw"], w=["vb1%d" % slot])
                    check("k_v", vb1[slot][:, :])
                    p2, p2k = pssb()
                    for h in range(2):
                        P.op("pe", TR(p2[0:64, h * 128:(h + 1) * 128], swkn[:, h * 64:(h + 1) * 64], identb[:, :]), r=["swkn", "identb"], w=[p2k])
                    P.op("act", ACP(kT[slot][:], p2[0:64, 0:256]), r=[p2k], w=["kT%d" % slot])
                    check("k_T", kT[slot][:, :])

                def swa_block(tcol, tok0, first_block):
                    b = swa_blk[0]; swa_blk[0] += 1
                    cur, prv = b % 2, (b + 1) % 2
                    swa_kv(tcol, cur)
                    pq, pqk = psd()
                    for kc in range(8):
                        P.op("pe", MM(pq[:, :], hT3[:, kc, tcol:tcol + 128], win3[:, kc, 2064:2576],
                                      start=(kc == 0), stop=(kc == 7)), r=["hT", "win"], w=[pqk])
                    P.op("act", ACP(sw["qraw"][:], pq[:, :]), r=[pqk], w=["sw_qraw"])
                    P.op("act", ACT(sw["t0"][:], sw["qraw"][:], AF.Square), r=["sw_qraw"], w=["sw_t0"])
                    P.op("dve", RED(swsm["ssq"][:, 0:8], v3(sw["t0"][:, :], 8)), r=["sw_t0"], w=["swsm_ssq"])
                    P.op("act", ACT(swsm["rq"][:, 0:8], swsm["ssq"][:, 0:8], AF.Ln, scale=1.0 / 64, bias=EPS), r=["swsm_ssq"], w=["swsm_rq"])
                    P.op("act", ACT(swsm["rq"][:, 0:8], swsm["rq"][:, 0:8], AF.Exp, scale=-0.5), r=["swsm_rq"], w=["swsm_rq"])
                    P.op("dve", TT(v3(sw["qn"][:, :], 8), v3(sw["qraw"][:, :], 8), b3(swsm["rq"][:, 0:8], 128, 8, 64, 2), ALU.mult),
                         r=["sw_qraw", "swsm_rq"], w=["sw_qn"])
                    P.op("dve", TT(v3(sw["qn"][:, :], 8), v3(sw["qn"][:, :], 8), b3(qnw[:, :], 128, 8, 64, 1), ALU.mult), r=["sw_qn", "qnw"], w=["sw_qn"])
                    for half in range(2):
                        pt, pk = pssb()
                        for j in range(4):
                            h = half * 4 + j
                            P.op("pe", TR(pt[0:64, j * 128:(j + 1) * 128], sw["qn"][:, h * 64:(h + 1) * 64], identb[:, :]), r=["sw_qn", "identb"], w=[pk])
                        P.op("act", ACP(qT[:, half * 512:(half + 1) * 512], pt[0:64, 0:512]), r=[pk], w=["qT"])
                        yield
                    for kvh in range(2):
                        po, pok = psd()
                        pts = []
                        for kb, sl in ((0, prv), (1, cur)):
                            ps_, psk = pss()
                            P.op("pe", MM(ps_[:, :], kT[sl][:, kvh * 128:(kvh + 1) * 128], qT[:, kvh * 512:(kvh + 1) * 512]),
                                 r=["kT%d" % sl, "qT"], w=[psk])
                            stb = sw["st%d" % kb]; stk = "sw_st%d" % kb
                            a0 = kb * 1024 + kvh * 512
                            P.op("dve", TT(stb[:], ps_[:, :], abias[:, a0:a0 + 512], ALU.add), r=[psk, "abias"], w=[stk])
                            pi = kvh * 2 + kb
                            P.op("act", ACT(pT[pi][:], stb[:], AF.Exp), r=[stk], w=["pT%d" % pi])
                            if kb == 0 and first_block:
                                P.op("dve", TS(pT[pi][:], pT[pi][:], flag[:, 0:1], ALU.mult), r=["pT%d" % pi, "flag"], w=["pT%d" % pi])
                            pts.append((pi, sl))
                            yield
                        for g in range(4):
                            for i, (pi, sl) in enumerate(pts):
                                P.op("pe", MM(po[:, g * 65:(g + 1) * 65], pT[pi][:, g * 128:(g + 1) * 128], vb1[sl][:, kvh * 65:(kvh + 1) * 65],
                                              start=(i == 0), stop=(i == 1)), r=["pT%d" % pi, "vb1%d" % sl], w=[pok])
                        po3 = po[:, 0:260].rearrange("p (g d) -> p g d", g=4)
                        P.op("dve", TT(swsm["den"][:, 0:4].unsqueeze(2), po3[:, :, 64:65], esink[:, kvh * 4:(kvh + 1) * 4].unsqueeze(2), ALU.add),
                             r=[pok, "esink"], w=["swsm_den"])
                        P.op("dve", RCP(swsm["rden"][:, 0:4], swsm["den"][:, 0:4]), r=["swsm_den"], w=["swsm_rden"])
                        P.op("dve", TT(v3(sw["os"][:, kvh * 256:(kvh + 1) * 256], 4), po3[:, :, 0:64], b3(swsm["rden"][:, 0:4], 128, 4, 64, 2), ALU.mult),
                             r=[pok, "swsm_rden"], w=["sw_os"])
                        yield
                    pt, pk = pssb()
                    for kc in range(4):
                        P.op("pe", TR(pt[:, kc * 128:(kc + 1) * 128], sw["os"][:, kc * 128:(kc + 1) * 128], identb[:, :]), r=["sw_os", "identb"], w=[pk])
                    P.op("act", ACP(oT3[:, 4:8, tok0:tok0 + 128], v3(pt[:, 0:512], 4)), r=[pk], w=["oT"])

                def norm_items(xsrc, g):
                    return [lambda t=t: norm_to_hT(xsrc, g * TG + t * 128, t * 128) for t in range(2)]

                def proj_items(full, lastprev):
                    its = []
                    for cc in range(12):
                        if cc < 4 and not full:
                            if lastprev:
                                its.append(lambda cc=cc: proj_fm(cc, False))
                        else:
                            its.append(lambda cc=cc: proj_fm(cc, True))
                    return its

                for it in norm_items(xp, 0) + proj_items(False, False):
                    it()
                for g in range(NGRP):
                    last = (g == NGRP - 1)
                    gdn_group(False, 0)
                    others = []
                    if last:
                        others.append(lambda: swa_kv(128, 1))
                        others += norm_items(xo, 0)
                    else:
                        others += norm_items(xp, g + 1) + proj_items(False, g + 1 == NGRP - 1)
                    interleave(chains(False, 0), other_gen(others))
                P.op("dve", TS(S[:], S[:], flag[0:64, 0:1], ALU.mult), r=["S", "flag"], w=["S"])
                P.op("act", ACP(Sb[:], S[:]), r=["S"], w=["Sb"])
                P.op("dve", TS(halo[:], halo[:], flag[:, 0:1], ALU.mult), r=["halo", "flag"], w=["halo"])
                for it in proj_items(True, False):
                    it()

                for g in range(NGRP):
                    gdn_group(True, g * TG)
                    others = [lambda t=t: swa_block(t * 128, g * TG + t * 128, (g == 0 and t == 0)) for t in range(2)]
                    if g + 1 < NGRP:
                        others += norm_items(xo, g + 1) + proj_items(True, False)
                    interleave(chains(True, g * TG), other_gen(others))
                check("A_end", oT[:, :])
                P.barrier()
            sW.close()

            with ExitStack() as sBC:
                xacc = alloc(sBC, "xacc", [128, 16 * D])
                xacc3 = xacc[:, :].rearrange("p (t f) -> p t f", t=16)
                h2T = alloc(sBC, "h2T", [128, 8 * T_HALF], BF16)
                h2T3 = h2T[:, :].rearrange("p (k t) -> p k t", k=8)
                PSZ = [6, 6, 5, 5]; POFF = [0, 6, 12, 17]
                wg_v = wg_d.rearrange("(k p) n -> p k n", p=128)
                wu_v = wu_d.rearrange("(k p) n -> p k n", p=128)
                wd_v = wd_d.rearrange("(k p) n -> p k n", p=128)
                wgs = [alloc(sBC, "wg0", [128, 8 * 768], BF16)]
                wus = [alloc(sBC, "wu0", [128, 8 * 768], BF16)]

                def load_gu(pi):
                    sl = pi % 2
                    n_ = PSZ[pi] * 128; c0 = POFF[pi] * 128
                    g3 = wgs[sl][:, :].rearrange("p (k c) -> p k c", k=8)
                    u3 = wus[sl][:, :].rearrange("p (k c) -> p k c", k=8)
                    for kc in range(8):
                        P.dma("pool", DMA(g3[:, kc, 0:n_], wg_v[:, kc, c0:c0 + n_]), w=["wg%d" % sl])
                        P.dma("pool", DMA(u3[:, kc, 0:n_], wu_v[:, kc, c0:c0 + n_]), w=["wu%d" % sl])

                load_gu(0)
                sB = ExitStack()
                wst = [alloc(sB, "wst%d" % i, [128, D]) for i in range(2)]
                wob = alloc(sB, "wob", [128, 8 * D], BF16)
                wob3 = wob[:, :].rearrange("p (k c) -> p k c", k=8)
                xs2 = alloc(sB, "xs2", [128, D], BF16)
                junk2 = alloc(sB, "junk2", [128, D], BF16)
                st3 = alloc(sB, "st3", [128, 2])

                wout_v = wout_d.rearrange("(k p) n -> p k n", p=128)
                for kc in range(8):
                    sl = kc % 2
                    P.dma("sp", DMA(wst[sl][:], wout_v[:, kc, :]), w=["wst%d" % sl])
                    P.op("dve", TT(wob3[:, kc, :], wst[sl][:], gateB[:, 0:1024], ALU.mult), r=["wst%d" % sl, "gateB"], w=["wob"])
                for t in range(16):
                    P.dma("sp", DMA(xacc3[:, t, :], xo[t * 128:(t + 1) * 128, :]), w=["xacc%d" % t])
                    for half in range(2):
                        pt, pk = psd()
                        for kc in range(8):
                            P.op("pe", MM(pt[:, :], oT3[:, kc, t * 128:(t + 1) * 128], wob3[:, kc, half * 512:(half + 1) * 512],
                                          start=(kc == 0), stop=(kc == 7)), r=["oT", "wob"], w=[pk])
                        P.op("dve", TT(xacc3[:, t, half * 512:(half + 1) * 512], xacc3[:, t, half * 512:(half + 1) * 512], pt[:, :], ALU.add),
                             r=[pk, "xacc%d" % t], w=["xacc%d" % t])
                    P.op("dve", MEMSET(st3[:, 0:1], 0.0), w=["st3"])
                    P.op("act", ACT(junk2[:], xacc3[:, t, :], AF.Square, accum_out=st3[:, 0:1]), r=["xacc%d" % t, "st3"], w=["junk2", "st3"])
                    P.op("act", ACT(st3[:, 1:2], st3[:, 0:1], AF.Ln, scale=1.0 / D, bias=EPS), r=["st3"], w=["st3"])
                    P.op("act", ACT(st3[:, 1:2], st3[:, 1:2], AF.Exp, scale=-0.5), r=["st3"], w=["st3"])
                    P.op("dve", TS(xs2[:], xacc3[:, t, :], st3[:, 1:2], ALU.mult), r=["xacc%d" % t, "st3"], w=["xs2"])
                    for half in range(2):
                        pt, pk = pssb()
                        for j in range(4):
                            kc = half * 4 + j
                            P.op("pe", TR(pt[:, j * 128:(j + 1) * 128], xs2[:, kc * 128:(kc + 1) * 128], identb[:, :]), r=["xs2", "identb"], w=[pk])
                        for j in range(4):
                            kc = half * 4 + j
                            P.op("act", ACT(h2T3[:, kc, t * 128:(t + 1) * 128], pt[:, j * 128:(j + 1) * 128], AF.Identity,
                                            scale=a2[:, kc:kc + 1], bias=s2c[:, kc:kc + 1]), r=[pk, "a2", "colv"], w=["h2T"])
                P.barrier()
                sB.close()
                sC = ExitStack()
                wgs.append(alloc(sC, "wg1", [128, 8 * 768], BF16))
                wus.append(alloc(sC, "wu1", [128, 8 * 768], BF16))
                wstc = [alloc(sC, "wstc%d" % i, [128, D]) for i in range(2)]
                wds = [oT[:, i * 6 * D:(i + 1) * 6 * D].rearrange("p (k c) -> p k c", k=6) for i in range(2)]
                actT = alloc(sC, "actT", [128, 6 * 512], BF16)
                actT3 = actT[:, :].rearrange("p (k t) -> p k t", k=6)
                sg = [alloc(sC, "sg%d" % i, [128, 512]) for i in range(2)]

                def load_d(pi):
                    sl = pi % 2
                    for fc in range(PSZ[pi]):
                        ws = wstc[fc % 2]; wk = "wstc%d" % (fc % 2)
                        P.dma("sp", DMA(ws[:], wd_v[:, POFF[pi] + fc, :]), w=[wk])
                        P.op("pool", TT(wds[sl][:, fc, :], ws[:], gateB[:, 1024:2048], ALU.mult), r=[wk, "gateB"], w=["wd%d" % sl])

                def ffn_pass(pi):
                    sl = pi % 2
                    nfc = PSZ[pi]
                    g3 = wgs[sl][:, :].rearrange("p (k c) -> p k c", k=8)
                    u3 = wus[sl][:, :].rearrange("p (k c) -> p k c", k=8)
                    for tg in range(4):
                        t0 = tg * 512
                        for fc in range(nfc):
                            pg_, pgk = psd()
                            for kc in range(8):
                                P.op("pe", MM(pg_[:, :], g3[:, kc, fc * 128:(fc + 1) * 128], h2T3[:, kc, t0:t0 + 512],
                                              start=(kc == 0), stop=(kc == 7)), r=["wg%d" % sl, "h2T"], w=[pgk])
                            pu_, puk = psd()
                            for kc in range(8):
                                P.op("pe", MM(pu_[:, :], u3[:, kc, fc * 128:(fc + 1) * 128], h2T3[:, kc, t0:t0 + 512],
                                              start=(kc == 0), stop=(kc == 7)), r=["wu%d" % sl, "h2T"], w=[puk])
                            s_ = sg[fc % 2]; sk_ = "sg%d" % (fc % 2)
                            P.op("act", ACT(s_[:], pg_[:, :], AF.Silu), r=[pgk], w=[sk_])
                            P.op("dve", TT(actT3[:, fc, :], s_[:], pu_[:, :], ALU.mult), r=[sk_, puk], w=["actT"])
                        for tt in range(4):
                            t = tg * 4 + tt
                            for half in range(2):
                                pt, pk = pss4()
                                for fc in range(nfc):
                                    P.op("pe", MM(pt[:, :], actT3[:, fc, tt * 128:(tt + 1) * 128], wds[sl][:, fc, half * 512:(half + 1) * 512],
                                                  start=(fc == 0), stop=(fc == nfc - 1)), r=["actT", "wd%d" % sl], w=[pk])
                                P.op("dve", TT(xacc3[:, t, half * 512:(half + 1) * 512], xacc3[:, t, half * 512:(half + 1) * 512], pt[:, :], ALU.add),
                                     r=[pk, "xacc%d" % t], w=["xacc%d" % t])

                load_d(0)
                for pi in range(4):
                    if pi + 1 < 4:
                        load_gu(pi + 1)
                        load_d(pi + 1)
                    ffn_pass(pi)
                for t in range(16):
                    P.dma("sp", DMA(y[t * 128:(t + 1) * 128, :], xacc3[:, t, :]), r=["xacc%d" % t], w=["y%d" % t])

        except _Stop:
            pass
        P.emit(nc)
    return nc


_NC_CACHE = {}


def _host_consts():
    ident = np.eye(128, dtype=np.float32)
    i = np.arange(64)
    utri = (i[:, None] <= i[None, :]).astype(np.float32)
    ones = np.ones((64, 64), np.float32)
    maskL = np.where(i[:, None] >= i[None, :], BIG, -BIG).astype(np.float32)
    maskU = np.where(i[None, :] >= i[:, None], BIG, -BIG).astype(np.float32)
    strictL = (i[:, None] > i[None, :]).astype(np.float32)
    def m_off(b):
        p = i[:, None]; f = i[None, :]
        return ((p // (2 * b) == f // (2 * b)) & (p % (2 * b) >= b) & (f % (2 * b) < b)).astype(np.float32)
    mts = [m_off(1).T.copy()] + [m_off(b) for b in (2, 4, 8, 16, 32)]
    st2 = lambda m: np.concatenate([m, m], axis=0)
    bd = lambda m: np.block([[m, np.zeros_like(m)], [np.zeros_like(m), m]])
    c64 = np.concatenate([bd(utri), bd(ones), st2(np.eye(64, dtype=np.float32)), st2(maskL), st2(maskU), st2(strictL),
                          st2(m_off(1))] + [st2(m) for m in mts], axis=1).astype(np.float32)
    key = np.arange(128)[:, None]
    q = np.arange(128)[None, :]
    slopes = 2.0 ** (-8.0 * (np.arange(8, dtype=np.float32) + 1.0) / 8)
    ab = np.zeros((128, 2, 8, 128), np.float32)
    for h in range(8):
        d0 = (q + 128 - key).astype(np.float32)
        ab[:, 0, h, :] = np.where(key > q, -slopes[h] * d0, -BIG)
        d1 = (q - key).astype(np.float32)
        ab[:, 1, h, :] = np.where(key <= q, -slopes[h] * d1, -BIG)
    return ident, c64, ab.reshape(128, 2048)


def make_in_maps(x, c, w_ada, b_ada, norm1_w, w_in, conv_w, a_log, dt_bias, gdn_norm_w,
                 q_norm_w, k_norm_w, sinks, w_out, norm2_w, w_gate, w_up, w_down):
    f = lambda a: np.ascontiguousarray(np.asarray(a, dtype=np.float32))
    ident, c64, abias = _host_consts()
    fm = lambda v: f(np.asarray(v).reshape(8, 128).T)
    rep = lambda v, p: f(np.broadcast_to(np.asarray(v).reshape(1, -1), (p, np.asarray(v).size)))
    convw = f(np.asarray(conv_w)[0, :, 0, :].reshape(4, 12, 128).transpose(2, 1, 0).reshape(128, 48))
    shared = {
        "w_ada": f(w_ada[0]), "b_ada_b": rep(b_ada[0], 128), "n1w": fm(norm1_w[0]), "n2w": fm(norm2_w[0]),
        "w_in": f(w_in[0]), "convw": convw, "alog_b": rep(a_log[0], 128), "dtb_b": rep(dt_bias[0], 128),
        "gnw_b": rep(gdn_norm_w[0], 128), "qnw_b": rep(q_norm_w[0], 128), "knw_b": rep(k_norm_w[0], 128),
        "sinks_b": rep(sinks[0], 128), "w_out": f(w_out[0]), "w_gate": f(w_gate[0]), "w_up": f(w_up[0]),
        "w_down": f(w_down[0]), "ident": ident, "c64": c64, "abias": abias,
    }
    x = np.asarray(x); c = np.asarray(c)
    maps = []
    for core in range(8):
        b, half = core // 2, core % 2
        m = dict(shared)
        m["xo"] = f(x[b, half * T_HALF:(half + 1) * T_HALF])
        m["xp"] = f(x[b, 0:T_HALF])
        m["flag"] = np.full((128, 1), float(half), np.float32)
        m["cfm"] = fm(c[b])
        maps.append(m)
    return maps


def kernel(**inputs):
    if "nc" not in _NC_CACHE:
        _NC_CACHE["nc"] = build_program()
    nc = _NC_CACHE["nc"]
    maps = make_in_maps(**inputs)
    res = run_bass_kernel_spmd(nc, maps, core_ids=list(range(8)))
    out = np.empty((4, 2 * T_HALF, D), np.float32)
    for core in range(8):
        b, half = core // 2, core % 2
        out[b, half * T_HALF:(half + 1) * T_HALF] = res.results[core]["y"]
    return out
```

```python
import numpy as np
import concourse.bass as bass
import concourse.mybir as mybir
from concourse.bass_utils import run_bass_kernel_spmd
from contextlib import ExitStack

F32 = mybir.dt.float32
BF16 = mybir.dt.bfloat16
ALU = mybir.AluOpType
AF = mybir.ActivationFunctionType
AX = mybir.AxisListType

D = 1024
T_HALF = 2048
TG = 256
NGRP = T_HALF // TG
CH = 64
PROJ = 2832
DFF = 2816
EPS = 1e-6
CHAIN_PRIO = 2
PRE_STAGGER = 2
BIG = 30000.0

ENGS = ("pe", "act", "dve", "pool", "sp")
NDMASEM = 6
ATTACH_WAITS = True


import sys as _sys


def _caller_line():
    f = _sys._getframe(2)
    lines = []
    while f is not None and len(lines) < 3:
        lines.append(f.f_lineno)
        f = f.f_back
    return lines


class Prog:
    def __init__(self):
        self.q = {e: [] for e in ENGS}
        self.last_w = {}
        self.readers = {}
        self.dma_uses = {}
        self.dma_rr = {e: 0 for e in ENGS}
        self.ncomp = {e: 0 for e in ENGS}

    def _deps(self, r, w):
        deps = set()
        for k in r:
            if k in self.last_w:
                deps.add(self.last_w[k])
        for k in w:
            if k in self.last_w:
                deps.add(self.last_w[k])
            for x in self.readers.get(k, ()):
                deps.add(x)
        return deps

    def _record(self, node, r, w):
        for k in r:
            self.readers.setdefault(k, []).append(node)
        for k in w:
            self.last_w[k] = node
            self.readers[k] = []

    def _waits(self, deps, eng, skip_same_pe=True):
        waits = {}
        for d in deps:
            if d[0] == "c":
                _, de, di = d
                if skip_same_pe and de == eng and eng == "pe":
                    continue
                key = ("c", de)
                waits[key] = max(waits.get(key, 0), di + 1)
            else:
                _, de, slot, use = d
                key = ("d", de, slot)
                waits[key] = max(waits.get(key, 0), 16 * (use + 1))
        return waits

    def op(self, eng, fn, r=(), w=()):
        for k in r:
            if k.startswith("ps") and eng != "pe":
                for x in self.readers.get(k, ()):
                    assert x[1] == eng or x[1] == "pe", ("PSUM bank read by two engines", k, eng, x, _caller_line())
        deps = self._deps(r, w)
        idx = self.ncomp[eng]
        self.ncomp[eng] += 1
        node = ("c", eng, idx)
        self.q[eng].append(dict(fn=fn, waits=self._waits(deps, eng), dma=None, line=_caller_line()))
        self._record(node, r, w)
        return node

    def dma(self, eng, fn, r=(), w=()):
        deps = self._deps(r, w)
        slot = self.dma_rr[eng] % NDMASEM
        self.dma_rr[eng] += 1
        use = self.dma_uses.get((eng, slot), 0)
        self.dma_uses[(eng, slot)] = use + 1
        node = ("d", eng, slot, use)
        waits = self._waits(deps, eng, skip_same_pe=False)
        if use > 0:
            key = ("d", eng, slot)
            waits[key] = max(waits.get(key, 0), 16 * use)
        self.q[eng].append(dict(fn=fn, waits=waits, dma=(slot, use), line=_caller_line()))
        self._record(node, r, w)
        return node

    def barrier(self):
        waits = {}
        for e in ENGS:
            if self.ncomp[e] > 0:
                waits[("c", e)] = self.ncomp[e]
        for (e, s), u in self.dma_uses.items():
            if u > 0:
                waits[("d", e, s)] = 16 * u
        for e in ENGS:
            w = {k: v for k, v in waits.items() if not (k[0] == "c" and k[1] == e)}
            self.q[e].append(dict(fn=None, waits=w, dma=None))

    def emit(self, nc, final_wait_eng="sp"):
        with ExitStack() as es:
            csem = {e: es.enter_context(nc.semaphore("c_" + e)) for e in ENGS}
            dsem = {}
            for e in ENGS:
                if self.dma_rr[e] > 0:
                    for s in range(NDMASEM):
                        dsem[(e, s)] = es.enter_context(nc.semaphore("d_%s%d" % (e, s)))
            block = es.enter_context(nc.Block())
            prog = self

            def sem_of(key):
                if key[0] == "c":
                    return csem[key[1]]
                return dsem[(key[1], key[2])]

            def run(engname, e):
                seen = {}
                for item in prog.q[engname]:
                    pend = []
                    for key, val in item["waits"].items():
                        if seen.get(key, 0) >= val:
                            continue
                        pend.append((key, val))
                        seen[key] = val
                    attach = None
                    if ATTACH_WAITS and pend and item["fn"] is not None and item["dma"] is None:
                        attach = pend.pop()
                    for key, val in pend:
                        e.wait_ge(sem_of(key), val)
                    if item["fn"] is None:
                        continue
                    try:
                        ins = item["fn"](e)
                    except Exception:
                        print("EMIT FAILED for op issued at line", item.get("line"), "engine", engname)
                        raise
                    if attach is not None:
                        ins._wait_ge(sem_of(attach[0]), attach[1])
                    if item["dma"] is not None:
                        slot, use = item["dma"]
                        ins.then_inc(dsem[(engname, slot)], 16)
                    else:
                        ins.then_inc(csem[engname], 1)
                if engname == final_wait_eng:
                    for en in ENGS:
                        n = prog.ncomp[en]
                        if n > 0 and en != engname:
                            e.wait_ge(csem[en], n)
                    for (en, s), sem in dsem.items():
                        u = prog.dma_uses.get((en, s), 0)
                        if u > 0:
                            e.wait_ge(sem, 16 * u)

            block.tensor(lambda e: run("pe", e))
            block.scalar(lambda e: run("act", e))
            block.vector(lambda e: run("dve", e))
            block.gpsimd(lambda e: run("pool", e))
            block.sync(lambda e: run("sp", e))


def MM(out, lhsT, rhs, start=True, stop=True):
    return lambda e: e.matmul(out, lhsT=lhsT, rhs=rhs, start=start, stop=stop)


def TR(out, in_, idn):
    return lambda e: e.transpose(out, in_, idn)


def ACT(out, in_, func, **kw):
    return lambda e: e.activation(out=out, in_=in_, func=func, **kw)


def TT(out, in0, in1, op):
    return lambda e: e.tensor_tensor(out=out, in0=in0, in1=in1, op=op)


def TS(out, in0, s1, op0, s2=None, op1=None):
    if op1 is None:
        return lambda e: e.tensor_scalar(out=out, in0=in0, scalar1=s1, scalar2=None, op0=op0)
    return lambda e: e.tensor_scalar(out=out, in0=in0, scalar1=s1, scalar2=s2, op0=op0, op1=op1)


def STT(out, in0, scalar, in1, op0, op1):
    return lambda e: e.scalar_tensor_tensor(out=out, in0=in0, scalar=scalar, in1=in1, op0=op0, op1=op1)


def CP(out, in_):
    return lambda e: e.tensor_copy(out, in_)


def ACP(out, in_):
    return lambda e: e.activation(out=out, in_=in_, func=AF.Copy)


def RED(out, in_, op=ALU.add):
    return lambda e: e.tensor_reduce(out=out, in_=in_, axis=AX.X, op=op)


def RCP(out, in_):
    return lambda e: e.reciprocal(out, in_)


def MEMSET(ap, v):
    return lambda e: e.memset(ap, v)


def DMA(out, in_):
    return lambda e: e.dma_start(out=out, in_=in_)


def b3(ap, P, h, d, axis):
    return ap.unsqueeze(axis).to_broadcast([P, h, d])


def v3(ap, h):
    return ap.rearrange("p (h d) -> p h d", h=h)


class _Stop(Exception):
    pass


def build_program(debug=None, stop=None):
    nc = bass.Bass("TRN2", target_bir_lowering=False)
    dram_in = lambda name, shape: nc.dram_tensor(name, list(shape), F32, kind="ExternalInput").ap()
    xo = dram_in("xo", [T_HALF, D])
    xp = dram_in("xp", [T_HALF, D])
    flag_d = dram_in("flag", [128, 1])
    cfm_d = dram_in("cfm", [128, 8])
    wada_d = dram_in("w_ada", [D, 6 * D])
    bada_d = dram_in("b_ada_b", [128, 6 * D])
    n1w_d = dram_in("n1w", [128, 8])
    n2w_d = dram_in("n2w", [128, 8])
    win_d = dram_in("w_in", [D, PROJ])
    convw_d = dram_in("convw", [128, 48])
    alog_d = dram_in("alog_b", [128, 8])
    dtb_d = dram_in("dtb_b", [128, 8])
    gnw_d = dram_in("gnw_b", [128, 64])
    qnw_d = dram_in("qnw_b", [128, 64])
    knw_d = dram_in("knw_b", [128, 64])
    sinks_d = dram_in("sinks_b", [128, 8])
    wout_d = dram_in("w_out", [D, D])
    wg_d = dram_in("w_gate", [D, DFF])
    wu_d = dram_in("w_up", [D, DFF])
    wd_d = dram_in("w_down", [DFF, D])
    ident_d = dram_in("ident", [128, 128])
    c64_d = dram_in("c64", [128, 256 + 11 * 64])
    abias_d = dram_in("abias", [128, 2 * 8 * 128])
    y = nc.dram_tensor("y", [T_HALF, D], F32, kind="ExternalOutput").ap()
    dbg = None
    dbgb = None
    if debug is not None:
        dbg = nc.dram_tensor("dbg", [128, debug], F32, kind="ExternalOutput").ap()
        dbgb = nc.dram_tensor("dbgb", [128, debug], BF16, kind="ExternalOutput").ap()

    P = Prog()
    top = ExitStack()

    def check(tag, ap=None):
        if stop != tag:
            return
        if ap is not None and dbg is not None:
            p, n = ap.shape[0], ap.shape[1]
            dst = dbgb if ap.dtype == BF16 else dbg
            P.dma("sp", DMA(dst[0:p, 0:n], ap), r=list(P.last_w.keys()), w=["dbg"])
        raise _Stop()

    ARENA_COLS = 53200
    arena_state = {"cur": 0, "ar": None, "marks": {}}

    def alloc(es, name, shape, dt=F32):
        st = arena_state
        if id(es) not in st["marks"]:
            st["marks"][id(es)] = st["cur"]
            mark = st["cur"]
            es.callback(lambda: st.__setitem__("cur", mark))
        p, n = shape
        nbytes = n * (4 if dt == F32 else 2)
        ncols = (nbytes + 3) // 4
        ncols = (ncols + 7) // 8 * 8
        a0 = st["cur"]
        st["cur"] += ncols
        assert st["cur"] <= ARENA_COLS, ("SBUF arena overflow", name, st["cur"])
        ap = st["ar"][0:p, a0:a0 + ncols]
        if dt != F32:
            ap = ap.bitcast(dt)
        return ap[:, 0:n]

    with top:
        arena_state["ar"] = top.enter_context(nc.sbuf_tensor("arena", [128, ARENA_COLS], F32))
        psb = [top.enter_context(nc.psum_tensor("ps%d" % i, [128, 512], F32)) for i in range(8)]
        ps_rr = {"d": 0, "s": 0}

        def psd():
            i = ps_rr["d"] % 2
            ps_rr["d"] += 1
            return psb[i], "ps%d" % i

        def pss():
            i = 6 + ps_rr["s"] % 2
            ps_rr["s"] += 1
            return psb[i], "ps%d" % i

        ps4_rr = [0]

        def pss4():
            i = 2 + ps4_rr[0] % 6
            ps4_rr[0] += 1
            return psb[i], "ps%d" % i

        def pssb():
            t, k = pss()
            return t[:, :].bitcast(BF16), k

        ps_set_rr = [0, 0]

        def pset(bs):
            i = 2 + 2 * bs + ps_set_rr[bs] % 2
            ps_set_rr[bs] += 1
            return psb[i], "ps%d" % i

        def psetb(bs):
            t, k = pset(bs)
            return t[:, :].bitcast(BF16), k

        def pfix(i):
            return psb[i], "ps%d" % i

        def pfixb(i):
            return psb[i][:, :].bitcast(BF16), "ps%d" % i

        ident = alloc(top, "ident", [128, 128])
        c64 = alloc(top, "c64", [128, 960])
        utri = c64[:, 0:128]; ones_bd = c64[:, 128:256]
        I2, maskL, maskU, strictL, mA1 = [c64[:, 256 + i * 64:256 + (i + 1) * 64] for i in range(5)]
        mTs = [c64[:, 256 + (5 + i) * 64:256 + (6 + i) * 64] for i in range(6)]
        flag = alloc(top, "flag", [128, 1])
        colv = alloc(top, "colv", [128, 32])
        a1 = alloc(top, "a1", [128, 8]); a2 = alloc(top, "a2", [128, 8])
        n1w = alloc(top, "n1w", [128, 8]); n2w = alloc(top, "n2w", [128, 8])
        gateB = alloc(top, "gateB", [128, 2048])
        identb = alloc(top, "identb", [128, 128], BF16)
        oT = alloc(top, "oT", [128, 8 * T_HALF], BF16)
        oT3 = oT[:, :].rearrange("p (k t) -> p k t", k=8)

        for dst, src, key in ((ident, ident_d, "ident"), (c64, c64_d, "c64"),
                              (flag, flag_d, "flag"), (n1w, n1w_d, "n1w"), (n2w, n2w_d, "n2w")):
            P.dma("sp", DMA(dst[:], src), w=[key])
        P.op("dve", CP(identb[:], ident[:]), r=["ident"], w=["identb"])

        try:
            sW = ExitStack()
            win = alloc(sW, "win", [128, 8 * PROJ], BF16)
            win3 = win[:, :].rearrange("p (k c) -> p k c", k=8)
            win_v = win_d.rearrange("(k p) n -> p k n", p=128)
            with ExitStack() as s0:
                cfm = alloc(s0, "cfm", [128, 8])
                cactB = alloc(s0, "cactB", [128, 8 * 128], BF16)
                NWA = 3
                wa = [alloc(s0, "wa%d" % i, [128, 8 * 512], BF16) for i in range(NWA)]
                ba = [alloc(s0, "ba%d" % i, [128, 512]) for i in range(NWA)]
                modc = alloc(s0, "modc", [128, 512])
                dtmp = alloc(s0, "dtmp", [128, 512])
                P.dma("sp", DMA(cfm[:], cfm_d), w=["cfm"])
                P.op("act", ACT(cfm[:], cfm[:], AF.Silu), r=["cfm"], w=["cfm"])
                P.op("dve", CP(v3(cactB[:, :], 8), b3(cfm[:, :], 128, 8, 128, 2)), r=["cfm"], w=["cactB"])
                wada_v = wada_d.rearrange("(k p) n -> p k n", p=128)
                for ci in range(12):
                    sl = ci % NWA
                    vec, half = ci // 2, ci % 2
                    P.dma("pool", DMA(v3(wa[sl][:, :], 8), wada_v[:, :, ci * 512:(ci + 1) * 512]), w=["wa%d" % sl])
                    if ci in (2, 5):
                        for kc in range((ci // 3) * 4, (ci // 3) * 4 + 4):
                            P.dma("pool", DMA(win3[:, kc, :], win_v[:, kc, :]), w=["win"])
                    P.dma("sp", DMA(ba[sl][:], bada_d[:, ci * 512:(ci + 1) * 512]), w=["ba%d" % sl])
                    pt, pk = psd()
                    for kc in range(8):
                        P.op("pe", MM(pt[:, :], cactB[:, kc * 128:(kc + 1) * 128], wa[sl][:, kc * 512:(kc + 1) * 512],
                                      start=(kc == 0), stop=(kc == 7)), r=["cactB", "wa%d" % sl], w=[pk])
                    if vec in (2, 5):
                        g0 = (0 if vec == 2 else 1024) + half * 512
                        P.op("dve", TT(gateB[:, g0:g0 + 512], pt[:, :], ba[sl][:], ALU.add), r=[pk, "ba%d" % sl], w=["gateB"])
                    else:
                        vi = {0: 0, 1: 1, 3: 2, 4: 3}[vec]
                        P.op("dve", TT(modc[:], pt[:, :], ba[sl][:], ALU.add), r=[pk, "ba%d" % sl], w=["modc"])
                        P.op("dve", TT(v3(dtmp[:, :], 4), v3(modc[:, :], 4), b3(ident[:, :], 128, 4, 128, 1), ALU.mult),
                             r=["modc", "ident"], w=["dtmp"])
                        c0 = vi * 8 + half * 4
                        P.op("dve", RED(colv[:, c0:c0 + 4], v3(dtmp[:, :], 4)), r=["dtmp"], w=["colv"])
                P.op("dve", STT(a1[:], colv[:, 8:16], 1.0, n1w[:], ALU.add, ALU.mult), r=["colv", "n1w"], w=["a1"])
                P.op("dve", STT(a2[:], colv[:, 24:32], 1.0, n2w[:], ALU.add, ALU.mult), r=["colv", "n2w"], w=["a2"])
                P.barrier()
                check("p0_colv", colv[:, :])
                check("p0_gate", gateB[:, :])
            s1c = colv[:, 0:8]
            s2c = colv[:, 16:24]

            with ExitStack() as sA:
                abias = alloc(sA, "abias", [128, 2048])
                convw = alloc(sA, "convw", [128, 48])
                negA = alloc(sA, "negA", [128, 8]); dtb = alloc(sA, "dtb", [128, 8])
                gnw = alloc(sA, "gnw", [128, 64])
                qnw = alloc(sA, "qnw", [128, 64]); knw = alloc(sA, "knw", [128, 64])
                esink = alloc(sA, "esink", [128, 8])
                for dst, src, key in ((abias, abias_d, "abias"), (convw, convw_d, "convw"), (negA, alog_d, "negA"),
                                      (dtb, dtb_d, "dtb"), (gnw, gnw_d, "gnw"), (qnw, qnw_d, "qnw"), (knw, knw_d, "knw"),
                                      (esink, sinks_d, "esink")):
                    P.dma("sp", DMA(dst[:], src), w=[key])
                P.op("act", ACT(negA[:], negA[:], AF.Exp), r=["negA"], w=["negA"])
                P.op("dve", TS(negA[:], negA[:], -1.0, ALU.mult), r=["negA"], w=["negA"])
                P.op("act", ACT(esink[:], esink[:], AF.Exp), r=["esink"], w=["esink"])
                P.op("dve", TS(qnw[:], qnw[:], 0.125, ALU.mult), r=["qnw"], w=["qnw"])
                xt = alloc(sA, "xt", [128, D])
                st2 = alloc(sA, "st2", [128, 2])
                hT = alloc(sA, "hT", [128, 8 * TG], BF16)
                hT3 = hT[:, :].rearrange("p (k t) -> p k t", k=8)
                halo = alloc(sA, "halo", [128, 36])
                praw = [alloc(sA, "praw%d" % i, [128, TG + 3]) for i in range(2)]
                cacc = alloc(sA, "cacc", [128, TG])
                cs = alloc(sA, "cs", [128, 12 * TG], BF16)
                NSET = 2
                tqs = [[alloc(sA, "tqkv%d_%d" % (i, b_), [128, 512], BF16) for i in range(3)] for b_ in range(NSET)]
                Sb = alloc(sA, "Sb", [64, 512], BF16)
                zts = [alloc(sA, "zt%d" % b_, [128, 512], BF16) for b_ in range(NSET)]
                S = alloc(sA, "S", [64, 512])
                smn = ("gab", "g", "e1", "beta", "nbeta", "G16", "eG", "eGL", "eGLB", "dG", "eGLmG", "beG", "ssk", "rk", "ssq", "rq", "sso", "ro")
                smw = {"gab": 16, "G16": 24}
                sms = [{n: alloc(sA, "sm%d_%s" % (b_, n), [128, smw.get(n, 8)]) for n in smn} for b_ in range(NSET)]
                gts = []
                for b_ in range(NSET):
                    d_ = {n: alloc(sA, "gt%d_%s" % (b_, n), [128, 512]) for n in ("t0", "t1", "o")}
                    d_.update({n: alloc(sA, "gt%d_%s" % (b_, n), [128, 512], BF16) for n in
                               ("D", "DT", "nbs", "bv", "kn", "qn", "P0", "P1", "Q0", "Q1", "W0", "W1", "T0", "T1",
                                "QKmT", "kd2", "r", "vn", "og")})
                    d_.update({n: alloc(sA, "gt%d_%s" % (b_, n), [64, 1024], BF16) for n in ("knT", "qnT")})
                    gts.append(d_)
                sw = {n: alloc(sA, "sw_" + n, [128, 512]) for n in ("t0", "qn", "st0", "st1", "os", "qraw")}
                sw["kv"] = sw["qraw"]
                sw["qn"] = sw["qn"][:, :].bitcast(BF16)[:, 0:512]
                sw["os"] = sw["os"][:, :].bitcast(BF16)[:, 0:512]
                junk = sw["t0"][:, :].bitcast(BF16)
                swkn = alloc(sA, "swkn", [128, 128], BF16)
                swsm = {n: alloc(sA, "swsm_" + n, [128, 8]) for n in ("ssq", "rq", "ssk", "rk", "den", "rden")}
                qT = alloc(sA, "qT", [64, 1024], BF16)
                kT = [alloc(sA, "kT%d" % i, [64, 256], BF16) for i in range(2)]
                vb1 = [alloc(sA, "vb1%d" % i, [128, 130], BF16) for i in range(2)]
                pT = [alloc(sA, "pT%d" % i, [128, 512], BF16) for i in range(4)]

                P.op("dve", MEMSET(halo[:], 0.0), w=["halo"])
                P.op("dve", MEMSET(S[:], 0.0), w=["S"])
                P.op("dve", MEMSET(Sb[:], 0.0), w=["Sb"])
                for i in range(2):
                    P.op("dve", MEMSET(vb1[i][:], 1.0), w=["vb1%d" % i])
                    P.op("dve", MEMSET(kT[i][:], 0.0), w=["kT%d" % i])

                id64 = ident[0:64, 0:64]
                idb64 = identb[0:64, 0:64]
                swa_blk = [0]

                def norm_to_hT(xsrc, row0, tcol):
                    P.dma("sp", DMA(xt[:], xsrc[row0:row0 + 128, :]), w=["xt"])
                    P.op("dve", MEMSET(st2[:, 0:1], 0.0), w=["st2"])
                    P.op("act", ACT(junk[:], xt[:], AF.Square, accum_out=st2[:, 0:1]), r=["xt", "st2"], w=["sw_t0", "st2"])
                    P.op("act", ACT(st2[:, 1:2], st2[:, 0:1], AF.Ln, scale=1.0 / D, bias=EPS), r=["st2"], w=["st2"])
                    P.op("act", ACT(st2[:, 1:2], st2[:, 1:2], AF.Exp, scale=-0.5), r=["st2"], w=["st2"])
                    P.op("dve", TS(junk[:], xt[:], st2[:, 1:2], ALU.mult), r=["xt", "st2", "sw_t0"], w=["sw_t0"])
                    for half in range(2):
                        pt, pk = pssb()
                        for j in range(4):
                            kc = half * 4 + j
                            P.op("pe", TR(pt[:, j * 128:(j + 1) * 128], junk[:, kc * 128:(kc + 1) * 128], identb[:, :]),
                                 r=["sw_t0", "identb"], w=[pk])
                        for j in range(4):
                            kc = half * 4 + j
                            P.op("act", ACT(hT3[:, kc, tcol:tcol + 128], pt[:, j * 128:(j + 1) * 128], AF.Identity,
                                            scale=a1[:, kc:kc + 1], bias=s1c[:, kc:kc + 1]),
                                 r=[pk, "a1", "colv"], w=["hT"])

                def proj_fm(cc, do_conv):
                    pr = praw[cc % 2]; prk = "praw%d" % (cc % 2)
                    pt, pk = psd()
                    for kc in range(8):
                        P.op("pe", MM(pt[:, 0:TG], win3[:, kc, cc * 128:(cc + 1) * 128], hT3[:, kc, :],
                                      start=(kc == 0), stop=(kc == 7)), r=["win", "hT"], w=[pk])
                    P.op("dve", CP(pr[:, 0:3], halo[:, cc * 3:cc * 3 + 3]), r=["halo"], w=[prk])
                    P.op("act", ACP(pr[:, 3:3 + TG], pt[:, 0:TG]), r=[pk], w=[prk])
                    P.op("dve", CP(halo[:, cc * 3:cc * 3 + 3], pr[:, TG:TG + 3]), r=[prk], w=["halo"])
                    if not do_conv:
                        return
                    P.op("dve", TS(cacc[:], pr[:, 0:TG], convw[:, cc * 4:cc * 4 + 1], ALU.mult), r=[prk, "convw"], w=["cacc"])
                    for j in range(1, 4):
                        P.op("dve", STT(cacc[:], pr[:, j:j + TG], convw[:, cc * 4 + j:cc * 4 + j + 1], cacc[:], ALU.mult, ALU.add),
                             r=[prk, "convw", "cacc"], w=["cacc"])
                    P.op("act", ACT(cs[:, cc * TG:(cc + 1) * TG], cacc[:], AF.Silu), r=["cacc"], w=["cs"])

                RA = slice(0, 64); RB = slice(64, 128)
                idbA = identb[0:64, 0:64]; idbB = identb[64:128, 64:128]

                def tok_major(typ, tcol, bs):
                    pt, pk = psetb(bs)
                    for j in range(4):
                        cc = typ * 4 + j
                        P.op("pe", TR(pt[:, j * 128:(j + 1) * 128], cs[:, cc * TG + tcol:cc * TG + tcol + 128], identb[:, :]),
                             r=["cs", "identb"], w=[pk])
                    P.op("act", ACP(tqs[bs][typ][:], pt[:, 0:512]), r=[pk], w=["tqkv%d_%d" % (typ, bs)])

                def l2n(src, srck, ss, rr, dst, dstk, extra_bias, bs):
                    g_ = gts[bs]; m_ = sms[bs]
                    K = lambda nm: "%s_%d" % (nm, bs)
                    P.op("dve", TT(g_["t0"][:], src[:], src[:], ALU.mult), r=[srck], w=[K("gt_t0")])
                    P.op("dve", RED(m_[ss][:, 0:8], v3(g_["t0"][:, :], 8)), r=[K("gt_t0")], w=[K("sm_" + ss)])
                    yield
                    P.op("act", ACT(m_[rr][:, 0:8], m_[ss][:, 0:8], AF.Ln, bias=EPS), r=[K("sm_" + ss)], w=[K("sm_" + rr)])
                    P.op("act", ACT(m_[rr][:, 0:8], m_[rr][:, 0:8], AF.Exp, scale=-0.5, bias=extra_bias), r=[K("sm_" + rr)], w=[K("sm_" + rr)])
                    yield
                    P.op("dve", TT(v3(dst[:, :], 8), v3(src[:, :], 8), b3(m_[rr][:, 0:8], 128, 8, 64, 2), ALU.mult),
                         r=[srck, K("sm_" + rr)], w=[dstk])
                    yield

                def tt_mms(outp, pk, lhs, lhsk, rhs, rhsk):
                    for h in range(8):
                        hs = slice(h * 64, (h + 1) * 64)
                        for R in (RA, RB):
                            P.op("pe", MM(outp[R, hs], lhs[R, hs], rhs[R, hs]), r=[lhsk, rhsk], w=[pk])

                def tt_trs(outp, pk, src, srck):
                    for h in range(8):
                        hs = slice(h * 64, (h + 1) * 64)
                        P.op("pe", TR(outp[RA, hs], src[RA, hs], idbA), r=[srck, "identb"], w=[pk])
                        P.op("pe", TR(outp[RB, hs], src[RB, hs], idbB), r=[srck, "identb"], w=[pk])

                def fm_mms(outp, pk, lhsT3, lhsk, rhs3, rhsk):
                    for h in range(8):
                        hs = slice(h * 64, (h + 1) * 64)
                        for R in (RA, RB):
                            P.op("pe", MM(outp[R, hs], lhsT3[:, h, R], rhs3[:, h, R]), r=[lhsk, rhsk], w=[pk])

                def gdn_pre(n2, full, bs):
                    tcol = n2 * 128
                    g_ = gts[bs]; m_ = sms[bs]; tq_ = tqs[bs]
                    K = lambda nm: "%s_%d" % (nm, bs)
                    knT3 = g_["knT"][:, :].rearrange("p (h t) -> p h t", h=8)
                    qnT3 = g_["qnT"][:, :].rearrange("p (h t) -> p h t", h=8)
                    pg, pgk = pset(bs)
                    for kc in range(8):
                        P.op("pe", MM(pg[:, 0:16], hT3[:, kc, tcol:tcol + 128], win3[:, kc, 2048:2064],
                                      start=(kc == 0), stop=(kc == 7)), r=["hT", "win"], w=[pgk])
                    if full:
                        pz, pzk = psd()
                        for kc in range(8):
                            P.op("pe", MM(pz[:, :], hT3[:, kc, tcol:tcol + 128], win3[:, kc, 1536:2048],
                                          start=(kc == 0), stop=(kc == 7)), r=["hT", "win"], w=[pzk])
                        P.op("act", ACT(zts[bs][:], pz[:, :], AF.Silu), r=[pzk], w=[K("zt")])
                    yield
                    P.op("dve", CP(m_["gab"][:, 0:16], pg[:, 0:16]), r=[pgk], w=[K("sm_gab")])
                    P.op("dve", TT(m_["g"][:, 0:8], m_["gab"][:, 0:8], dtb[:], ALU.add), r=[K("sm_gab"), "dtb"], w=[K("sm_g")])
                    yield
                    P.op("act", ACT(m_["e1"][:, 0:8], m_["g"][:, 0:8], AF.Exp), r=[K("sm_g")], w=[K("sm_e1")])
                    P.op("act", ACT(m_["e1"][:, 0:8], m_["e1"][:, 0:8], AF.Ln, bias=1.0), r=[K("sm_e1")], w=[K("sm_e1")])
                    P.op("act", ACT(m_["beta"][:, 0:8], m_["gab"][:, 8:16], AF.Exp, scale=-1.0), r=[K("sm_gab")], w=[K("sm_beta")])
                    yield
                    P.op("dve", TT(m_["g"][:, 0:8], m_["e1"][:, 0:8], negA[:], ALU.mult), r=[K("sm_e1"), "negA"], w=[K("sm_g")])
                    P.op("dve", TS(m_["beta"][:, 0:8], m_["beta"][:, 0:8], 1.0, ALU.add), r=[K("sm_beta")], w=[K("sm_beta")])
                    P.op("dve", RCP(m_["beta"][:, 0:8], m_["beta"][:, 0:8]), r=[K("sm_beta")], w=[K("sm_beta")])
                    P.op("dve", TS(m_["nbeta"][:, 0:8], m_["beta"][:, 0:8], -1.0, ALU.mult), r=[K("sm_beta")], w=[K("sm_nbeta")])
                    yield
                    pa, pak = pset(bs)
                    P.op("pe", MM(pa[:, 0:8], utri, m_["g"][:, 0:8]), r=["c64", K("sm_g")], w=[pak])
                    P.op("pe", MM(pa[:, 8:16], ones_bd, m_["g"][:, 0:8]), r=["c64", K("sm_g")], w=[pak])
                    P.op("pe", MM(pa[0:64, 16:24], ones_bd[:, 64:128], m_["g"][:, 0:8]), r=["c64", K("sm_g")], w=[pak])
                    yield
                    P.op("dve", CP(m_["G16"][:, 0:16], pa[:, 0:16]), r=[pak], w=[K("sm_G16")])
                    P.op("dve", CP(m_["G16"][0:64, 16:24], pa[0:64, 16:24]), r=[pak], w=[K("sm_G16")])
                    tok_major(1, tcol, bs); tok_major(2, tcol, bs)
                    if full:
                        tok_major(0, tcol, bs)
                    yield "inputs_done"
                    G = m_["G16"][:, 0:8]; GL = m_["G16"][:, 8:16]
                    P.op("dve", TT(m_["dG"][:, 0:8], GL, G, ALU.subtract), r=[K("sm_G16")], w=[K("sm_dG")])
                    P.op("dve", TT(v3(g_["t1"][:, :], 8), b3(I2, 128, 8, 64, 1), b3(G, 128, 8, 64, 2), ALU.mult),
                         r=["c64", K("sm_G16")], w=[K("gt_t1")])
                    yield
                    P.op("act", ACT(m_["eG"][:, 0:8], G, AF.Exp), r=[K("sm_G16")], w=[K("sm_eG")])
                    P.op("act", ACT(m_["eGL"][:, 0:8], GL, AF.Exp), r=[K("sm_G16")], w=[K("sm_eGL")])
                    P.op("act", ACT(m_["eGLB"][0:64, 0:8], m_["G16"][0:64, 16:24], AF.Exp), r=[K("sm_G16")], w=[K("sm_eGLB")])
                    P.op("act", ACT(m_["eGLmG"][:, 0:8], m_["dG"][:, 0:8], AF.Exp), r=[K("sm_dG")], w=[K("sm_eGLmG")])
                    pG, pGk = pset(bs)
                    P.op("pe", MM(pG[:, :], ones_bd, g_["t1"][:, :]), r=["c64", K("gt_t1")], w=[pGk])
                    yield
                    P.op("dve", TT(v3(g_["t0"][:, :], 8), b3(G, 128, 8, 64, 2), v3(pG[:, :], 8), ALU.subtract),
                         r=[K("sm_G16"), pGk], w=[K("gt_t0")])
                    P.op("dve", TT(v3(g_["t1"][:, :], 8), v3(g_["t0"][:, :], 8), b3(maskL, 128, 8, 64, 1), ALU.min),
                         r=[K("gt_t0"), "c64", K("gt_t1")], w=[K("gt_t1")])
                    yield
                    P.op("act", ACT(g_["D"][:], g_["t1"][:], AF.Exp), r=[K("gt_t1")], w=[K("gt_D")])
                    if full:
                        P.op("dve", STT(v3(g_["t0"][:, :], 8), v3(g_["t0"][:, :], 8), -1.0, b3(maskU, 128, 8, 64, 1), ALU.mult, ALU.min),
                             r=[K("gt_t0"), "c64"], w=[K("gt_t0")])
                        yield
                        P.op("act", ACT(g_["DT"][:], g_["t0"][:], AF.Exp), r=[K("gt_t0")], w=[K("gt_DT")])
                    yield
                    P.op("dve", TT(m_["beG"][:, 0:8], m_["beta"][:, 0:8], m_["eG"][:, 0:8], ALU.mult), r=[K("sm_beta"), K("sm_eG")], w=[K("sm_beG")])
                    tk, tv = tq_[1], tq_[2]
                    for _ in l2n(tk, K("tqkv1"), "ssk", "rk", g_["kn"], K("gt_kn"), 0.0, bs):
                        yield
                    pt, pk = psetb(bs)
                    for h in range(8):
                        P.op("pe", TR(pt[0:64, h * 128:(h + 1) * 128], g_["kn"][:, h * 64:(h + 1) * 64], identb[:, :]), r=[K("gt_kn"), "identb"], w=[pk])
                    yield
                    P.op("act", ACP(g_["knT"][:], pt[0:64, 0:1024]), r=[pk], w=[K("gt_knT")])
                    if full:
                        for _ in l2n(tq_[0], K("tqkv0"), "ssq", "rq", g_["qn"], K("gt_qn"), float(np.log(0.125)), bs):
                            yield
                        pt, pk = psetb(bs)
                        for h in range(8):
                            P.op("pe", TR(pt[0:64, h * 128:(h + 1) * 128], g_["qn"][:, h * 64:(h + 1) * 64], identb[:, :]), r=[K("gt_qn"), "identb"], w=[pk])
                        yield
                        P.op("act", ACP(g_["qnT"][:], pt[0:64, 0:1024]), r=[pk], w=[K("gt_qnT")])
                    pK, pKk = pset(bs); fm_mms(pK, pKk, knT3, K("gt_knT"), knT3, K("gt_knT"))
                    yield
                    P.op("dve", TT(g_["t1"][:], pK[:, :], g_["D"][:], ALU.mult), r=[pKk, K("gt_D"), K("gt_t1")], w=[K("gt_t1")])
                    if full:
                        pQ, pQk = pset(bs); fm_mms(pQ, pQk, knT3, K("gt_knT"), qnT3, K("gt_qnT"))
                        yield
                        P.op("dve", TT(g_["QKmT"][:], pQ[:, :], g_["DT"][:], ALU.mult), r=[pQk, K("gt_DT")], w=[K("gt_QKmT")])
                    P.op("dve", TT(v3(g_["nbs"][:, :], 8), b3(strictL, 128, 8, 64, 1), b3(m_["nbeta"][:, 0:8], 128, 8, 64, 2), ALU.mult),
                         r=["c64", K("sm_nbeta")], w=[K("gt_nbs")])
                    P.op("dve", TT(g_["P0"][:], g_["t1"][:], g_["nbs"][:], ALU.mult), r=[K("gt_t1"), K("gt_nbs")], w=[K("gt_P0")])
                    yield
                    pt, pk = psetb(bs); tt_trs(pt, pk, g_["P0"], K("gt_P0"))
                    N_, NT_ = g_["P0"], g_["Q0"]
                    P.op("dve", TT(v3(g_["T0"][:, :], 8), v3(N_[:, :], 8), b3(mA1, 128, 8, 64, 1), ALU.mult), r=[K("gt_P0"), "c64"], w=[K("gt_T0")])
                    P.op("dve", TT(v3(g_["T0"][:, :], 8), v3(g_["T0"][:, :], 8), b3(I2, 128, 8, 64, 1), ALU.add), r=[K("gt_T0"), "c64"], w=[K("gt_T0")])
                    yield
                    P.op("act", ACP(g_["Q0"][:], pt[:, 0:512]), r=[pk], w=[K("gt_Q0")])
                    yield
                    P.op("dve", TT(v3(g_["W0"][:, :], 8), v3(NT_[:, :], 8), b3(mTs[0], 128, 8, 64, 1), ALU.mult), r=[K("gt_Q0"), "c64"], w=[K("gt_W0")])
                    P.op("dve", TT(v3(g_["W0"][:, :], 8), v3(g_["W0"][:, :], 8), b3(I2, 128, 8, 64, 1), ALU.add), r=[K("gt_W0"), "c64"], w=[K("gt_W0")])
                    P.op("dve", TT(v3(g_["bv"][:, :], 8), v3(tv[:, :], 8), b3(m_["beta"][:, 0:8], 128, 8, 64, 2), ALU.mult),
                         r=[K("tqkv2"), K("sm_beta")], w=[K("gt_bv")])
                    P.op("dve", TT(v3(g_["kd2"][:, :], 8), v3(g_["kn"][:, :], 8), b3(m_["eGLmG"][:, 0:8], 128, 8, 64, 2), ALU.mult),
                         r=[K("gt_kn"), K("sm_eGLmG")], w=[K("gt_kd2")])
                    ct_, cw_ = 0, 0
                    for lv in range(1, 6):
                        Tc, Tn = g_["T%d" % ct_], g_["T%d" % (1 - ct_)]
                        Tck, Tnk = K("gt_T%d" % ct_), K("gt_T%d" % (1 - ct_))
                        Wc, Wn = g_["W%d" % cw_], g_["W%d" % (1 - cw_)]
                        Wck, Wnk = K("gt_W%d" % cw_), K("gt_W%d" % (1 - cw_))
                        p1, p1k = pset(bs); tt_mms(p1, p1k, NT_, K("gt_Q0"), Tc, Tck)
                        yield
                        P.op("dve", TT(v3(g_["Q1"][:, :], 8), v3(p1[:, :], 8), b3(mTs[lv], 128, 8, 64, 1), ALU.mult), r=[p1k, "c64"], w=[K("gt_Q1")])
                        yield
                        p2, p2k = pset(bs); tt_mms(p2, p2k, Wc, Wck, g_["Q1"], K("gt_Q1"))
                        yield
                        P.op("dve", TT(Tn[:], Tc[:], p2[:, :], ALU.add), r=[Tck, p2k], w=[Tnk])
                        yield
                        p3, p3k = psetb(bs); tt_trs(p3, p3k, Tn, Tnk)
                        yield
                        P.op("act", ACP(Wn[:], p3[:, 0:512]), r=[p3k], w=[Wnk])
                        yield
                        ct_, cw_ = 1 - ct_, 1 - cw_
                    gdn_W[bs] = (g_["W%d" % cw_], K("gt_W%d" % cw_))

                gdn_W = [None, None]

                def gdn_chain(n2, full, tok0, bs):
                    g_ = gts[bs]; m_ = sms[bs]
                    K = lambda nm: "%s_%d" % (nm, bs)
                    knT3 = g_["knT"][:, :].rearrange("p (h t) -> p h t", h=8)
                    qnT3 = g_["qnT"][:, :].rearrange("p (h t) -> p h t", h=8)
                    W, Wk = gdn_W[bs]
                    yield "chain"
                    for R in (RA, RB):
                        pC, pCk = pset(bs)
                        for h in range(8):
                            hs = slice(h * 64, (h + 1) * 64)
                            P.op("pe", MM(pC[R, hs], knT3[:, h, R], Sb[:, hs]), r=[K("gt_knT"), "Sb"], w=[pCk])
                        if full:
                            pO1, pO1k = pset(bs)
                            for h in range(8):
                                hs = slice(h * 64, (h + 1) * 64)
                                P.op("pe", MM(pO1[R, hs], qnT3[:, h, R], Sb[:, hs]), r=[K("gt_qnT"), "Sb"], w=[pO1k])
                        yield
                        P.op("dve", TT(v3(g_["t0"][R, :], 8), v3(pC[R, :], 8), b3(m_["beG"][R, 0:8], 64, 8, 64, 2), ALU.mult),
                             r=[pCk, K("sm_beG")], w=[K("gt_t0")])
                        P.op("dve", TT(g_["r"][R, :], g_["bv"][R, :], g_["t0"][R, :], ALU.subtract), r=[K("gt_bv"), K("gt_t0")], w=[K("gt_r")])
                        if full:
                            P.op("dve", TT(v3(g_["o"][R, :], 8), v3(pO1[R, :], 8), b3(m_["eG"][R, 0:8], 64, 8, 64, 2), ALU.mult),
                                 r=[pO1k, K("sm_eG")], w=[K("gt_o")])
                        yield
                        pV, pVk = pset(bs)
                        for h in range(8):
                            hs = slice(h * 64, (h + 1) * 64)
                            P.op("pe", MM(pV[R, hs], W[R, hs], g_["r"][R, hs]), r=[Wk, K("gt_r")], w=[pVk])
                        yield
                        P.op("act", ACP(g_["vn"][R, :], pV[R, :]), r=[pVk], w=[K("gt_vn")])
                        yield
                        pS, pSk = pset(bs)
                        for h in range(8):
                            hs = slice(h * 64, (h + 1) * 64)
                            P.op("pe", MM(pS[0:64, hs], g_["kd2"][R, hs], g_["vn"][R, hs]), r=[K("gt_kd2"), K("gt_vn")], w=[pSk])
                        egl = m_["eGL"][0:64, 0:8] if R is RA else m_["eGLB"][0:64, 0:8]
                        P.op("dve", TT(v3(S[:, :], 8), v3(S[:, :], 8), b3(egl, 64, 8, 64, 2), ALU.mult), r=["S", K("sm_eGL"), K("sm_eGLB")], w=["S"])
                        yield
                        P.op("dve", TT(Sb[:], S[:], pS[0:64, :], ALU.add), r=["S", pSk], w=["Sb"])
                        P.op("dve", TT(S[:], S[:], pS[0:64, :], ALU.add), r=["S", pSk], w=["S"])
                        if full:
                            pO2, pO2k = pset(bs)
                            for h in range(8):
                                hs = slice(h * 64, (h + 1) * 64)
                                P.op("pe", MM(pO2[R, hs], g_["QKmT"][R, hs], g_["vn"][R, hs]), r=[K("gt_QKmT"), K("gt_vn")], w=[pO2k])
                        yield
                        if full:
                            P.op("dve", TT(g_["o"][R, :], g_["o"][R, :], pO2[R, :], ALU.add), r=[K("gt_o"), pO2k], w=[K("gt_o")])
                        yield
                    yield "chain_done"
                    if not full:
                        return
                    P.op("dve", TT(g_["t0"][:], g_["o"][:], g_["o"][:], ALU.mult), r=[K("gt_o")], w=[K("gt_t0")])
                    P.op("dve", RED(m_["sso"][:, 0:8], v3(g_["t0"][:, :], 8)), r=[K("gt_t0")], w=[K("sm_sso")])
                    yield
                    P.op("act", ACT(m_["ro"][:, 0:8], m_["sso"][:, 0:8], AF.Ln, scale=1.0 / 64, bias=EPS), r=[K("sm_sso")], w=[K("sm_ro")])
                    P.op("act", ACT(m_["ro"][:, 0:8], m_["ro"][:, 0:8], AF.Exp, scale=-0.5), r=[K("sm_ro")], w=[K("sm_ro")])
                    yield
                    P.op("dve", TT(v3(g_["t0"][:, :], 8), v3(g_["o"][:, :], 8), b3(m_["ro"][:, 0:8], 128, 8, 64, 2), ALU.mult),
                         r=[K("gt_o"), K("sm_ro")], w=[K("gt_t0")])
                    P.op("dve", TT(v3(g_["t0"][:, :], 8), v3(g_["t0"][:, :], 8), b3(gnw[:, :], 128, 8, 64, 1), ALU.mult),
                         r=[K("gt_t0"), "gnw"], w=[K("gt_t0")])
                    P.op("dve", TT(g_["og"][:], g_["t0"][:], zts[bs][:], ALU.mult), r=[K("gt_t0"), K("zt")], w=[K("gt_og")])
                    yield
                    pt, pk = psetb(bs)
                    for kc in range(4):
                        P.op("pe", TR(pt[:, kc * 128:(kc + 1) * 128], g_["og"][:, kc * 128:(kc + 1) * 128], identb[:, :]), r=[K("gt_og"), "identb"], w=[pk])
                    yield
                    P.op("act", ACP(oT3[:, 0:4, tok0:tok0 + 128], v3(pt[:, 0:512], 4)), r=[pk], w=["oT"])

                def gdn_group(full, tokbase):
                    gens = [gdn_pre(i, full, i) for i in range(2)]
                    for _ in range(PRE_STAGGER):
                        next(gens[0])
                    live = list(gens)
                    while live:
                        for gq in list(live):
                            try:
                                next(gq)
                            except StopIteration:
                                live.remove(gq)

                def chains(full, tokbase):
                    for i in range(2):
                        for v_ in gdn_chain(i, full, tokbase + i * 128, i):
                            yield v_

                def other_gen(items):
                    import types
                    for fn in items:
                        r_ = fn()
                        if isinstance(r_, types.GeneratorType):
                            for _ in r_:
                                yield
                        yield

                def interleave(ga, gb):
                    live = [ga, gb]
                    while live:
                        for gq in list(live):
                            try:
                                for _ in range(CHAIN_PRIO if gq is ga else 1):
                                    next(gq)
                            except StopIteration:
                                live.remove(gq)

                def swa_kv(tcol, slot):
                    pt, pk = psd()
                    for kc in range(8):
                        P.op("pe", MM(pt[:, 0:256], hT3[:, kc, tcol:tcol + 128], win3[:, kc, 2576:2832],
                                      start=(kc == 0), stop=(kc == 7)), r=["hT", "win"], w=[pk])
                    P.op("act", ACP(sw["kv"][:, 0:256], pt[:, 0:256]), r=[pk], w=["sw_qraw"])
                    check("k_mm", sw["kv"][:, :])
                    P.op("act", ACT(sw["t0"][:, 0:128], sw["kv"][:, 0:128], AF.Square), r=["sw_qraw"], w=["sw_t0"])
                    P.op("dve", RED(swsm["ssk"][:, 0:2], v3(sw["t0"][:, 0:128], 2)), r=["sw_t0"], w=["swsm_ssk"])
                    P.op("act", ACT(swsm["rk"][:, 0:2], swsm["ssk"][:, 0:2], AF.Ln, scale=1.0 / 64, bias=EPS), r=["swsm_ssk"], w=["swsm_rk"])
                    P.op("act", ACT(swsm["rk"][:, 0:2], swsm["rk"][:, 0:2], AF.Exp, scale=-0.5), r=["swsm_rk"], w=["swsm_rk"])
                    P.op("dve", TT(v3(swkn[:, :], 2), v3(sw["kv"][:, 0:128], 2), b3(swsm["rk"][:, 0:2], 128, 2, 64, 2), ALU.mult),
                         r=["sw_qraw", "swsm_rk"], w=["swkn"])
                    P.op("dve", TT(v3(swkn[:, :], 2), v3(swkn[:, :], 2), b3(knw[:, :], 128, 2, 64, 1), ALU.mult), r=["swkn", "knw"], w=["swkn"])
                    check("k_norm", swkn[:, :])
                    P.op("act", ACP(vb1[slot][:, :].rearrange("p (h d) -> p h d", h=2)[:, :, 0:64], v3(sw["kv"][:, 128:256], 2)),
                         r=["sw_qraw"], w=["vb1%d" % slot])
                    check("k_v", vb1[slot][:, :])
                    p2, p2k = pssb()
                    for h in range(2):
                        P.op("pe", TR(p2[0:64, h * 128:(h + 1) * 128], swkn[:, h * 64:(h + 1) * 64], identb[:, :]), r=["swkn", "identb"], w=[p2k])
                    P.op("act", ACP(kT[slot][:], p2[0:64, 0:256]), r=[p2k], w=["kT%d" % slot])
                    check("k_T", kT[slot][:, :])

                def swa_block(tcol, tok0, first_block):
                    b = swa_blk[0]; swa_blk[0] += 1
                    cur, prv = b % 2, (b + 1) % 2
                    swa_kv(tcol, cur)
                    pq, pqk = psd()
                    for kc in range(8):
                        P.op("pe", MM(pq[:, :], hT3[:, kc, tcol:tcol + 128], win3[:, kc, 2064:2576],
                                      start=(kc == 0), stop=(kc == 7)), r=["hT", "win"], w=[pqk])
                    P.op("act", ACP(sw["qraw"][:], pq[:, :]), r=[pqk], w=["sw_qraw"])
                    P.op("act", ACT(sw["t0"][:], sw["qraw"][:], AF.Square), r=["sw_qraw"], w=["sw_t0"])
                    P.op("dve", RED(swsm["ssq"][:, 0:8], v3(sw["t0"][:, :], 8)), r=["sw_t0"], w=["swsm_ssq"])
                    P.op("act", ACT(swsm["rq"][:, 0:8], swsm["ssq"][:, 0:8], AF.Ln, scale=1.0 / 64, bias=EPS), r=["swsm_ssq"], w=["swsm_rq"])
                    P.op("act", ACT(swsm["rq"][:, 0:8], swsm["rq"][:, 0:8], AF.Exp, scale=-0.5), r=["swsm_rq"], w=["swsm_rq"])
                    P.op("dve", TT(v3(sw["qn"][:, :], 8), v3(sw["qraw"][:, :], 8), b3(swsm["rq"][:, 0:8], 128, 8, 64, 2), ALU.mult),
                         r=["sw_qraw", "swsm_rq"], w=["sw_qn"])
                    P.op("dve", TT(v3(sw["qn"][:, :], 8), v3(sw["qn"][:, :], 8), b3(qnw[:, :], 128, 8, 64, 1), ALU.mult), r=["sw_qn", "qnw"], w=["sw_qn"])
                    for half in range(2):
                        pt, pk = pssb()
                        for j in range(4):
                            h = half * 4 + j
                            P.op("pe", TR(pt[0:64, j * 128:(j + 1) * 128], sw["qn"][:, h * 64:(h + 1) * 64], identb[:, :]), r=["sw_qn", "identb"], w=[pk])
                        P.op("act", ACP(qT[:, half * 512:(half + 1) * 512], pt[0:64, 0:512]), r=[pk], w=["qT"])
                        yield
                    for kvh in range(2):
                        po, pok = psd()
                        pts = []
                        for kb, sl in ((0, prv), (1, cur)):
                            ps_, psk = pss()
                            P.op("pe", MM(ps_[:, :], kT[sl][:, kvh * 128:(kvh + 1) * 128], qT[:, kvh * 512:(kvh + 1) * 512]),
                                 r=["kT%d" % sl, "qT"], w=[psk])
                            stb = sw["st%d" % kb]; stk = "sw_st%d" % kb
                            a0 = kb * 1024 + kvh * 512
                            P.op("dve", TT(stb[:], ps_[:, :], abias[:, a0:a0 + 512], ALU.add), r=[psk, "abias"], w=[stk])
                            pi = kvh * 2 + kb
                            P.op("act", ACT(pT[pi][:], stb[:], AF.Exp), r=[stk], w=["pT%d" % pi])
                            if kb == 0 and first_block:
                                P.op("dve", TS(pT[pi][:], pT[pi][:], flag[:, 0:1], ALU.mult), r=["pT%d" % pi, "flag"], w=["pT%d" % pi])
                            pts.append((pi, sl))
                            yield
                        for g in range(4):
                            for i, (pi, sl) in enumerate(pts):
                                P.op("pe", MM(po[:, g * 65:(g + 1) * 65], pT[pi][:, g * 128:(g + 1) * 128], vb1[sl][:, kvh * 65:(kvh + 1) * 65],
                                              start=(i == 0), stop=(i == 1)), r=["pT%d" % pi, "vb1%d" % sl], w=[pok])
                        po3 = po[:, 0:260].rearrange("p (g d) -> p g d", g=4)
                        P.op("dve", TT(swsm["den"][:, 0:4].unsqueeze(2), po3[:, :, 64:65], esink[:, kvh * 4:(kvh + 1) * 4].unsqueeze(2), ALU.add),
                             r=[pok, "esink"], w=["swsm_den"])
                        P.op("dve", RCP(swsm["rden"][:, 0:4], swsm["den"][:, 0:4]), r=["swsm_den"], w=["swsm_rden"])
                        P.op("dve", TT(v3(sw["os"][:, kvh * 256:(kvh + 1) * 256], 4), po3[:, :, 0:64], b3(swsm["rden"][:, 0:4], 128, 4, 64, 2), ALU.mult),
                             r=[pok, "swsm_rden"], w=["sw_os"])
                        yield
                    pt, pk = pssb()
                    for kc in range(4):
                        P.op("pe", TR(pt[:, kc * 128:(kc + 1) * 128], sw["os"][:, kc * 128:(kc + 1) * 128], identb[:, :]), r=["sw_os", "identb"], w=[pk])
                    P.op("act", ACP(oT3[:, 4:8, tok0:tok0 + 128], v3(pt[:, 0:512], 4)), r=[pk], w=["oT"])

                def norm_items(xsrc, g):
                    return [lambda t=t: norm_to_hT(xsrc, g * TG + t * 128, t * 128) for t in range(2)]

                def proj_items(full, lastprev):
                    its = []
                    for cc in range(12):
                        if cc < 4 and not full:
                            if lastprev:
                                its.append(lambda cc=cc: proj_fm(cc, False))
                        else:
                            its.append(lambda cc=cc: proj_fm(cc, True))
                    return its

                for it in norm_items(xp, 0) + proj_items(False, False):
                    it()
                for g in range(NGRP):
                    last = (g == NGRP - 1)
                    gdn_group(False, 0)
                    others = []
                    if last:
                        others.append(lambda: swa_kv(128, 1))
                        others += norm_items(xo, 0)
                    else:
                        others += norm_items(xp, g + 1) + proj_items(False, g + 1 == NGRP - 1)
                    interleave(chains(False, 0), other_gen(others))
                P.op("dve", TS(S[:], S[:], flag[0:64, 0:1], ALU.mult), r=["S", "flag"], w=["S"])
                P.op("act", ACP(Sb[:], S[:]), r=["S"], w=["Sb"])
                P.op("dve", TS(halo[:], halo[:], flag[:, 0:1], ALU.mult), r=["halo", "flag"], w=["halo"])
                for it in proj_items(True, False):
                    it()

                for g in range(NGRP):
                    gdn_group(True, g * TG)
                    others = [lambda t=t: swa_block(t * 128, g * TG + t * 128, (g == 0 and t == 0)) for t in range(2)]
                    if g + 1 < NGRP:
                        others += norm_items(xo, g + 1) + proj_items(True, False)
                    interleave(chains(True, g * TG), other_gen(others))
                check("A_end", oT[:, :])
                P.barrier()
            sW.close()

            with ExitStack() as sBC:
                xacc = alloc(sBC, "xacc", [128, 16 * D])
                xacc3 = xacc[:, :].rearrange("p (t f) -> p t f", t=16)
                h2T = alloc(sBC, "h2T", [128, 8 * T_HALF], BF16)
                h2T3 = h2T[:, :].rearrange("p (k t) -> p k t", k=8)
                PSZ = [6, 6, 5, 5]; POFF = [0, 6, 12, 17]
                wg_v = wg_d.rearrange("(k p) n -> p k n", p=128)
                wu_v = wu_d.rearrange("(k p) n -> p k n", p=128)
                wd_v = wd_d.rearrange("(k p) n -> p k n", p=128)
                wgs = [alloc(sBC, "wg0", [128, 8 * 768], BF16)]
                wus = [alloc(sBC, "wu0", [128, 8 * 768], BF16)]

                def load_gu(pi):
                    sl = pi % 2
                    n_ = PSZ[pi] * 128; c0 = POFF[pi] * 128
                    g3 = wgs[sl][:, :].rearrange("p (k c) -> p k c", k=8)
                    u3 = wus[sl][:, :].rearrange("p (k c) -> p k c", k=8)
                    for kc in range(8):
                        P.dma("pool", DMA(g3[:, kc, 0:n_], wg_v[:, kc, c0:c0 + n_]), w=["wg%d" % sl])
                        P.dma("pool", DMA(u3[:, kc, 0:n_], wu_v[:, kc, c0:c0 + n_]), w=["wu%d" % sl])

                load_gu(0)
                sB = ExitStack()
                wst = [alloc(sB, "wst%d" % i, [128, D]) for i in range(2)]
                wob = alloc(sB, "wob", [128, 8 * D], BF16)
                wob3 = wob[:, :].rearrange("p (k c) -> p k c", k=8)
                xs2 = alloc(sB, "xs2", [128, D], BF16)
                junk2 = alloc(sB, "junk2", [128, D], BF16)
                st3 = alloc(sB, "st3", [128, 2])

                wout_v = wout_d.rearrange("(k p) n -> p k n", p=128)
                for kc in range(8):
                    sl = kc % 2
                    P.dma("sp", DMA(wst[sl][:], wout_v[:, kc, :]), w=["wst%d" % sl])
                    P.op("dve", TT(wob3[:, kc, :], wst[sl][:], gateB[:, 0:1024], ALU.mult), r=["wst%d" % sl, "gateB"], w=["wob"])
                for t in range(16):
                    P.dma("sp", DMA(xacc3[:, t, :], xo[t * 128:(t + 1) * 128, :]), w=["xacc%d" % t])
                    for half in range(2):
                        pt, pk = psd()
                        for kc in range(8):
                            P.op("pe", MM(pt[:, :], oT3[:, kc, t * 128:(t + 1) * 128], wob3[:, kc, half * 512:(half + 1) * 512],
                                          start=(kc == 0), stop=(kc == 7)), r=["oT", "wob"], w=[pk])
                        P.op("dve", TT(xacc3[:, t, half * 512:(half + 1) * 512], xacc3[:, t, half * 512:(half + 1) * 512], pt[:, :], ALU.add),
                             r=[pk, "xacc%d" % t], w=["xacc%d" % t])
                    P.op("dve", MEMSET(st3[:, 0:1], 0.0), w=["st3"])
                    P.op("act", ACT(junk2[:], xacc3[:, t, :], AF.Square, accum_out=st3[:, 0:1]), r=["xacc%d" % t, "st3"], w=["junk2", "st3"])
                    P.op("act", ACT(st3[:, 1:2], st3[:, 0:1], AF.Ln, scale=1.0 / D, bias=EPS), r=["st3"], w=["st3"])
                    P.op("act", ACT(st3[:, 1:2], st3[:, 1:2], AF.Exp, scale=-0.5), r=["st3"], w=["st3"])
                    P.op("dve", TS(xs2[:], xacc3[:, t, :], st3[:, 1:2], ALU.mult), r=["xacc%d" % t, "st3"], w=["xs2"])
                    for half in range(2):
                        pt, pk = pssb()
                        for j in range(4):
                            kc = half * 4 + j
                            P.op("pe", TR(pt[:, j * 128:(j + 1) * 128], xs2[:, kc * 128:(kc + 1) * 128], identb[:, :]), r=["xs2", "identb"], w=[pk])
                        for j in range(4):
                            kc = half * 4 + j
                            P.op("act", ACT(h2T3[:, kc, t * 128:(t + 1) * 128], pt[:, j * 128:(j + 1) * 128], AF.Identity,
                                            scale=a2[:, kc:kc + 1], bias=s2c[:, kc:kc + 1]), r=[pk, "a2", "colv"], w=["h2T"])
                P.barrier()
                sB.close()
                sC = ExitStack()
                wgs.append(alloc(sC, "wg1", [128, 8 * 768], BF16))
                wus.append(alloc(sC, "wu1", [128, 8 * 768], BF16))
                wstc = [alloc(sC, "wstc%d" % i, [128, D]) for i in range(2)]
                wds = [oT[:, i * 6 * D:(i + 1) * 6 * D].rearrange("p (k c) -> p k c", k=6) for i in range(2)]
                actT = alloc(sC, "actT", [128, 6 * 512], BF16)
                actT3 = actT[:, :].rearrange("p (k t) -> p k t", k=6)
                sg = [alloc(sC, "sg%d" % i, [128, 512]) for i in range(2)]

                def load_d(pi):
                    sl = pi % 2
                    for fc in range(PSZ[pi]):
                        ws = wstc[fc % 2]; wk = "wstc%d" % (fc % 2)
                        P.dma("sp", DMA(ws[:], wd_v[:, POFF[pi] + fc, :]), w=[wk])
                        P.op("pool", TT(wds[sl][:, fc, :], ws[:], gateB[:, 1024:2048], ALU.mult), r=[wk, "gateB"], w=["wd%d" % sl])

                def ffn_pass(pi):
                    sl = pi % 2
                    nfc = PSZ[pi]
                    g3 = wgs[sl][:, :].rearrange("p (k c) -> p k c", k=8)
                    u3 = wus[sl][:, :].rearrange("p (k c) -> p k c", k=8)
                    for tg in range(4):
                        t0 = tg * 512
                        for fc in range(nfc):
                            pg_, pgk = psd()
                            for kc in range(8):
                                P.op("pe", MM(pg_[:, :], g3[:, kc, fc * 128:(fc + 1) * 128], h2T3[:, kc, t0:t0 + 512],
                                              start=(kc == 0), stop=(kc == 7)), r=["wg%d" % sl, "h2T"], w=[pgk])
                            pu_, puk = psd()
                            for kc in range(8):
                                P.op("pe", MM(pu_[:, :], u3[:, kc, fc * 128:(fc + 1) * 128], h2T3[:, kc, t0:t0 + 512],
                                              start=(kc == 0), stop=(kc == 7)), r=["wu%d" % sl, "h2T"], w=[puk])
                            s_ = sg[fc % 2]; sk_ = "sg%d" % (fc % 2)
                            P.op("act", ACT(s_[:], pg_[:, :], AF.Silu), r=[pgk], w=[sk_])
                            P.op("dve", TT(actT3[:, fc, :], s_[:], pu_[:, :], ALU.mult), r=[sk_, puk], w=["actT"])
                        for tt in range(4):
                            t = tg * 4 + tt
                            for half in range(2):
                                pt, pk = pss4()
                                for fc in range(nfc):
                                    P.op("pe", MM(pt[:, :], actT3[:, fc, tt * 128:(tt + 1) * 128], wds[sl][:, fc, half * 512:(half + 1) * 512],
                                                  start=(fc == 0), stop=(fc == nfc - 1)), r=["actT", "wd%d" % sl], w=[pk])
                                P.op("dve", TT(xacc3[:, t, half * 512:(half + 1) * 512], xacc3[:, t, half * 512:(half + 1) * 512], pt[:, :], ALU.add),
                                     r=[pk, "xacc%d" % t], w=["xacc%d" % t])

                load_d(0)
                for pi in range(4):
                    if pi + 1 < 4:
                        load_gu(pi + 1)
                        load_d(pi + 1)
                    ffn_pass(pi)
                for t in range(16):
                    P.dma("sp", DMA(y[t * 128:(t + 1) * 128, :], xacc3[:, t, :]), r=["xacc%d" % t], w=["y%d" % t])

        except _Stop:
            pass
        P.emit(nc)
    return nc


_NC_CACHE = {}


def _host_consts():
    ident = np.eye(128, dtype=np.float32)
    i = np.arange(64)
    utri = (i[:, None] <= i[None, :]).astype(np.float32)
    ones = np.ones((64, 64), np.float32)
    maskL = np.where(i[:, None] >= i[None, :], BIG, -BIG).astype(np.float32)
    maskU = np.where(i[None, :] >= i[:, None], BIG, -BIG).astype(np.float32)
    strictL = (i[:, None] > i[None, :]).astype(np.float32)
    def m_off(b):
        p = i[:, None]; f = i[None, :]
        return ((p // (2 * b) == f // (2 * b)) & (p % (2 * b) >= b) & (f % (2 * b) < b)).astype(np.float32)
    mts = [m_off(1).T.copy()] + [m_off(b) for b in (2, 4, 8, 16, 32)]
    st2 = lambda m: np.concatenate([m, m], axis=0)
    bd = lambda m: np.block([[m, np.zeros_like(m)], [np.zeros_like(m), m]])
    c64 = np.concatenate([bd(utri), bd(ones), st2(np.eye(64, dtype=np.float32)), st2(maskL), st2(maskU), st2(strictL),
                          st2(m_off(1))] + [st2(m) for m in mts], axis=1).astype(np.float32)
    key = np.arange(128)[:, None]
    q = np.arange(128)[None, :]
    slopes = 2.0 ** (-8.0 * (np.arange(8, dtype=np.float32) + 1.0) / 8)
    ab = np.zeros((128, 2, 8, 128), np.float32)
    for h in range(8):
        d0 = (q + 128 - key).astype(np.float32)
        ab[:, 0, h, :] = np.where(key > q, -slopes[h] * d0, -BIG)
        d1 = (q - key).astype(np.float32)
        ab[:, 1, h, :] = np.where(key <= q, -slopes[h] * d1, -BIG)
    return ident, c64, ab.reshape(128, 2048)


def make_in_maps(x, c, w_ada, b_ada, norm1_w, w_in, conv_w, a_log, dt_bias, gdn_norm_w,
                 q_norm_w, k_norm_w, sinks, w_out, norm2_w, w_gate, w_up, w_down):
    f = lambda a: np.ascontiguousarray(np.asarray(a, dtype=np.float32))
    ident, c64, abias = _host_consts()
    fm = lambda v: f(np.asarray(v).reshape(8, 128).T)
    rep = lambda v, p: f(np.broadcast_to(np.asarray(v).reshape(1, -1), (p, np.asarray(v).size)))
    convw = f(np.asarray(conv_w)[0, :, 0, :].reshape(4, 12, 128).transpose(2, 1, 0).reshape(128, 48))
    shared = {
        "w_ada": f(w_ada[0]), "b_ada_b": rep(b_ada[0], 128), "n1w": fm(norm1_w[0]), "n2w": fm(norm2_w[0]),
        "w_in": f(w_in[0]), "convw": convw, "alog_b": rep(a_log[0], 128), "dtb_b": rep(dt_bias[0], 128),
        "gnw_b": rep(gdn_norm_w[0], 128), "qnw_b": rep(q_norm_w[0], 128), "knw_b": rep(k_norm_w[0], 128),
        "sinks_b": rep(sinks[0], 128), "w_out": f(w_out[0]), "w_gate": f(w_gate[0]), "w_up": f(w_up[0]),
        "w_down": f(w_down[0]), "ident": ident, "c64": c64, "abias": abias,
    }
    x = np.asarray(x); c = np.asarray(c)
    maps = []
    for core in range(8):
        b, half = core // 2, core % 2
        m = dict(shared)
        m["xo"] = f(x[b, half * T_HALF:(half + 1) * T_HALF])
        m["xp"] = f(x[b, 0:T_HALF])
        m["flag"] = np.full((128, 1), float(half), np.float32)
        m["cfm"] = fm(c[b])
        maps.append(m)
    return maps


def kernel(**inputs):
    if "nc" not in _NC_CACHE:
        _NC_CACHE["nc"] = build_program()
    nc = _NC_CACHE["nc"]
    maps = make_in_maps(**inputs)
    res = run_bass_kernel_spmd(nc, maps, core_ids=list(range(8)))
    out = np.empty((4, 2 * T_HALF, D), np.float32)
    for core in range(8):
        b, half = core // 2, core % 2
        out[b, half * T_HALF:(half + 1) * T_HALF] = res.results[core]["y"]
    return out
```
